# Optimizing a Trainium2 kernel written in Bass

```python
import math
import jax
import jax.numpy as jnp
from jax import lax
import numpy as np

D_MODEL = 1024
BATCH = 2
SEQ = 8192
DEPTH = 2

GRID_W = 64
CTX_LEN = 256
N_EVEN = (DEPTH + 1) // 2
N_ODD = DEPTH // 2
EPS = 1e-6
CHUNK = 64
Q_BLOCK = 128
ROPE_THETA = 10000.0

MLSTM_HEADS = 4
MLSTM_DH = 128
MLSTM_W = MLSTM_HEADS * MLSTM_DH
HYENA_W = 512
HYENA_ORDER = 2
HYENA_SHORT = 3
HYENA_EMB = 33
HYENA_BANDS = (HYENA_EMB - 1) // 2
HYENA_FFN = 64
HYENA_NFILT = HYENA_ORDER * 2 * HYENA_W
HYENA_FAST_DECAY = 0.3
HYENA_SLOW_DECAY = 1.5
HYENA_TARGET = 1e-2
GDN_HEADS = 4
GDN_DK = 128
GDN_DV = 128
GDN_CONV = 3
MLA_HEADS = 4
MLA_NOPE = 128
MLA_ROPE = 64
MLA_QK = MLA_NOPE + MLA_ROPE
MLA_V = 128
MLA_Q_RANK = 384
MLA_KV_RANK = 256
D_FF = ((8 * D_MODEL // 3 + 127) // 128) * 128
FFN_CONV = 3

EVEN_LAYOUT = (('mq', MLSTM_W), ('mk', MLSTM_W), ('mv', MLSTM_W), ('mo', MLSTM_W),
               ('mi', 2 * MLSTM_HEADS), ('mf', 2 * MLSTM_HEADS), ('hy', (HYENA_ORDER + 1) * HYENA_W))
EVEN_CTX_STATE = ('mk', 'mv', 'mi', 'mf')
EVEN_IN = sum(w for _, w in EVEN_LAYOUT)
EVEN_MIX = MLSTM_W + HYENA_W
ODD_LAYOUT = (('gq', GDN_HEADS * GDN_DK), ('gk', GDN_HEADS * GDN_DK), ('gv', GDN_HEADS * GDN_DV),
              ('gg', GDN_HEADS * GDN_DV), ('gb', 2 * GDN_HEADS), ('ga', 2 * GDN_HEADS),
              ('lq', MLA_Q_RANK), ('lkv', MLA_KV_RANK), ('lkr', MLA_ROPE))
ODD_CTX_STATE = ('gk', 'gv', 'gb', 'ga', 'lkv', 'lkr')
ODD_IN = sum(w for _, w in ODD_LAYOUT)
ODD_MIX = GDN_HEADS * GDN_DV + MLA_HEADS * MLA_V

kernel_name = "hybrid_mlstm_hyena_gdn_mla_prefix_dit"

F32 = jnp.float32


def rms_norm(x):
    xf = x.astype(F32)
    return (xf * lax.rsqrt(jnp.mean(xf * xf, axis=-1, keepdims=True) + EPS)).astype(x.dtype)


def l2_norm(x):
    xf = x.astype(F32)
    return (xf * lax.rsqrt(jnp.sum(xf * xf, axis=-1, keepdims=True) + EPS)).astype(x.dtype)


def modulate(x, shift, scale):
    return rms_norm(x) * (1.0 + scale) + shift


def dwconv(x, w, b):
    k, ch = w.shape
    y = lax.conv_general_dilated(x, w[:, None, :], window_strides=(1,), padding=[((k - 1) // 2, k // 2)],
                                 dimension_numbers=('NWC', 'WIO', 'NWC'), feature_group_count=ch)
    return y + b


def to_heads(t, n_heads):
    b, l, _ = t.shape
    return t.reshape(b, l, n_heads, -1).transpose(0, 2, 1, 3)


def from_heads(t):
    b, h, l, d = t.shape
    return t.transpose(0, 2, 1, 3).reshape(b, l, h * d)


def dir_heads(t, n_heads):
    b, l, _ = t.shape
    return t.reshape(b, l, 2, n_heads).transpose(2, 0, 3, 1)


def flip_seq(t):
    return None if t is None else jnp.flip(t, axis=2)


def keep_seq(t):
    return t


def chunked(t):
    b, h, l = t.shape[:3]
    return t.reshape(b, h, l // CHUNK, CHUNK, *t.shape[3:])


def unchunk(t):
    nc, b, h, cs = t.shape[:4]
    return jnp.moveaxis(t, 0, 2).reshape(b, h, nc * cs, *t.shape[4:])


def project(h, w_in, layout, names=None):
    offsets, start = {}, 0
    for name, width in layout:
        offsets[name] = (start, width)
        start += width
    if names is None:
        names, w = tuple(n for n, _ in layout), w_in
    else:
        w = jnp.concatenate([w_in[:, offsets[n][0]:offsets[n][0] + offsets[n][1]] for n in names], axis=1)
    z = h @ w
    out, start = {}, 0
    for n in names:
        width = offsets[n][1]
        out[n] = z[..., start:start + width]
        start += width
    return out


def axial_rope(rows, dtype):
    row = jnp.repeat(jnp.arange(rows), GRID_W).astype(F32)
    col = jnp.tile(jnp.arange(GRID_W), rows).astype(F32)
    n_freq = MLA_ROPE // 4
    inv = ROPE_THETA ** (-jnp.arange(n_freq, dtype=F32) / n_freq)
    ang = jnp.concatenate([row[:, None] * inv, col[:, None] * inv], axis=-1)
    return jnp.cos(ang).astype(dtype), jnp.sin(ang).astype(dtype)


def rope_tail(x, cos, sin):
    half = MLA_ROPE // 2
    xp, x1, x2 = x[..., :-MLA_ROPE], x[..., -MLA_ROPE:-half], x[..., -half:]
    return jnp.concatenate([xp, x1 * cos - x2 * sin, x1 * sin + x2 * cos], axis=-1)


def block_attention(q, k, v):
    b, h, lq, dh = q.shape
    qb = jnp.moveaxis(q.reshape(b, h, lq // Q_BLOCK, Q_BLOCK, dh), 2, 0)
    scale = dh ** -0.5

    def one(qblk):
        s = jnp.einsum('bhqd,bhkd->bhqk', qblk, k).astype(F32) * scale
        p = jax.nn.softmax(s, axis=-1).astype(v.dtype)
        return jnp.einsum('bhqk,bhkd->bhqd', p, v)

    o = lax.map(one, qb)
    return jnp.moveaxis(o, 0, 2).reshape(b, h, lq, v.shape[-1])


def mlstm_scan(k, v, log_i, log_f, state, q=None):
    k, v, li, lf = chunked(k), chunked(v), chunked(log_i), chunked(log_f)
    b = jnp.cumsum(lf, axis=-1)
    b_last = b[..., -1]
    a = b_last[..., None] - b + li
    m_loc = jnp.max(a, axis=-1)
    wa = jnp.exp(a - m_loc[..., None])
    xs = (k, v, b_last, wa, m_loc)
    if q is not None:
        q = chunked(q)
        tri = jnp.tril(jnp.ones((CHUNK, CHUNK), dtype=bool))
        dmat = jnp.where(tri, b[..., :, None] - b[..., None, :] + li[..., None, :], -jnp.inf)
        m_in = jnp.max(dmat, axis=-1)
        s = jnp.einsum('bhcid,bhcjd->bhcij', q, k) * jnp.exp(dmat - m_in[..., None])
        xs = xs + (q, b, s @ v, jnp.sum(s, axis=-1), m_in)
    xs = tuple(jnp.moveaxis(t, 2, 0) for t in xs)

    def step(carry, xc):
        c_mat, n_vec, m = carry
        k_c, v_c, bl_c, wa_c, ml_c = xc[:5]
        h = None
        if q is not None:
            q_c, b_c, num_in, den_in, m_in_c = xc[5:]
            m_inter = b_c + m[..., None]
            m_t = jnp.maximum(m_inter, m_in_c)
            s_inter, s_in = jnp.exp(m_inter - m_t), jnp.exp(m_in_c - m_t)
            num = s_inter[..., None] * jnp.einsum('bhvk,bhik->bhiv', c_mat, q_c) + s_in[..., None] * num_in
            den = s_inter * jnp.einsum('bhk,bhik->bhi', n_vec, q_c) + s_in * den_in
            h = num / jnp.maximum(jnp.abs(den), jnp.exp(-m_t))[..., None]
        m_new = jnp.maximum(bl_c + m, ml_c)
        f_old, f_loc = jnp.exp(bl_c + m - m_new), jnp.exp(ml_c - m_new)
        kw = k_c * wa_c[..., None]
        c_mat = f_old[..., None, None] * c_mat + f_loc[..., None, None] * jnp.einsum('bhiv,bhik->bhvk', v_c, kw)
        n_vec = f_old[..., None] * n_vec + f_loc[..., None] * jnp.sum(kw, axis=-2)
        return (c_mat, n_vec, m_new), h

    state, hs = lax.scan(step, state, xs)
    return (unchunk(hs) if q is not None else None), state


def mlstm_prep(z, i_bias, f_bias, with_q):
    k = to_heads(z['mk'], MLSTM_HEADS).astype(F32) * MLSTM_DH ** -0.5
    v = to_heads(z['mv'], MLSTM_HEADS).astype(F32)
    log_i = dir_heads(z['mi'].astype(F32) + i_bias.astype(F32), MLSTM_HEADS)
    log_f = jax.nn.log_sigmoid(dir_heads(z['mf'].astype(F32) + f_bias.astype(F32), MLSTM_HEADS))
    q = to_heads(z['mq'], MLSTM_HEADS).astype(F32) if with_q else None
    return q, k, v, log_i, log_f


def mlstm_mixer(zc, zx, i_bias, f_bias, norm_g, need_ctx):
    qc, kc, vc, ic, fc = mlstm_prep(zc, i_bias, f_bias, need_ctx)
    qx, kx, vx, ix, fx = mlstm_prep(zx, i_bias, f_bias, True)
    bsz = kx.shape[0]
    zero = (jnp.zeros((bsz, MLSTM_HEADS, MLSTM_DH, MLSTM_DH), F32),
            jnp.zeros((bsz, MLSTM_HEADS, MLSTM_DH), F32), jnp.zeros((bsz, MLSTM_HEADS), F32))
    h_ctx, h_lat = [], []
    for d in range(2):
        fl = flip_seq if d == 1 else keep_seq
        hc, st = mlstm_scan(fl(kc), fl(vc), fl(ic[d]), fl(fc[d]), zero, fl(qc))
        hx, _ = mlstm_scan(fl(kx), fl(vx), fl(ix[d]), fl(fx[d]), st, fl(qx))
        h_lat.append(fl(hx))
        if need_ctx:
            h_ctx.append(fl(hc))

    def finish(hs, z):
        h = from_heads(rms_norm(hs[0] + hs[1])) * norm_g.astype(F32)
        return (h * jax.nn.sigmoid(z['mo'].astype(F32))).astype(z['mo'].dtype)

    return (finish(h_ctx, zc) if need_ctx else None), finish(h_lat, zx)


def hyena_filters(length, f_w1, f_b1, f_w2, f_b2, f_w3, f_b3, f_w4, f_freq):
    t = jnp.arange(length, dtype=F32)
    t_norm = t / (length - 1)
    w = 2.0 * math.pi * t / length
    bands = jnp.linspace(1e-4, HYENA_BANDS - 1, HYENA_BANDS, dtype=F32)
    ang = w[:, None] * bands
    feat = jnp.concatenate([t_norm[:, None], jnp.cos(ang), -jnp.sin(ang)], axis=-1)
    freq = f_freq.astype(F32)
    hdn = jnp.sin(freq * (feat @ f_w1.astype(F32) + f_b1.astype(F32)))
    hdn = jnp.sin(freq * (hdn @ f_w2.astype(F32) + f_b2.astype(F32)))
    hdn = jnp.sin(freq * (hdn @ f_w3.astype(F32) + f_b3.astype(F32)))
    h = hdn @ f_w4.astype(F32)
    deltas = jnp.linspace(math.log(HYENA_TARGET) / HYENA_SLOW_DECAY, math.log(HYENA_TARGET) / HYENA_FAST_DECAY,
                          HYENA_NFILT, dtype=F32)
    h = (h * jnp.exp(-t_norm[:, None] * jnp.abs(deltas))).reshape(length, HYENA_ORDER, 2, HYENA_W)
    h_fwd, h_bwd = h[:, :, 0], h[1:, :, 1]
    l1 = jnp.sum(jnp.abs(h_fwd), axis=0) + jnp.sum(jnp.abs(h_bwd), axis=0)
    gap = jnp.zeros((1, HYENA_ORDER, HYENA_W), F32)
    return jnp.concatenate([h_fwd, gap, jnp.flip(h_bwd, axis=0)], axis=0) / l1


def fft_long_conv(u, k_freq, d_skip):
    length = u.shape[1]
    uf = jnp.fft.rfft(u.astype(F32), n=2 * length, axis=1)
    y = jnp.fft.irfft(uf * k_freq, n=2 * length, axis=1)[:, :length]
    return (y + u.astype(F32) * d_skip.astype(F32)).astype(u.dtype)


def hyena_mixer(z, conv_w, conv_b, f_w1, f_b1, f_w2, f_b2, f_w3, f_b3, f_w4, f_freq, d_skip):
    u = dwconv(z, conv_w, conv_b)
    v, x1, x2 = u[..., :HYENA_W], u[..., HYENA_W:2 * HYENA_W], u[..., 2 * HYENA_W:]
    k_freq = jnp.fft.rfft(hyena_filters(z.shape[1], f_w1, f_b1, f_w2, f_b2, f_w3, f_b3, f_w4, f_freq), axis=0)
    y = x1 * fft_long_conv(v, k_freq[:, 0], d_skip[0])
    return x2 * fft_long_conv(y, k_freq[:, 1], d_skip[1])


def even_mixer(hc, hx, w_in, i_bias, f_bias, m_norm_g, h_conv_w, h_conv_b, f_w1, f_b1, f_w2, f_b2, f_w3, f_b3,
               f_w4, f_freq, h_d, w_out, need_ctx):
    zx = project(hx, w_in, EVEN_LAYOUT)
    zc = project(hc, w_in, EVEN_LAYOUT, None if need_ctx else EVEN_CTX_STATE)
    mc, mx = mlstm_mixer(zc, zx, i_bias, f_bias, m_norm_g, need_ctx)
    filt = (f_w1, f_b1, f_w2, f_b2, f_w3, f_b3, f_w4, f_freq)
    yx = hyena_mixer(zx['hy'], h_conv_w, h_conv_b, *filt, h_d)
    out_x = jnp.concatenate([mx, yx], axis=-1) @ w_out
    out_c = None
    if need_ctx:
        yc = hyena_mixer(zc['hy'], h_conv_w, h_conv_b, *filt, h_d)
        out_c = jnp.concatenate([mc, yc], axis=-1) @ w_out
    return out_c, out_x


def gdn_scan(k, v, log_a, beta, state, q=None):
    k, v, la, bt = chunked(k), chunked(v), chunked(log_a), chunked(beta)
    g = jnp.cumsum(la, axis=-1)
    idx = jnp.arange(CHUNK)
    lower = idx[:, None] >= idx[None, :]
    strict = idx[:, None] > idx[None, :]
    decay = jnp.exp(jnp.where(lower, g[..., :, None] - g[..., None, :], -jnp.inf))
    kb = k * bt[..., None]
    a_mat = jnp.where(strict, jnp.einsum('bhcid,bhcjd->bhcij', kb, k) * decay, 0.0)
    eye = jnp.eye(CHUNK, dtype=a_mat.dtype)
    tmat = lax.linalg.triangular_solve(a_mat + eye, jnp.broadcast_to(eye, a_mat.shape), left_side=True,
                                       lower=True, unit_diagonal=True)
    u = tmat @ (v * bt[..., None])
    w = tmat @ (kb * jnp.exp(g)[..., None])
    g_last = g[..., -1]
    kd = k * jnp.exp(g_last[..., None] - g)[..., None]
    xs = (u, w, kd, g_last)
    if q is not None:
        q = chunked(q)
        attn = jnp.einsum('bhcid,bhcjd->bhcij', q, k) * decay
        xs = xs + (attn, q * jnp.exp(g)[..., None])
    xs = tuple(jnp.moveaxis(t, 2, 0) for t in xs)

    def step(s_mat, xc):
        u_c, w_c, kd_c, gl_c = xc[:4]
        v_new = u_c - jnp.einsum('bhik,bhkv->bhiv', w_c, s_mat)
        o = None
        if q is not None:
            attn_c, qg_c = xc[4:]
            o = jnp.einsum('bhik,bhkv->bhiv', qg_c, s_mat) + jnp.einsum('bhij,bhjv->bhiv', attn_c, v_new)
        s_mat = s_mat * jnp.exp(gl_c)[..., None, None] + jnp.einsum('bhik,bhiv->bhkv', kd_c, v_new)
        return s_mat, o

    state, os_ = lax.scan(step, state, xs)
    return (unchunk(os_) if q is not None else None), state


def gdn_prep(z, conv_w, conv_b, a_log, dt_bias, with_q):
    wk, wv = GDN_HEADS * GDN_DK, GDN_HEADS * GDN_DV
    conv = lambda t, lo, hi: jax.nn.silu(dwconv(t, conv_w[:, lo:hi], conv_b[lo:hi]))
    k = l2_norm(to_heads(conv(z['gk'], wk, 2 * wk), GDN_HEADS)).astype(F32)
    v = to_heads(conv(z['gv'], 2 * wk, 2 * wk + wv), GDN_HEADS).astype(F32)
    beta = jax.nn.sigmoid(dir_heads(z['gb'].astype(F32), GDN_HEADS))
    b, l, _ = z['ga'].shape
    a = z['ga'].astype(F32).reshape(b, l, 2, GDN_HEADS)
    log_a = (-jnp.exp(a_log.astype(F32)) * jax.nn.softplus(a + dt_bias.astype(F32))).transpose(2, 0, 3, 1)
    q = None
    if with_q:
        q = l2_norm(to_heads(conv(z['gq'], 0, wk), GDN_HEADS)).astype(F32) * GDN_DK ** -0.5
    return q, k, v, log_a, beta


def gdn_mixer(zc, zx, conv_w, conv_b, a_log, dt_bias, norm_g, need_ctx):
    qc, kc, vc, ac, bc = gdn_prep(zc, conv_w, conv_b, a_log, dt_bias, need_ctx)
    qx, kx, vx, ax, bx = gdn_prep(zx, conv_w, conv_b, a_log, dt_bias, True)
    zero = jnp.zeros((kx.shape[0], GDN_HEADS, GDN_DK, GDN_DV), F32)
    o_ctx, o_lat = [], []
    for d in range(2):
        fl = flip_seq if d == 1 else keep_seq
        oc, st = gdn_scan(fl(kc), fl(vc), fl(ac[d]), fl(bc[d]), zero, fl(qc))
        ox, _ = gdn_scan(fl(kx), fl(vx), fl(ax[d]), fl(bx[d]), st, fl(qx))
        o_lat.append(fl(ox))
        if need_ctx:
            o_ctx.append(fl(oc))

    def finish(os_, z):
        o = from_heads(rms_norm(os_[0] + os_[1]) * norm_g.astype(F32))
        return (o * jax.nn.silu(z['gg'].astype(F32))).astype(z['gg'].dtype)

    return (finish(o_ctx, zc) if need_ctx else None), finish(o_lat, zx)


def mla_mixer(zc, zx, q_norm_g, w_q_up, kv_norm_g, w_kv_up, qn_g, kn_g, rope_cos, rope_sin, need_ctx):
    def keys_values(z, rope):
        ckv = rms_norm(z['lkv']) * kv_norm_g
        kv = to_heads(ckv @ w_kv_up, MLA_HEADS)
        k_rope = jnp.broadcast_to(z['lkr'][:, None], kv.shape[:3] + (MLA_ROPE,))
        k = rms_norm(jnp.concatenate([kv[..., :MLA_NOPE], k_rope], axis=-1)) * kn_g
        return (k if rope is None else rope_tail(k, *rope)), kv[..., MLA_NOPE:]

    def queries(z, rope):
        cq = rms_norm(z['lq']) * q_norm_g
        q = rms_norm(to_heads(cq @ w_q_up, MLA_HEADS)) * qn_g
        return q if rope is None else rope_tail(q, *rope)

    rope = (rope_cos, rope_sin)
    kc, vc = keys_values(zc, None)
    kx, vx = keys_values(zx, rope)
    keys = jnp.concatenate([kc, kx], axis=2)
    vals = jnp.concatenate([vc, vx], axis=2)
    out_x = from_heads(block_attention(queries(zx, rope), keys, vals))
    out_c = from_heads(block_attention(queries(zc, None), kc, vc)) if need_ctx else None
    return out_c, out_x


def odd_mixer(hc, hx, w_in, conv_w, conv_b, a_log, dt_bias, gdn_norm_g, q_norm_g, w_q_up, kv_norm_g, w_kv_up,
              qn_g, kn_g, w_out, rope_cos, rope_sin, need_ctx):
    zx = project(hx, w_in, ODD_LAYOUT)
    zc = project(hc, w_in, ODD_LAYOUT, None if need_ctx else ODD_CTX_STATE)
    gc, gx = gdn_mixer(zc, zx, conv_w, conv_b, a_log, dt_bias, gdn_norm_g, need_ctx)
    ac, ax = mla_mixer(zc, zx, q_norm_g, w_q_up, kv_norm_g, w_kv_up, qn_g, kn_g, rope_cos, rope_sin, need_ctx)
    out_x = jnp.concatenate([gx, ax], axis=-1) @ w_out
    out_c = jnp.concatenate([gc, ac], axis=-1) @ w_out if need_ctx else None
    return out_c, out_x


def conv_ffn(h, w_up, conv_w, conv_b, w_down):
    u = dwconv(h @ w_up, conv_w, conv_b)
    return (jax.nn.silu(u[..., :D_FF]) * u[..., D_FF:]) @ w_down


def setup_inputs(seed: int = 0) -> dict:
    key = jax.random.key(seed)
    ks = iter(jax.random.split(key, 48))

    def nrm(shape, scale=1.0):
        return scale * jax.random.normal(next(ks), shape, F32)

    def gain(shape):
        return 1.0 + nrm(shape, 0.1)

    D = D_MODEL
    H = MLSTM_HEADS
    f_bias = jnp.linspace(3.0, 6.0, 2 * H, dtype=F32)[None] + nrm((N_EVEN, 2 * H), 0.1)
    a_log = jnp.log(jax.random.uniform(next(ks), (N_ODD, 2, GDN_HEADS), F32, 1.0, 16.0))
    dt = jnp.exp(jax.random.uniform(next(ks), (N_ODD, 2, GDN_HEADS), F32, math.log(1e-3), math.log(1e-1)))
    dt_bias = dt + jnp.log(-jnp.expm1(-dt))
    return {
        'x': nrm((BATCH, SEQ, D)),
        'c': nrm((BATCH, D)),
        'ctx': nrm((BATCH, CTX_LEN, D)),
        'c_ctx': nrm((D,)),
        'mod_w': nrm((DEPTH, D, 6 * D), D ** -0.5),
        'mod_b': nrm((DEPTH, 6 * D), 0.02),
        'a_w_in': nrm((N_EVEN, D, EVEN_IN), D ** -0.5),
        'a_i_bias': nrm((N_EVEN, 2 * H), 0.1),
        'a_f_bias': f_bias,
        'a_norm_g': gain((N_EVEN, MLSTM_W)),
        'b_conv_w': nrm((N_EVEN, HYENA_SHORT, (HYENA_ORDER + 1) * HYENA_W), HYENA_SHORT ** -0.5),
        'b_conv_b': nrm((N_EVEN, (HYENA_ORDER + 1) * HYENA_W), 0.02),
        'b_f_w1': nrm((N_EVEN, HYENA_EMB, HYENA_FFN), HYENA_EMB ** -0.5),
        'b_f_b1': nrm((N_EVEN, HYENA_FFN), 0.1),
        'b_f_w2': nrm((N_EVEN, HYENA_FFN, HYENA_FFN), HYENA_FFN ** -0.5),
        'b_f_b2': nrm((N_EVEN, HYENA_FFN), 0.1),
        'b_f_w3': nrm((N_EVEN, HYENA_FFN, HYENA_FFN), HYENA_FFN ** -0.5),
        'b_f_b3': nrm((N_EVEN, HYENA_FFN), 0.1),
        'b_f_w4': nrm((N_EVEN, HYENA_FFN, HYENA_NFILT), HYENA_FFN ** -0.5),
        'b_f_freq': gain((N_EVEN, HYENA_FFN)),
        'b_d': nrm((N_EVEN, HYENA_ORDER, HYENA_W), 0.1),
        'ab_w_out': nrm((N_EVEN, EVEN_MIX, D), EVEN_MIX ** -0.5),
        'cd_w_in': nrm((N_ODD, D, ODD_IN), D ** -0.5),
        'c_conv_w': nrm((N_ODD, GDN_CONV, GDN_HEADS * (2 * GDN_DK + GDN_DV)), GDN_CONV ** -0.5),
        'c_conv_b': nrm((N_ODD, GDN_HEADS * (2 * GDN_DK + GDN_DV)), 0.02),
        'c_a_log': a_log,
        'c_dt_bias': dt_bias,
        'c_norm_g': gain((N_ODD, GDN_DV)),
        'd_q_norm_g': gain((N_ODD, MLA_Q_RANK)),
        'd_w_q_up': nrm((N_ODD, MLA_Q_RANK, MLA_HEADS * MLA_QK), MLA_Q_RANK ** -0.5),
        'd_kv_norm_g': gain((N_ODD, MLA_KV_RANK)),
        'd_w_kv_up': nrm((N_ODD, MLA_KV_RANK, MLA_HEADS * (MLA_NOPE + MLA_V)), MLA_KV_RANK ** -0.5),
        'd_qn_g': gain((N_ODD, MLA_QK)),
        'd_kn_g': gain((N_ODD, MLA_QK)),
        'cd_w_out': nrm((N_ODD, ODD_MIX, D), ODD_MIX ** -0.5),
        'ffn_w_up': nrm((DEPTH, D, 2 * D_FF), D ** -0.5),
        'ffn_conv_w': nrm((DEPTH, FFN_CONV, 2 * D_FF), FFN_CONV ** -0.5),
        'ffn_conv_b': nrm((DEPTH, 2 * D_FF), 0.02),
        'ffn_w_down': nrm((DEPTH, D_FF, D), D_FF ** -0.5),
    }


def reference(x, c, ctx, c_ctx, mod_w, mod_b, a_w_in, a_i_bias, a_f_bias, a_norm_g, b_conv_w, b_conv_b,
              b_f_w1, b_f_b1, b_f_w2, b_f_b2, b_f_w3, b_f_b3, b_f_w4, b_f_freq, b_d, ab_w_out,
              cd_w_in, c_conv_w, c_conv_b, c_a_log, c_dt_bias, c_norm_g, d_q_norm_g, d_w_q_up, d_kv_norm_g,
              d_w_kv_up, d_qn_g, d_kn_g, cd_w_out, ffn_w_up, ffn_conv_w, ffn_conv_b, ffn_w_down):
    rows = x.shape[1] // GRID_W
    rope_cos, rope_sin = axial_rope(rows, x.dtype)
    s_c = jax.nn.silu(c)
    s_ctx = jax.nn.silu(c_ctx)
    for layer in range(DEPTH):
        last = layer == DEPTH - 1
        need_ctx = not last
        mods_x = jnp.split((s_c @ mod_w[layer] + mod_b[layer])[:, None, :], 6, axis=-1)
        mods_c = jnp.split(s_ctx @ mod_w[layer] + mod_b[layer], 6, axis=-1)
        hx = modulate(x, mods_x[0], mods_x[1])
        hc = modulate(ctx, mods_c[0], mods_c[1])
        e = layer // 2
        if layer % 2 == 0:
            out_c, out_x = even_mixer(hc, hx, a_w_in[e], a_i_bias[e], a_f_bias[e], a_norm_g[e], b_conv_w[e],
                                      b_conv_b[e], b_f_w1[e], b_f_b1[e], b_f_w2[e], b_f_b2[e], b_f_w3[e],
                                      b_f_b3[e], b_f_w4[e], b_f_freq[e], b_d[e], ab_w_out[e], need_ctx)
        else:
            out_c, out_x = odd_mixer(hc, hx, cd_w_in[e], c_conv_w[e], c_conv_b[e], c_a_log[e], c_dt_bias[e],
                                     c_norm_g[e], d_q_norm_g[e], d_w_q_up[e], d_kv_norm_g[e], d_w_kv_up[e],
                                     d_qn_g[e], d_kn_g[e], cd_w_out[e], rope_cos, rope_sin, need_ctx)
        x = x + mods_x[2] * out_x
        x = x + mods_x[5] * conv_ffn(modulate(x, mods_x[3], mods_x[4]), ffn_w_up[layer], ffn_conv_w[layer],
                                     ffn_conv_b[layer], ffn_w_down[layer])
        if need_ctx:
            ctx = ctx + mods_c[2] * out_c
            ctx = ctx + mods_c[5] * conv_ffn(modulate(ctx, mods_c[3], mods_c[4]), ffn_w_up[layer],
                                             ffn_conv_w[layer], ffn_conv_b[layer], ffn_w_down[layer])
    return x
```

```python
import numpy as np
from contextlib import ExitStack
import concourse.bass as bass
import concourse.mybir as mybir
from concourse.bass_utils import run_bass_kernel_spmd

F32 = mybir.dt.float32
BF16 = mybir.dt.bfloat16
ALU = mybir.AluOpType
AF = mybir.ActivationFunctionType
AX = mybir.AxisListType


class Dep:
    __slots__ = ("w", "r")

    def __init__(self):
        self.w = None
        self.r = {}


class TT:
    def __init__(self, t, name, psum=False):
        self.t = t
        self.name = name
        self.psum = psum
        self.whole = Dep()
        self.parts = {}

    def __getitem__(self, idx):
        return self.t[idx]


class Prog:
    COMPUTE = ("pe", "act", "dve", "pool")
    QUEUES = ("sp", "pool")
    NDS = 12

    def __init__(self, nc, es, same_sync=True):
        self.nc = nc
        self.es = es
        self.same_sync = same_sync
        self.e = {"pe": nc.tensor, "act": nc.scalar, "dve": nc.vector, "pool": nc.gpsimd, "sp": nc.sync}
        self.semh = {}
        for k in self.COMPUTE:
            self.semh[k] = es.enter_context(nc.semaphore("s_" + k))
        for q in self.QUEUES:
            for i in range(self.NDS):
                self.semh[("d", q, i)] = es.enter_context(nc.semaphore(f"d_{q}{i}"))
        self.cnt = {k: 0 for k in self.COMPUTE}
        self.dcnt = {q: 0 for q in self.QUEUES}
        self.dfinal = {}
        self.waited = {k: {} for k in self.e}
        self.nins = 0
        self.uid = 0

    def sb(self, name, shape, dtype=F32, es=None):
        self.uid += 1
        t = (es or self.es).enter_context(self.nc.sbuf_tensor(f"{name}_{self.uid}", list(shape), dtype))
        return TT(t, name)

    def ps(self, name, shape, dtype=F32, es=None):
        self.uid += 1
        t = (es or self.es).enter_context(self.nc.psum_tensor(f"{name}_{self.uid}", list(shape), dtype))
        return TT(t, name, psum=True)

    def dram(self, name, shape, dtype=F32, kind="Internal"):
        t = self.nc.dram_tensor(name, list(shape), dtype, kind=kind)
        return TT(t.ap(), name)

    def _deps(self, items):
        out = []
        for it in items:
            if isinstance(it, TT):
                out.append((it, None))
            else:
                out.append(it)
        return out

    def _recs(self, tt, key):
        if key is None:
            return [tt.whole] + list(tt.parts.values())
        if key not in tt.parts:
            tt.parts[key] = Dep()
        return [tt.whole, tt.parts[key]]

    def _need(self, eng, toks):
        best = {}
        for sk, v in toks:
            if sk == eng and (eng == "pe" or not self.same_sync):
                continue
            if v > best.get(sk, 0):
                best[sk] = v
        for sk, v in best.items():
            if self.waited[eng].get(sk, 0) >= v:
                continue
            self.e[eng].wait_ge(self.semh[sk], v)
            self.waited[eng][sk] = v
            self.nins += 1

    def op(self, eng, fn, rd=(), wr=(), dma=False):
        rd = self._deps(rd)
        wr = self._deps(wr)
        toks = []
        for tt, key in rd:
            for d in self._recs(tt, key):
                if d.w:
                    toks.append(d.w)
                if tt.psum:
                    toks.extend((sk, v) for sk, v in d.r.items() if sk != eng)
        for tt, key in wr:
            for d in self._recs(tt, key):
                if d.w:
                    toks.append(d.w)
                toks.extend(d.r.items())
        if dma:
            i = self.dcnt[eng]
            idx = i % self.NDS
            val = 16 * (i // self.NDS + 1)
            sk = ("d", eng, idx)
            if val > 16:
                toks.append((sk, val - 16))
            self._need(eng, toks)
            ins = fn(self.e[eng])
            ins.then_inc(self.semh[sk], 16)
            self.dcnt[eng] += 1
            self.dfinal[sk] = val
            tok = (sk, val)
        else:
            self._need(eng, toks)
            ins = fn(self.e[eng])
            ins.then_inc(self.semh[eng], 1)
            self.cnt[eng] += 1
            tok = (eng, self.cnt[eng])
        self.nins += 1
        for tt, key in rd:
            if key is None:
                d = tt.whole
            else:
                d = self._recs(tt, key)[1]
            if tok[1] > d.r.get(tok[0], 0):
                d.r[tok[0]] = tok[1]
        for tt, key in wr:
            if key is None:
                tt.whole.w = tok
                tt.whole.r = {}
                tt.parts = {}
            else:
                d = self._recs(tt, key)[1]
                d.w = tok
                d.r = {}
        return ins

    def dma(self, out, in_, rd=(), wr=(), q="sp", **kw):
        return self.op(q, lambda e: e.dma_start(out=out, in_=in_, **kw), rd=rd, wr=wr, dma=True)

    def barrier(self, engs=None):
        toks = [(k, v) for k, v in self.cnt.items() if v > 0] + list(self.dfinal.items())
        for eng in (engs or self.e.keys()):
            ss = self.same_sync
            self.same_sync = True
            self._need(eng, toks)
            self.same_sync = ss

    def finish(self):
        self.barrier(["sp"])


EPS = 1e-6
D = 1024

def blocks(T, mx=512):
    nb = (T + mx - 1) // mx
    base, rem = T // nb, T % nb
    out, s = [], 0
    for i in range(nb):
        w = base + (1 if i < rem else 0)
        out.append((s, w)); s += w
    return out


class Ctx:
    def __init__(self, P):
        self.P = P
        self.ones = P.sb("ones", [128, 128])
        P.op("dve", lambda e: e.memset(self.ones[:], 1.0), wr=[self.ones])
        self.psb = [P.ps(f"psb{i}", [128, 512]) for i in range(8)]
        self.psi = 0
        self.tmp = {}
        self.qi = 0

    pool = None

    def ps(self):
        if self.pool is not None:
            p = self.pool
            t = self.psb[p[0] + p[2] % p[1]]; p[2] += 1
            return t
        t = self.psb[self.psi % 8]; self.psi += 1
        return t

    def rot(self, name, shape, n=2, dtype=F32):
        if name not in self.tmp:
            self.tmp[name] = [[self.P.sb(f"{name}{i}", shape, dtype) for i in range(n)], 0]
        ent = self.tmp[name]
        t = ent[0][ent[1] % n]; ent[1] += 1
        return t

    def q(self):
        self.qi += 1
        return "sp" if self.qi % 2 == 0 else "pool"


def compute_mods(C, modw, modbT, cT):
    P = C.P
    sc = P.sb("sc", [128, 8, 2]); P.dma(sc[:], cT[:], rd=[cT], wr=[sc])
    P.op("act", lambda e: e.activation(out=sc[:], in_=sc[:], func=AF.Silu), rd=[sc], wr=[sc])
    mb = P.sb("mb", [128, 48]); P.dma(mb[:], modbT[:], rd=[modbT], wr=[mb])
    mods = P.sb("mods", [128, 48, 2])
    for m in range(48):
        wt = C.rot("win", [128, 8, 128], 3)
        P.dma(wt[:], modw[:, m * 128:(m + 1) * 128].rearrange("(k p) m -> p k m", p=128), rd=[modw], wr=[wt], q=C.q())
        ps = C.ps()
        for k in range(8):
            P.op("pe", lambda e: e.matmul(ps[:, 0:2], wt[:, k, :], sc[:, k, :], start=(k == 0), stop=(k == 7)), rd=[wt, sc], wr=[ps])
        P.op("dve", lambda e: e.tensor_scalar(out=mods[:, m, :], in0=ps[:, 0:2], scalar1=mb[:, m:m + 1], scalar2=None, op0=ALU.add), rd=[ps, mb], wr=[(mods, m)])
    for j in (1, 4):
        P.op("dve", lambda e: e.tensor_scalar(out=mods[:, j * 8:(j + 1) * 8, :], in0=mods[:, j * 8:(j + 1) * 8, :], scalar1=1.0, scalar2=None, op0=ALU.add), rd=[mods], wr=[mods])
    return mods


def norm_mod(C, xt, hT, T, mods, jsh, jsc, col):
    P = C.P
    for bi, (s, w) in enumerate(blocks(T)):
        ps = C.ps()
        for c in range(8):
            sq = C.rot("sq", [128, 512], 3)
            P.op("act", lambda e: e.activation(out=sq[:, :w], in_=xt[:, c, s:s + w], func=AF.Square), rd=[(xt, bi)], wr=[sq])
            P.op("pe", lambda e: e.matmul(ps[:, :w], C.ones[:], sq[:, :w], start=(c == 0), stop=(c == 7)), rd=[sq, C.ones], wr=[ps])
        rs = C.rot("rs", [128, 512], 2)
        P.op("dve", lambda e: e.tensor_scalar(out=rs[:, :w], in0=ps[:, :w], scalar1=1.0 / D, scalar2=EPS, op0=ALU.mult, op1=ALU.add), rd=[ps], wr=[rs])
        P.op("act", lambda e: e.activation(out=rs[:, :w], in_=rs[:, :w], func=AF.Sqrt), rd=[rs], wr=[rs])
        P.op("dve", lambda e: e.reciprocal(out=rs[:, :w], in_=rs[:, :w]), rd=[rs], wr=[rs])
        for c in range(8):
            P.op("dve", lambda e: e.tensor_tensor(out=hT[:, c, s:s + w], in0=xt[:, c, s:s + w], in1=rs[:, :w], op=ALU.mult), rd=[(xt, bi), rs], wr=[(hT, bi)])
            P.op("act", lambda e: e.activation(out=hT[:, c, s:s + w], in_=hT[:, c, s:s + w], func=AF.Identity,
                                               scale=mods[:, jsc * 8 + c, col:col + 1], bias=mods[:, jsh * 8 + c, col:col + 1]), rd=[(hT, bi), mods], wr=[(hT, bi)])


def linear_T(C, wdram, col0, mw, hT, T, consume, kc=8, wname="win"):
    P = C.P
    wt = C.rot(wname, [128, kc, 128], 3)
    P.dma(wt[:, :, :mw], wdram[:, col0:col0 + mw].rearrange("(k p) m -> p k m", p=128), rd=[wdram], wr=[wt], q=C.q())
    for bi, (s, w) in enumerate(blocks(T)):
        ps = C.ps()
        for k in range(kc):
            P.op("pe", lambda e: e.matmul(ps[:mw, :w], wt[:, k, :mw], hT[:, k, s:s + w], start=(k == 0), stop=(k == kc - 1)), rd=[wt, (hT, bi)], wr=[ps])
        consume(ps, bi, s, w)


def conv3(C, z, u, T, H, wcol, bcol, mask, mw=128):
    P = C.P
    n = T - 2 * H
    P.op("dve", lambda e: e.tensor_scalar(out=z[:mw, 0:H], in0=z[:mw, 0:H], scalar1=mask[:mw, 0:1], scalar2=None, op0=ALU.mult), rd=[z], wr=[z])
    P.op("dve", lambda e: e.tensor_scalar(out=z[:mw, T - H:T], in0=z[:mw, T - H:T], scalar1=mask[:mw, 1:2], scalar2=None, op0=ALU.mult), rd=[z], wr=[z])
    P.op("dve", lambda e: e.tensor_scalar(out=u[:mw, 0:n], in0=z[:mw, H - 1:H - 1 + n], scalar1=wcol[:mw, 0:1], scalar2=bcol[:mw, 0:1], op0=ALU.mult, op1=ALU.add), rd=[z], wr=[u])
    P.op("dve", lambda e: e.scalar_tensor_tensor(out=u[:mw, 0:n], in0=z[:mw, H:H + n], scalar=wcol[:mw, 1:2], in1=u[:mw, 0:n], op0=ALU.mult, op1=ALU.add), rd=[z, u], wr=[u])
    P.op("dve", lambda e: e.scalar_tensor_tensor(out=u[:mw, 0:n], in0=z[:mw, H + 1:H + 1 + n], scalar=wcol[:mw, 2:3], in1=u[:mw, 0:n], op0=ALU.mult, op1=ALU.add), rd=[z, u], wr=[u])


A_CHUNKS = [(i * 128, 128, None) for i in range(16)] + [(2048, 16, None)] + [(2064 + i * 128, 128, i) for i in range(12)]
SEG_A = [(1024, 0), (1024, 0), (64, 1)]

def build_A():
    H = 1
    nc = bass.Bass("TRN2", target_bir_lowering=False)
    es = ExitStack()
    with es:
        P = Prog(nc, es)
        C = Ctx(P)
        xin = [P.dram(f"x{i}", [128, 8, n + 2 * H], kind="ExternalInput") for i, (n, _) in enumerate(SEG_A)]
        masks = P.dram("masks", [128, 3, 2], kind="ExternalInput")
        cT = P.dram("cT", [128, 8, 2], kind="ExternalInput")
        modw = P.dram("modw", [1024, 6144], kind="ExternalInput")
        modbT = P.dram("modbT", [128, 48], kind="ExternalInput")
        win = P.dram("win", [1024, 3600], kind="ExternalInput")
        cw = P.dram("cw", [128, 12, 3], kind="ExternalInput")
        cb = P.dram("cb", [128, 12], kind="ExternalInput")
        outs = [P.dram(f"o{i}", [3600, n], kind="ExternalOutput") for i, (n, _) in enumerate(SEG_A)]
        mk = P.sb("mk", [128, 3, 2]); P.dma(mk[:], masks[:], rd=[masks], wr=[mk])
        cwt = P.sb("cwt", [128, 12, 3]); P.dma(cwt[:], cw[:], rd=[cw], wr=[cwt])
        cbt = P.sb("cbt", [128, 12]); P.dma(cbt[:], cb[:], rd=[cb], wr=[cbt])
        mods = compute_mods(C, modw, modbT, cT)
        TM = 1024 + 2 * H
        xt = P.sb("xt", [128, 8, TM]); hT = P.sb("hT", [128, 8, TM])
        for si, (n, col) in enumerate(SEG_A):
            T = n + 2 * H
            P.dma(xt[:, :, :T], xin[si][:], rd=[xin[si]], wr=[xt])
            norm_mod(C, xt, hT, T, mods, 0, 1, col)
            for (col0, mw, hy) in A_CHUNKS:
                z = C.rot("zrow", [128, TM], 3)
                def consume(ps, bi, s, w, z=z, mw=mw):
                    P.op("act", lambda e: e.copy(out=z[:mw, s:s + w], in_=ps[:mw, :w]), rd=[ps], wr=[(z, bi)])
                linear_T(C, win, col0, mw, hT, T, consume)
                if hy is None:
                    P.dma(outs[si][col0:col0 + mw, :], z[:mw, H:H + n], rd=[z], wr=[(outs[si], col0)], q=C.q())
                else:
                    u = C.rot("urow", [128, 1024], 2)
                    conv3(C, z, u, T, H, cwt[:, hy, :], cbt[:, hy:hy + 1], mk[:, si, :])
                    P.dma(outs[si][col0:col0 + mw, :], u[:mw, 0:n], rd=[u], wr=[(outs[si], col0)], q=C.q())
        P.finish()
        print("phase A instructions", P.nins)
    return nc


def seg_slice_T(xb, start, n, H):
    L = xb.shape[0]
    out = np.zeros((n + 2 * H, xb.shape[1]), np.float32)
    lo, hi = start - H, start + n + H
    a, b = max(lo, 0), min(hi, L)
    out[a - lo:b - lo] = xb[a:b]
    return np.ascontiguousarray(out.T.reshape(8, 128, n + 2 * H).transpose(1, 0, 2)), float(lo >= 0), float(hi <= L)


def colT(v):
    return np.ascontiguousarray(v.reshape(-1, 128).T)


def run_A(inp):
    H = 1
    nc = build_A()
    maps = []
    for k in range(8):
        b, qtr = k // 4, k % 4
        m = {}
        mk = np.zeros((128, 3, 2), np.float32)
        for si in range(2):
            xs, l, r = seg_slice_T(inp["x"][b], qtr * 2048 + si * 1024, 1024, H)
            m[f"x{si}"] = xs; mk[:, si, 0] = l; mk[:, si, 1] = r
        xs, l, r = seg_slice_T(inp["ctx"][b], qtr * 64, 64, H)
        m["x2"] = xs; mk[:, 2, 0] = l; mk[:, 2, 1] = r
        m["masks"] = mk
        m["cT"] = np.ascontiguousarray(np.stack([colT(inp["c"][b]), colT(inp["c_ctx"])], -1))
        m["modw"] = inp["mod_w"][0]
        m["modbT"] = colT(inp["mod_b"][0])
        m["win"] = inp["a_w_in"][0]
        m["cw"] = np.ascontiguousarray(inp["b_conv_w"][0].T.reshape(12, 128, 3).transpose(1, 0, 2))
        m["cb"] = colT(inp["b_conv_b"][0])
        maps.append(m)
    res = run_bass_kernel_spmd(nc, maps, core_ids=list(range(8)))
    zT = np.zeros((2, 3600, 8192), np.float32); zcT = np.zeros((2, 3600, 256), np.float32)
    for k in range(8):
        b, qtr = k // 4, k % 4
        r = res.results[k]
        for si in range(2):
            zT[b][:, qtr * 2048 + si * 1024: qtr * 2048 + (si + 1) * 1024] = r[f"o{si}"]
        zcT[b][:, qtr * 64:(qtr + 1) * 64] = r["o2"]
    return zT, zcT


NCH = 132
SEGS = [(0, 4)] + [(4 + 16 * i, 16) for i in range(8)]


def run_interleaved(gens, C=None, pools=None):
    gens = list(gens)
    idx = {id(g): i for i, g in enumerate(gens)}
    pl = [[b, n, 0] for (b, n) in pools] if pools else None
    while gens:
        for g in list(gens):
            if pl:
                C.pool = pl[idx[id(g)]]
            try:
                next(g)
            except StopIteration:
                gens.remove(g)
    if pl:
        C.pool = None

MODE = 0
def mlstm_part(P, C, qT, kT, ktok, vtok, motok, g4, gbias, tri, normg, out_m):
    DH = 128
    tr = P.sb("tri", [64, 2, 64]); P.dma(tr[:], tri[:], rd=[tri], wr=[tr])
    gb = P.sb("gb", [64, 4]); P.dma(gb[:], gbias[:], rd=[gbias], wr=[gb])
    ng = P.sb("ng", [64, 128]); P.dma(ng[:], normg[:], rd=[normg], wr=[ng])
    g = P.sb("g", [64, 4, NCH]); P.dma(g[:], g4[:], rd=[g4], wr=[g])
    hsum = P.sb("hsum", [64, NCH, 128])
    S = P.sb("S", [128, 129])
    Et = P.sb("Et", [64, 2, NCH]); wt = P.sb("wt", [64, 2, NCH]); w2t = P.sb("w2t", [64, 2, NCH]); ebl = P.sb("ebl", [128, 2, NCH])
    STEP = 10 ** 9
    cntr = [0]
    def po(*a, **k):
        cntr[0] += 1
        if cntr[0] <= STEP:
            P.op(*a, **k)
    for d in range(2 if MODE != 4 else 0):
        li, lf = g[:, 2 * d, :], g[:, 2 * d + 1, :]
        po("dve", lambda e: e.tensor_scalar(out=li, in0=li, scalar1=gb[:, 2 * d:2 * d + 1], scalar2=None, op0=ALU.add), rd=[g, gb], wr=[g])
        po("act", lambda e: e.activation(out=lf, in_=lf, func=AF.Sigmoid, bias=gb[:, 2 * d + 1:2 * d + 2], scale=1.0), rd=[g, gb], wr=[g])
        po("act", lambda e: e.activation(out=lf, in_=lf, func=AF.Ln), rd=[g], wr=[g])
        ps = C.ps()
        po("pe", lambda e: e.matmul(ps[:64, :NCH], tr[:, d, :], lf, start=True, stop=True), rd=[tr, g], wr=[ps])
        ps2 = C.ps()
        po("pe", lambda e: e.matmul(ps2[:, :NCH], C.ones[:64, :], lf, start=True, stop=True), rd=[C.ones, g], wr=[ps2])
        po("act", lambda e: e.activation(out=Et[:, d, :], in_=ps[:64, :NCH], func=AF.Exp), rd=[ps], wr=[Et])
        po("act", lambda e: e.activation(out=ebl[:, d, :], in_=ps2[:, :NCH], func=AF.Exp), rd=[ps2], wr=[ebl])
        tmp = C.rot("gtmp", [64, NCH], 2)
        po("dve", lambda e: e.tensor_tensor(out=tmp[:], in0=ps[:64, :NCH], in1=li, op=ALU.subtract), rd=[g, ps], wr=[tmp])
        po("act", lambda e: e.activation(out=wt[:, d, :], in_=tmp[:], func=AF.Exp, scale=-1.0), rd=[tmp], wr=[wt])
        po("dve", lambda e: e.tensor_scalar(out=wt[:, d, :], in0=wt[:, d, :], scalar1=float(DH ** -0.5), scalar2=None, op0=ALU.mult), rd=[wt], wr=[wt])
        tmp2 = C.rot("gtmp", [64, NCH], 2)
        po("dve", lambda e: e.tensor_tensor(out=tmp2[:], in0=ps2[:64, :NCH], in1=tmp[:], op=ALU.subtract), rd=[tmp, ps2], wr=[tmp2])
        po("act", lambda e: e.activation(out=w2t[:, d, :], in_=tmp2[:], func=AF.Exp), rd=[tmp2], wr=[w2t])
        po("dve", lambda e: e.tensor_scalar(out=w2t[:, d, :], in0=w2t[:, d, :], scalar1=float(DH ** -0.5), scalar2=None, op0=ALU.mult), rd=[w2t], wr=[w2t])
    P.op("dve", lambda e: e.memset(hsum[:], 0.0), wr=[hsum])
    Ss = [S, P.sb("S_b", [128, 129])]
    def scan_dir(d):
        S = Ss[d]
        P.op("dve", lambda e: e.memset(S[:], 0.0), wr=[S])
        yield
        segs = SEGS if d == 0 else [SEGS[0]] + SEGS[:0:-1]
        for (c0, ncs) in segs:
            T = ncs * 64
            qs = C.rot("qs%d" % d, [128, 1024], 2); ks = C.rot("ks%d" % d, [128, 1024], 2)
            kk = C.rot("kk%d" % d, [64, 16, 128], 2); vv = C.rot("vv%d" % d, [64, 16, 129], 2)
            P.dma(qs[:, :T], qT[:, c0 * 64:c0 * 64 + T], rd=[qT], wr=[qs], q="sp")
            yield
            P.dma(ks[:, :T], kT[:, c0 * 64:c0 * 64 + T], rd=[kT], wr=[ks], q="pool")
            yield
            P.dma(kk[:, :ncs, :], ktok[:, c0:c0 + ncs, :], rd=[ktok], wr=[kk], q="sp")
            yield
            P.op("dve", lambda e: e.memset(vv[:, :, 128:129], 1.0), wr=[vv])
            yield
            P.dma(vv[:, :ncs, 0:128], vtok[:, c0:c0 + ncs, :], rd=[vtok], wr=[vv], q="pool")
            yield
            order = range(ncs) if d == 0 else range(ncs - 1, -1, -1)
            for cl in order:
                c = c0 + cl
                tk = slice(cl * 64, cl * 64 + 64)
                vh = C.rot("vh%d" % d, [64, 129], 3); vh2 = C.rot("vh2%d" % d, [64, 129], 3)
                P.op("dve", lambda e: e.tensor_scalar(out=vh[:], in0=vv[:, cl, :], scalar1=wt[:, d, c:c + 1], scalar2=None, op0=ALU.mult), rd=[vv, wt], wr=[vh])
                yield
                P.op("act", lambda e: e.activation(out=vh2[:], in_=vv[:, cl, :], func=AF.Identity, scale=w2t[:, d, c:c + 1]), rd=[vv, w2t], wr=[vh2])
                yield
                ps = C.ps()
                P.op("pe", lambda e: e.matmul(ps[:64, :64], ks[:, tk], qs[:, tk], start=True, stop=True), rd=[ks, qs], wr=[ps])
                yield
                sm = C.rot("sm%d" % d, [64, 64], 3)
                P.op("dve", lambda e: e.tensor_tensor(out=sm[:], in0=ps[:64, :64], in1=tr[:, d, :], op=ALU.mult), rd=[ps, tr], wr=[sm])
                yield
                pn = C.ps()
                P.op("pe", lambda e: e.matmul(pn[:64, :129], sm[:], vh[:], start=True, stop=False), rd=[sm, vh], wr=[pn])
                yield
                P.op("pe", lambda e: e.matmul(pn[:64, :129], qs[:, tk], S[:], start=False, stop=True), rd=[qs, S], wr=[pn])
                yield
                pu = C.ps()
                P.op("pe", lambda e: e.matmul(pu[:, :129], kk[:, cl, :], vh2[:], start=True, stop=True), rd=[kk, vh2], wr=[pu])
                yield
                P.op("dve", lambda e: e.tensor_scalar(out=S[:], in0=S[:], scalar1=ebl[:, d, c:c + 1], scalar2=None, op0=ALU.mult), rd=[S, ebl], wr=[S])
                yield
                P.op("dve", lambda e: e.tensor_tensor(out=S[:], in0=pu[:, :129], in1=S[:], op=ALU.add), rd=[S, pu], wr=[S])
                yield
                t = C.rot("den%d" % d, [64, 4], 3)
                P.op("dve", lambda e: e.tensor_tensor(out=t[:, 0:1], in0=pn[:64, 128:129], in1=Et[:, d, c:c + 1], op=ALU.mult), rd=[pn, Et], wr=[t])
                yield
                P.op("act", lambda e: e.activation(out=t[:, 1:2], in_=t[:, 0:1], func=AF.Abs), rd=[t], wr=[t])
                yield
                P.op("dve", lambda e: e.tensor_scalar(out=t[:, 1:2], in0=t[:, 1:2], scalar1=1.0, scalar2=None, op0=ALU.max), rd=[t], wr=[t])
                yield
                P.op("dve", lambda e: e.reciprocal(out=t[:, 2:3], in_=t[:, 1:2]), rd=[t], wr=[t])
                yield
                P.op("dve", lambda e: e.tensor_tensor(out=t[:, 3:4], in0=t[:, 2:3], in1=Et[:, d, c:c + 1], op=ALU.mult), rd=[t, Et], wr=[t])
                yield
                P.op("dve", lambda e: e.scalar_tensor_tensor(out=hsum[:, c, :], in0=pn[:64, 0:128], scalar=t[:, 3:4], in1=hsum[:, c, :], op0=ALU.mult, op1=ALU.add), rd=[pn, t, (hsum, c)], wr=[(hsum, c)])
                yield
    run_interleaved([scan_dir(0), scan_dir(1)])
    for (c0, ncs) in (SEGS if MODE in (0, 3) else []):
        mo = C.rot("kk0", [64, 16, 128], 2)
        P.dma(mo[:, :ncs, :], motok[:, c0:c0 + ncs, :], rd=[motok], wr=[mo], q="sp")
        P.op("act", lambda e: e.activation(out=mo[:, :ncs, :], in_=mo[:, :ncs, :], func=AF.Sigmoid), rd=[mo], wr=[mo])
        for cl in range(ncs):
            c = c0 + cl
            sq = C.rot("hsq", [64, 128], 2); t = C.rot("den0", [64, 4], 3)
            P.op("act", lambda e: e.activation(out=sq[:], in_=hsum[:, c, :], func=AF.Square, accum_out=t[:, 0:1]), rd=[(hsum, c)], wr=[sq, t])
            P.op("dve", lambda e: e.tensor_scalar(out=t[:, 1:2], in0=t[:, 0:1], scalar1=1.0 / 128, scalar2=EPS, op0=ALU.mult, op1=ALU.add), rd=[t], wr=[t])
            P.op("act", lambda e: e.activation(out=t[:, 1:2], in_=t[:, 1:2], func=AF.Sqrt), rd=[t], wr=[t])
            P.op("dve", lambda e: e.reciprocal(out=t[:, 2:3], in_=t[:, 1:2]), rd=[t], wr=[t])
            P.op("dve", lambda e: e.scalar_tensor_tensor(out=hsum[:, c, :], in0=hsum[:, c, :], scalar=t[:, 2:3], in1=ng[:], op0=ALU.mult, op1=ALU.mult), rd=[(hsum, c), t, ng], wr=[(hsum, c)])
            P.op("dve", lambda e: e.tensor_tensor(out=hsum[:, c, :], in0=hsum[:, c, :], in1=mo[:, cl, :], op=ALU.mult), rd=[(hsum, c), mo], wr=[(hsum, c)])
    for (c0, ncs) in SEGS:
        P.dma(out_m[:, c0:c0 + ncs, :], hsum[:, c0:c0 + ncs, :], rd=[hsum], wr=[(out_m, c0)], q=C.q())


def build_B1():
    nc = bass.Bass("TRN2", target_bir_lowering=False)
    es = ExitStack()
    with es:
        P = Prog(nc, es)
        C = Ctx(P)
        I = lambda n, s: P.dram(n, s, kind="ExternalInput")
        qT = I("qT", [128, 8448]); kT = I("kT", [128, 8448]); ktok = I("ktok", [64, NCH, 128]); vtok = I("vtok", [64, NCH, 128]); motok = I("motok", [64, NCH, 128])
        g4 = I("g4", [64, 4, NCH]); gbias = I("gbias", [64, 4]); tri = I("tri", [64, 2, 64]); normg = I("normg", [64, 128])
        out_m = P.dram("out_m", [64, NCH, 128], kind="ExternalOutput")
        mlstm_part(P, C, qT, kT, ktok, vtok, motok, g4, gbias, tri, normg, out_m)
        P.finish()
        print("phase B1 instructions", P.nins)
    return nc


def tok64(a):
    T, d = a.shape
    return np.ascontiguousarray(a.reshape(T // 64, 64, d).transpose(1, 0, 2))


def mlstm_maps(inp, zT, zcT):
    maps = []
    jj = np.arange(64)
    trif = (jj[:, None] <= jj[None, :]).astype(np.float32)
    tri = np.ascontiguousarray(np.stack([trif, trif.T], 1))
    for k in range(8):
        b, h = k // 4, k % 4
        full = np.concatenate([zcT[b], zT[b]], axis=1)
        rows = lambda base: full[base + h * 128: base + (h + 1) * 128]
        m = {"qT": np.ascontiguousarray(rows(0)), "kT": np.ascontiguousarray(rows(512)),
             "ktok": tok64(rows(512).T), "vtok": tok64(rows(1024).T), "motok": tok64(rows(1536).T)}
        gi = [full[2048 + d * 4 + h] for d in range(2)]; gf = [full[2056 + d * 4 + h] for d in range(2)]
        g4 = np.stack([gi[0], gf[0], gi[1], gf[1]], 0)
        m["g4"] = np.ascontiguousarray(g4.reshape(4, NCH, 64).transpose(2, 0, 1))
        ib, fb = inp["a_i_bias"][0], inp["a_f_bias"][0]
        m["gbias"] = np.ascontiguousarray(np.broadcast_to(np.array([ib[h], fb[h], ib[4 + h], fb[4 + h]], np.float32)[None], (64, 4)))
        m["tri"] = tri
        m["normg"] = np.ascontiguousarray(np.broadcast_to(inp["a_norm_g"][0][h * 128:(h + 1) * 128][None], (64, 128)))
        maps.append(m)
    return maps


import math
I32 = mybir.dt.int32
TWO_PI = float(2 * np.pi)


def hyena_feat(L):
    t = np.arange(L, dtype=np.float32)
    t_norm = (t / np.float32(L - 1)).astype(np.float32)
    w = (np.float32(2.0 * math.pi) * t / np.float32(L)).astype(np.float32)
    bands = np.linspace(1e-4, 15, 16, dtype=np.float32)
    ang = (w[:, None] * bands).astype(np.float32)
    feat = np.concatenate([t_norm[:, None], np.cos(ang), -np.sin(ang)], axis=-1).astype(np.float32)
    return np.ascontiguousarray(feat.T), np.ascontiguousarray(np.broadcast_to(t_norm[None], (128, L)))


def hyena_part(P, C, L, uin, featT, tn, wts, out, tag):
    HA = P.sb("HA" + tag, [128, L]); HB = P.sb("HB" + tag, [128, L]); U = P.sb("U" + tag, [128, L]); Y = P.sb("Y" + tag, [128, L]); X = P.sb("X" + tag, [128, L])
    w1, w2, w3, w4, fr, fb, nd, dsk = (wts[k] for k in ("w1", "w2", "w3", "w4", "fr", "fb", "nd", "dsk"))
    blks = blocks(L)
    nb = len(blks)
    P.dma(U[:], uin[:, 0, :], rd=[uin], wr=[U])
    for o in range(2):
        l1p = P.sb(f"l1p{tag}{o}", [128, 2, nb + 1])
        P.op("dve", lambda e: e.memset(l1p[:], 0.0), wr=[l1p])
        P.dma(X[:], uin[:, 1 + o, :], rd=[uin], wr=[X], q="pool")
        for bi, (s, w) in enumerate(blks):
            ft = C.rot("ft", [33, 512], 2); tnb = C.rot("tnb", [128, 512], 2)
            P.dma(ft[:, :w], featT[:, s:s + w], rd=[featT], wr=[ft], q="sp")
            P.dma(tnb[:, :w], tn[:, s:s + w], rd=[tn], wr=[tnb], q="pool")
            hin, kdim = ft, 33
            for li, wl in enumerate((w1, w2, w3)):
                ps = C.ps()
                P.op("pe", lambda e: e.matmul(ps[:64, :w], wl[:kdim, :], hin[:kdim, :w], start=True, stop=True), rd=[wl, hin], wr=[ps])
                pre = C.rot("pre", [64, 512], 2); kf = C.rot("kf", [64, 512], 2); ki = C.rot("ki", [64, 512], 2, I32)
                P.op("dve", lambda e: e.tensor_scalar(out=pre[:, :w], in0=ps[:64, :w], scalar1=fr[:, 0:1], scalar2=fb[:, li:li + 1], op0=ALU.mult, op1=ALU.add), rd=[ps, fr, fb], wr=[pre])
                P.op("dve", lambda e: e.tensor_scalar(out=kf[:, :w], in0=pre[:, :w], scalar1=1.0 / TWO_PI, scalar2=None, op0=ALU.mult), rd=[pre], wr=[kf])
                P.op("dve", lambda e: e.tensor_copy(out=ki[:, :w], in_=kf[:, :w]), rd=[kf], wr=[ki])
                P.op("dve", lambda e: e.tensor_copy(out=kf[:, :w], in_=ki[:, :w]), rd=[ki], wr=[kf])
                P.op("dve", lambda e: e.scalar_tensor_tensor(out=pre[:, :w], in0=kf[:, :w], scalar=-TWO_PI, in1=pre[:, :w], op0=ALU.mult, op1=ALU.add), rd=[kf, pre], wr=[pre])
                P.op("dve", lambda e: e.tensor_scalar(out=pre[:, :w], in0=pre[:, :w], scalar1=3.14159, scalar2=-3.14159, op0=ALU.min, op1=ALU.max), rd=[pre], wr=[pre])
                hcur = C.rot(f"hl{li}", [64, 512], 2)
                P.op("act", lambda e: e.activation(out=hcur[:, :w], in_=pre[:, :w], func=AF.Sin), rd=[pre], wr=[hcur])
                hin, kdim = hcur, 64
            for di, Hd in enumerate((HA, HB)):
                gi = o * 2 + di
                ps = C.ps()
                P.op("pe", lambda e: e.matmul(ps[:, :w], w4[:, gi, :], hin[:, :w], start=True, stop=True), rd=[w4, hin], wr=[ps])
                dec = C.rot("dec", [128, 512], 2)
                P.op("act", lambda e: e.activation(out=dec[:, :w], in_=tnb[:, :w], func=AF.Exp, scale=nd[:, gi:gi + 1]), rd=[tnb, nd], wr=[dec])
                P.op("dve", lambda e: e.tensor_tensor(out=Hd[:, s:s + w], in0=ps[:, :w], in1=dec[:, :w], op=ALU.mult), rd=[ps, dec], wr=[(Hd, bi)])
                ab = C.rot("ab", [128, 512], 2)
                P.op("act", lambda e: e.activation(out=ab[:, :w], in_=Hd[:, s:s + w], func=AF.Abs, accum_out=l1p[:, di, bi:bi + 1]), rd=[(Hd, bi)], wr=[ab, (l1p, (di, bi))])
        t4 = P.sb(f"t4{tag}{o}", [128, 6])
        P.op("dve", lambda e: e.tensor_reduce(out=t4[:, 0:1], in_=l1p[:], axis=AX.XY, op=ALU.add), rd=[l1p], wr=[t4])
        P.op("act", lambda e: e.activation(out=t4[:, 1:2], in_=HB[:, 0:1], func=AF.Abs), rd=[HB], wr=[t4])
        P.op("dve", lambda e: e.tensor_tensor(out=t4[:, 2:3], in0=t4[:, 0:1], in1=t4[:, 1:2], op=ALU.subtract), rd=[t4], wr=[t4])
        P.op("dve", lambda e: e.reciprocal(out=t4[:, 3:4], in_=t4[:, 2:3]), rd=[t4], wr=[t4])
        P.op("dve", lambda e: e.tensor_scalar(out=HA[:], in0=HA[:], scalar1=t4[:, 3:4], scalar2=None, op0=ALU.mult), rd=[HA, t4], wr=[HA])
        P.op("dve", lambda e: e.tensor_scalar(out=HB[:], in0=HB[:], scalar1=t4[:, 3:4], scalar2=None, op0=ALU.mult), rd=[HB, t4], wr=[HB])
        P.op("dve", lambda e: e.tensor_tensor(out=HA[:, 0:1], in0=HA[:, 0:1], in1=dsk[:, o:o + 1], op=ALU.add), rd=[HA, dsk], wr=[HA])
        P.op("dve", lambda e: e.tensor_scalar(out=Y[:], in0=U[:], scalar1=HA[:, 0:1], scalar2=None, op0=ALU.mult), rd=[U, HA], wr=[Y])
        for off in range(1, L):
            n = L - off
            P.op("dve", lambda e: e.scalar_tensor_tensor(out=Y[:, off:L], in0=U[:, 0:n], scalar=HA[:, off:off + 1], in1=Y[:, off:L], op0=ALU.mult, op1=ALU.add), rd=[U, HA, Y], wr=[Y])
            P.op("dve", lambda e: e.scalar_tensor_tensor(out=Y[:, 0:n], in0=U[:, off:L], scalar=HB[:, off:off + 1], in1=Y[:, 0:n], op0=ALU.mult, op1=ALU.add), rd=[U, HB, Y], wr=[Y])
        P.op("dve", lambda e: e.tensor_tensor(out=U[:], in0=Y[:], in1=X[:], op=ALU.mult), rd=[Y, X], wr=[U])
    P.dma(out[:], U[:], rd=[U], wr=[out])


def build_H(Ls=(8192, 256)):
    nc = bass.Bass("TRN2", target_bir_lowering=False)
    es = ExitStack()
    with es:
        P = Prog(nc, es)
        C = Ctx(P)
        I = lambda n, s: P.dram(n, s, kind="ExternalInput")
        wts = {}
        for nm, shp in (("w1", [33, 64]), ("w2", [64, 64]), ("w3", [64, 64]), ("w4", [64, 4, 128]), ("fr", [64, 1]), ("fb", [64, 3]), ("nd", [128, 4]), ("dsk", [128, 2])):
            d = I("h_" + nm, shp); t = P.sb("s_" + nm, shp); P.dma(t[:], d[:], rd=[d], wr=[t]); wts[nm] = t
        P.op("dve", lambda e: e.tensor_scalar(out=wts["fb"][:], in0=wts["fb"][:], scalar1=wts["fr"][:, 0:1], scalar2=None, op0=ALU.mult), rd=[wts["fb"], wts["fr"]], wr=[wts["fb"]])
        for L in Ls:
            with ExitStack() as es2:
                P.es, old = es2, P.es
                uin = I(f"u{L}", [128, 3, L]); featT = I(f"feat{L}", [33, L]); tn = I(f"tn{L}", [128, L])
                out = P.dram(f"y{L}", [128, L], kind="ExternalOutput")
                hyena_part(P, C, L, uin, featT, tn, wts, out, str(L))
                P.barrier()
                P.es = old
        P.finish()
        print("phase H instructions", P.nins, P.cnt)
    return nc


def hyena_maps(inp, zT, zcT, Ls=(8192, 256)):
    maps = []
    deltas = np.abs(np.linspace(math.log(1e-2) / 1.5, math.log(1e-2) / 0.3, 2048, dtype=np.float32))
    for k in range(8):
        b, h = k // 4, k % 4
        m = {}
        for L, src in ((8192, zT[b]), (256, zcT[b])):
            if L not in Ls:
                continue
            m[f"u{L}"] = np.ascontiguousarray(np.stack([src[2064 + j * 512 + h * 128: 2064 + j * 512 + (h + 1) * 128] for j in range(3)], 1))
            ft, tn = hyena_feat(L)
            m[f"feat{L}"] = ft; m[f"tn{L}"] = tn
        m["h_w1"] = inp["b_f_w1"][0]; m["h_w2"] = inp["b_f_w2"][0]; m["h_w3"] = inp["b_f_w3"][0]
        w4 = inp["b_f_w4"][0]
        m["h_w4"] = np.ascontiguousarray(np.stack([w4[:, g * 512 + h * 128: g * 512 + (h + 1) * 128] for g in range(4)], 1))
        m["h_fr"] = np.ascontiguousarray(inp["b_f_freq"][0][:, None])
        m["h_fb"] = np.ascontiguousarray(np.stack([inp["b_f_b1"][0], inp["b_f_b2"][0], inp["b_f_b3"][0]], 1))
        m["h_nd"] = np.ascontiguousarray(np.stack([-deltas[g * 512 + h * 128: g * 512 + (h + 1) * 128] for g in range(4)], 1))
        m["h_dsk"] = np.ascontiguousarray(inp["b_d"][0][:, h * 128:(h + 1) * 128].T)
        maps.append(m)
    return maps


NF = 16384


def load_consts2(P, specs):
    out = {}
    for nm, shp in specs:
        d = P.dram(nm, shp, kind="ExternalInput"); t = P.sb("c_" + nm, shp)
        P.dma(t[:], d[:], rd=[d], wr=[t]); out[nm] = t
    return out
GCH = 4


def dft_tables():
    a = np.arange(128, dtype=np.float64)
    th128 = 2 * np.pi * np.outer(a, a) / 128.0
    thN = 2 * np.pi * np.outer(a, a) / NF
    C2, S2 = np.cos(th128), np.sin(th128)
    F1 = np.concatenate([C2[:64], -S2[:64]], 1)
    Tc, Ts = np.cos(thN), -np.sin(thN)
    TT1 = np.stack([np.concatenate([Tc, Tc], 1)] * 2, 1)
    TT2 = np.stack([np.concatenate([Ts, Ts], 1)] * 2, 1)
    IC = np.concatenate([C2, S2], 1); IS = np.concatenate([-S2, C2], 1)
    Cf = C2[:, :64] / NF; nSf = -S2[:, :64] / NF
    f = lambda x: np.ascontiguousarray(x.astype(np.float32))
    return {"F1": f(F1), "TT1": f(TT1), "TT2": f(TT2), "C2": f(C2), "S2": f(S2), "IC": f(IC), "IS": f(IS), "Cf": f(Cf), "nSf": f(nSf)}


def fwd_pair(P, C, K, xs, ci, sid):
    pa = C.ps()
    for j in range(2):
        P.op("pe", lambda e: e.matmul(pa[:, j * 256:(j + 1) * 256], xs[:, ci + j, :], K["F1"][:, :], start=True, stop=True), rd=[xs, K["F1"]], wr=[pa])
        yield
    p1 = C.rot("fp1" + sid, [128, 512], 1); p2 = C.rot("fp2" + sid, [128, 512], 1)
    P.op("dve", lambda e: e.tensor_tensor(out=p1[:], in0=pa[:, :512], in1=K["TT1"][:].rearrange("p a b -> p (a b)"), op=ALU.mult), rd=[pa, K["TT1"]], wr=[p1])
    yield
    P.op("dve", lambda e: e.tensor_tensor(out=p2[:], in0=pa[:, :512], in1=K["TT2"][:].rearrange("p a b -> p (a b)"), op=ALU.mult), rd=[pa, K["TT2"]], wr=[p2])
    yield
    b1 = C.rot("fb1" + sid, [128, 2, 256], 1); b2 = C.rot("fb2" + sid, [128, 2, 256], 1)
    v1 = p1[:].rearrange("p (a b) -> p a b", a=2); v2 = p2[:].rearrange("p (a b) -> p a b", a=2)
    P.op("pool", lambda e: e.tensor_tensor(out=b1[:, :, 0:128], in0=v1[:, :, 0:128], in1=v2[:, :, 128:256], op=ALU.subtract), rd=[p1, p2], wr=[(b1, 0)])
    yield
    P.op("pool", lambda e: e.tensor_tensor(out=b1[:, :, 128:256], in0=v2[:, :, 0:128], in1=v1[:, :, 128:256], op=ALU.add), rd=[p1, p2], wr=[(b1, 1)])
    yield
    P.op("act", lambda e: e.copy(out=b2[:, :, 0:128], in_=b1[:, :, 128:256]), rd=[(b1, 1)], wr=[(b2, 0)])
    yield
    P.op("pool", lambda e: e.tensor_tensor(out=b2[:, :, 128:256], in0=v2[:, :, 128:256], in1=v1[:, :, 0:128], op=ALU.subtract), rd=[p1, p2], wr=[(b2, 1)])
    yield
    px = C.ps()
    P.op("pe", lambda e: e.matmul(px[:, :512], K["C2"][:, :], b1[:].rearrange("p a b -> p (a b)"), start=True, stop=False), rd=[K["C2"], b1], wr=[px])
    yield
    P.op("pe", lambda e: e.matmul(px[:, :512], K["S2"][:, :], b2[:].rearrange("p a b -> p (a b)"), start=False, stop=True), rd=[K["S2"], b2], wr=[px])
    yield
    return px


def conv_group(P, C, K, xs, hf, hb, gate, outt, sid):
    qr = C.rot("qr" + sid, [128, GCH, 128], 1); qi = C.rot("qi" + sid, [128, GCH, 128], 1)
    for ci in range(0, GCH, 2):
        pf = yield from fwd_pair(P, C, K, hf, ci, sid)
        hfs = C.rot("hfs" + sid, [128, 512], 1)
        P.op("act", lambda e: e.copy(out=hfs[:], in_=pf[:, :512]), rd=[pf], wr=[hfs])
        yield
        pb = yield from fwd_pair(P, C, K, hb, ci, sid)
        k1 = C.rot("k1" + sid, [128, 2, 256], 1); k2 = C.rot("k2" + sid, [128, 2, 256], 1)
        hv = hfs[:].rearrange("p (a b) -> p a b", a=2); pbv = pb[:, :512].rearrange("p (a b) -> p a b", a=2)
        P.op("dve", lambda e: e.tensor_tensor(out=k1[:, :, 0:128], in0=pbv[:, :, 0:128], in1=hv[:, :, 0:128], op=ALU.add), rd=[pb, hfs], wr=[(k1, 0)])
        yield
        P.op("dve", lambda e: e.tensor_tensor(out=k2[:, :, 0:128], in0=pbv[:, :, 128:256], in1=hv[:, :, 128:256], op=ALU.subtract), rd=[pb, hfs], wr=[(k2, 0)])
        yield
        P.op("act", lambda e: e.copy(out=k1[:, :, 128:256], in_=k1[:, :, 0:128]), rd=[(k1, 0)], wr=[(k1, 1)])
        yield
        P.op("act", lambda e: e.activation(out=k2[:, :, 0:128], in_=k2[:, :, 0:128], func=AF.Identity, scale=-1.0), rd=[(k2, 0)], wr=[(k2, 0)])
        yield
        P.op("act", lambda e: e.copy(out=k2[:, :, 128:256], in_=k2[:, :, 0:128]), rd=[(k2, 0)], wr=[(k2, 1)])
        yield
        pu = yield from fwd_pair(P, C, K, xs, ci, sid)
        p1 = C.rot("fp1" + sid, [128, 512], 1); p2 = C.rot("fp2" + sid, [128, 512], 1)
        P.op("dve", lambda e: e.tensor_tensor(out=p1[:], in0=pu[:, :512], in1=k1[:].rearrange("p a b -> p (a b)"), op=ALU.mult), rd=[pu, k1], wr=[p1])
        yield
        P.op("dve", lambda e: e.tensor_tensor(out=p2[:], in0=pu[:, :512], in1=k2[:].rearrange("p a b -> p (a b)"), op=ALU.mult), rd=[pu, k2], wr=[p2])
        yield
        v1 = p1[:].rearrange("p (a b) -> p a b", a=2); v2 = p2[:].rearrange("p (a b) -> p a b", a=2)
        yr = C.rot("yr" + sid, [128, 2, 128], 1); yi = C.rot("yi" + sid, [128, 2, 128], 1)
        P.op("pool", lambda e: e.tensor_tensor(out=yr[:], in0=v1[:, :, 0:128], in1=v2[:, :, 128:256], op=ALU.subtract), rd=[p1, p2], wr=[yr])
        yield
        P.op("pool", lambda e: e.tensor_tensor(out=yi[:], in0=v2[:, :, 0:128], in1=v1[:, :, 128:256], op=ALU.add), rd=[p1, p2], wr=[yi])
        yield
        pp = C.ps()
        for j in range(2):
            P.op("pe", lambda e: e.matmul(pp[:, j * 256:(j + 1) * 256], yr[:, j, :], K["IC"][:, :], start=True, stop=False), rd=[yr, K["IC"]], wr=[pp])
            yield
            P.op("pe", lambda e: e.matmul(pp[:, j * 256:(j + 1) * 256], yi[:, j, :], K["IS"][:, :], start=False, stop=True), rd=[yi, K["IS"]], wr=[pp])
            yield
        p1 = C.rot("fp1" + sid, [128, 512], 1); p2 = C.rot("fp2" + sid, [128, 512], 1)
        P.op("dve", lambda e: e.tensor_tensor(out=p1[:], in0=pp[:, :512], in1=K["TT1"][:].rearrange("p a b -> p (a b)"), op=ALU.mult), rd=[pp, K["TT1"]], wr=[p1])
        yield
        P.op("dve", lambda e: e.tensor_tensor(out=p2[:], in0=pp[:, :512], in1=K["TT2"][:].rearrange("p a b -> p (a b)"), op=ALU.mult), rd=[pp, K["TT2"]], wr=[p2])
        yield
        v1 = p1[:].rearrange("p (a b) -> p a b", a=2); v2 = p2[:].rearrange("p (a b) -> p a b", a=2)
        P.op("pool", lambda e: e.tensor_tensor(out=qr[:, ci:ci + 2, :], in0=v1[:, :, 0:128], in1=v2[:, :, 128:256], op=ALU.add), rd=[p1, p2], wr=[(qr, ci)])
        yield
        P.op("pool", lambda e: e.tensor_tensor(out=qi[:, ci:ci + 2, :], in0=v1[:, :, 128:256], in1=v2[:, :, 0:128], op=ALU.subtract), rd=[p1, p2], wr=[(qi, ci)])
        yield
    py = C.ps()
    P.op("pe", lambda e: e.matmul(py[:64, :GCH * 128], K["Cf"][:, :], qr[:].rearrange("p a b -> p (a b)"), start=True, stop=False), rd=[K["Cf"], qr], wr=[py])
    yield
    P.op("pe", lambda e: e.matmul(py[:64, :GCH * 128], K["nSf"][:, :], qi[:].rearrange("p a b -> p (a b)"), start=False, stop=True), rd=[K["nSf"], qi], wr=[py])
    yield
    P.op("dve", lambda e: e.tensor_tensor(out=outt[:].rearrange("p a b -> p (a b)"), in0=py[:64, :GCH * 128], in1=gate[:].rearrange("p a b -> p (a b)"), op=ALU.mult), rd=[py, gate], wr=[outt])
    yield


def hyena_filters_to_dram(P, C, L, featT, tn, wts, scratch, tag):
    HA = P.sb("HA" + tag, [128, L]); HB = P.sb("HB" + tag, [128, L])
    w1, w2, w3, w4, fr, fb, nd, dsk = (wts[k] for k in ("w1", "w2", "w3", "w4", "fr", "fb", "nd", "dsk"))
    blks = blocks(L)
    nb = len(blks)
    for o in range(2):
        l1p = P.sb(f"l1p{tag}{o}", [128, 2, nb + 1])
        P.op("dve", lambda e: e.memset(l1p[:], 0.0), wr=[l1p])
        for bi, (s, w) in enumerate(blks):
            ft = C.rot("ft", [33, 512], 2); tnb = C.rot("tnb", [128, 512], 2)
            P.dma(ft[:, :w], featT[:, s:s + w], rd=[featT], wr=[ft], q="sp")
            P.dma(tnb[:, :w], tn[:, s:s + w], rd=[tn], wr=[tnb], q="pool")
            hin, kdim = ft, 33
            for li, wl in enumerate((w1, w2, w3)):
                ps = C.ps()
                P.op("pe", lambda e: e.matmul(ps[:64, :w], wl[:kdim, :], hin[:kdim, :w], start=True, stop=True), rd=[wl, hin], wr=[ps])
                pre = C.rot("pre", [64, 512], 2); kf = C.rot("kf", [64, 512], 2); ki = C.rot("ki", [64, 512], 2, I32)
                P.op("dve", lambda e: e.tensor_scalar(out=pre[:, :w], in0=ps[:64, :w], scalar1=fr[:, 0:1], scalar2=fb[:, li:li + 1], op0=ALU.mult, op1=ALU.add), rd=[ps, fr, fb], wr=[pre])
                P.op("dve", lambda e: e.tensor_scalar(out=kf[:, :w], in0=pre[:, :w], scalar1=1.0 / TWO_PI, scalar2=None, op0=ALU.mult), rd=[pre], wr=[kf])
                P.op("dve", lambda e: e.tensor_copy(out=ki[:, :w], in_=kf[:, :w]), rd=[kf], wr=[ki])
                P.op("dve", lambda e: e.tensor_copy(out=kf[:, :w], in_=ki[:, :w]), rd=[ki], wr=[kf])
                P.op("dve", lambda e: e.scalar_tensor_tensor(out=pre[:, :w], in0=kf[:, :w], scalar=-TWO_PI, in1=pre[:, :w], op0=ALU.mult, op1=ALU.add), rd=[kf, pre], wr=[pre])
                P.op("dve", lambda e: e.tensor_scalar(out=pre[:, :w], in0=pre[:, :w], scalar1=3.14159, scalar2=-3.14159, op0=ALU.min, op1=ALU.max), rd=[pre], wr=[pre])
                hcur = C.rot(f"hl{li}", [64, 512], 2)
                P.op("act", lambda e: e.activation(out=hcur[:, :w], in_=pre[:, :w], func=AF.Sin), rd=[pre], wr=[hcur])
                hin, kdim = hcur, 64
            for di, Hd in enumerate((HA, HB)):
                gi = o * 2 + di
                ps = C.ps()
                P.op("pe", lambda e: e.matmul(ps[:, :w], w4[:, gi, :], hin[:, :w], start=True, stop=True), rd=[w4, hin], wr=[ps])
                dec = C.rot("dec", [128, 512], 2)
                P.op("act", lambda e: e.activation(out=dec[:, :w], in_=tnb[:, :w], func=AF.Exp, scale=nd[:, gi:gi + 1]), rd=[tnb, nd], wr=[dec])
                P.op("dve", lambda e: e.tensor_tensor(out=Hd[:, s:s + w], in0=ps[:, :w], in1=dec[:, :w], op=ALU.mult), rd=[ps, dec], wr=[(Hd, bi)])
                ab = C.rot("ab", [128, 512], 2)
                P.op("act", lambda e: e.activation(out=ab[:, :w], in_=Hd[:, s:s + w], func=AF.Abs, accum_out=l1p[:, di, bi:bi + 1]), rd=[(Hd, bi)], wr=[ab, (l1p, (di, bi))])
        t4 = P.sb(f"t4{tag}{o}", [128, 6])
        P.op("dve", lambda e: e.tensor_reduce(out=t4[:, 0:1], in_=l1p[:], axis=AX.XY, op=ALU.add), rd=[l1p], wr=[t4])
        P.op("act", lambda e: e.activation(out=t4[:, 1:2], in_=HB[:, 0:1], func=AF.Abs), rd=[HB], wr=[t4])
        P.op("dve", lambda e: e.tensor_tensor(out=t4[:, 2:3], in0=t4[:, 0:1], in1=t4[:, 1:2], op=ALU.subtract), rd=[t4], wr=[t4])
        P.op("dve", lambda e: e.reciprocal(out=t4[:, 3:4], in_=t4[:, 2:3]), rd=[t4], wr=[t4])
        P.op("dve", lambda e: e.tensor_scalar(out=HA[:], in0=HA[:], scalar1=t4[:, 3:4], scalar2=None, op0=ALU.mult), rd=[HA, t4], wr=[HA])
        P.op("dve", lambda e: e.tensor_scalar(out=HB[:], in0=HB[:], scalar1=t4[:, 3:4], scalar2=None, op0=ALU.mult), rd=[HB, t4], wr=[HB])
        P.op("dve", lambda e: e.tensor_tensor(out=HA[:, 0:1], in0=HA[:, 0:1], in1=dsk[:, o:o + 1], op=ALU.add), rd=[HA, dsk], wr=[HA])
        P.op("dve", lambda e: e.memset(HB[:, 0:1], 0.0), rd=[], wr=[HB])
        P.dma(scratch[2 * o, :, :], HA[:], rd=[HA], wr=[(scratch, 2 * o)], q="sp")
        P.dma(scratch[2 * o + 1, :, :], HB[:], rd=[HB], wr=[(scratch, 2 * o + 1)], q="pool")


def hyena_dft_part(P, C, uin, featT, tn, wts, K, scratch, out):
    with ExitStack() as es2:
        old, P.es = P.es, es2
        hyena_filters_to_dram(P, C, 8192, featT, tn, wts, scratch, "D")
        P.barrier()
        P.es = old
        C.tmp = {}
    def stream(sid, groups):
        for g in groups:
            c0 = g * GCH
            xs = C.rot("gxs" + sid, [64, GCH, 128], 1); x1 = C.rot("gx1" + sid, [64, GCH, 128], 1); x2 = C.rot("gx2" + sid, [64, GCH, 128], 1)
            P.dma(xs[:], uin[:, 0, c0:c0 + GCH, :], rd=[uin], wr=[xs], q="sp")
            yield
            P.dma(x1[:], uin[:, 1, c0:c0 + GCH, :], rd=[uin], wr=[x1], q="pool")
            yield
            P.dma(x2[:], uin[:, 2, c0:c0 + GCH, :], rd=[uin], wr=[x2], q="sp")
            yield
            hs = []
            for gi in range(4):
                h = C.rot("gh%d" % gi + sid, [64, GCH, 128], 1)
                P.dma(h[:], scratch[gi, c0:c0 + GCH, :].rearrange("c (a b) -> a c b", b=128), rd=[(scratch, gi)], wr=[h], q=C.q())
                yield
                hs.append(h)
            y1 = C.rot("gy1" + sid, [64, GCH, 128], 1); y2 = C.rot("gy2" + sid, [64, GCH, 128], 1)
            yield from conv_group(P, C, K, xs, hs[0], hs[1], x1, y1, sid)
            yield from conv_group(P, C, K, y1, hs[2], hs[3], x2, y2, sid)
            P.dma(out[:, c0:c0 + GCH, :], y2[:], rd=[y2], wr=[(out, g)], q=C.q())
            yield
    NS = 4
    ng = 128 // GCH
    with ExitStack() as es3:
        old, P.es = P.es, es3
        run_interleaved([stream("s%d" % i, range(i, ng, NS)) for i in range(NS)])
        P.barrier()
        P.es = old
        C.tmp = {}


DFT_SPECS = [("F1", [64, 256]), ("TT1", [128, 2, 256]), ("TT2", [128, 2, 256]), ("C2", [128, 128]), ("S2", [128, 128]), ("IC", [128, 256]), ("IS", [128, 256]), ("Cf", [128, 64]), ("nSf", [128, 64])]


def build_H2():
    nc = bass.Bass("TRN2", target_bir_lowering=False)
    es = ExitStack()
    with es:
        P = Prog(nc, es)
        C = Ctx(P)
        I = lambda n, s: P.dram(n, s, kind="ExternalInput")
        wts = {}
        for nm, shp in (("w1", [33, 64]), ("w2", [64, 64]), ("w3", [64, 64]), ("w4", [64, 4, 128]), ("fr", [64, 1]), ("fb", [64, 3]), ("nd", [128, 4]), ("dsk", [128, 2])):
            d = I("h_" + nm, shp); t = P.sb("s_" + nm, shp); P.dma(t[:], d[:], rd=[d], wr=[t]); wts[nm] = t
        P.op("dve", lambda e: e.tensor_scalar(out=wts["fb"][:], in0=wts["fb"][:], scalar1=wts["fr"][:, 0:1], scalar2=None, op0=ALU.mult), rd=[wts["fb"], wts["fr"]], wr=[wts["fb"]])
        K = load_consts2(P, DFT_SPECS)
        uin = I("ud", [64, 3, 128, 128]); featT = I("feat8192", [33, 8192]); tn = I("tn8192", [128, 8192])
        scratch = P.dram("hscr", [4, 128, 8192])
        out = P.dram("yd", [64, 128, 128], kind="ExternalOutput")
        hyena_dft_part(P, C, uin, featT, tn, wts, K, scratch, out)
        P.finish()
        print("phase H2 instructions", P.nins, P.cnt)
    return nc


def hyena_dft_maps(inp, zT, maps):
    tabs = dft_tables()
    for k in range(8):
        b, h = k // 4, k % 4
        m = maps[k]
        u = m.pop("u8192")
        m["ud"] = np.ascontiguousarray(u.reshape(128, 3, 64, 128).transpose(2, 1, 0, 3))
        m.update(tabs)
    return maps


DFF = 2816
DQK = 192


def conv3m(C, z, u, T, H, hm, wcol, bcol, mask, mw=128):
    P = C.P
    n = T - 2 * H
    P.op("dve", lambda e: e.tensor_scalar(out=z[:mw, 0:hm], in0=z[:mw, 0:hm], scalar1=mask[:mw, 0:1], scalar2=None, op0=ALU.mult), rd=[z], wr=[z])
    P.op("dve", lambda e: e.tensor_scalar(out=z[:mw, T - hm:T], in0=z[:mw, T - hm:T], scalar1=mask[:mw, 1:2], scalar2=None, op0=ALU.mult), rd=[z], wr=[z])
    P.op("dve", lambda e: e.tensor_scalar(out=u[:mw, 0:n], in0=z[:mw, H - 1:H - 1 + n], scalar1=wcol[:mw, 0:1], scalar2=bcol[:mw, 0:1], op0=ALU.mult, op1=ALU.add), rd=[z], wr=[u])
    P.op("dve", lambda e: e.scalar_tensor_tensor(out=u[:mw, 0:n], in0=z[:mw, H:H + n], scalar=wcol[:mw, 1:2], in1=u[:mw, 0:n], op0=ALU.mult, op1=ALU.add), rd=[z, u], wr=[u])
    P.op("dve", lambda e: e.scalar_tensor_tensor(out=u[:mw, 0:n], in0=z[:mw, H + 1:H + 1 + n], scalar=wcol[:mw, 2:3], in1=u[:mw, 0:n], op0=ALU.mult, op1=ALU.add), rd=[z, u], wr=[u])


def rms_rs(C, srcs, n, inv_d, name="rsx"):
    P = C.P
    rs = C.rot(name, [128, 1032], 2)
    for bi, (s, w) in enumerate(blocks(n)):
        ps = C.ps()
        for i, (tt, fn, rows) in enumerate(srcs):
            sq = C.rot("sq", [128, 512], 3)
            P.op("act", lambda e: e.activation(out=sq[:rows, :w], in_=fn(s, w), func=AF.Square), rd=[tt], wr=[sq])
            P.op("pe", lambda e: e.matmul(ps[:, :w], C.ones[:rows, :], sq[:rows, :w], start=(i == 0), stop=(i == len(srcs) - 1)), rd=[sq, C.ones], wr=[ps])
        P.op("dve", lambda e: e.tensor_scalar(out=rs[:, s:s + w], in0=ps[:, :w], scalar1=inv_d, scalar2=EPS, op0=ALU.mult, op1=ALU.add), rd=[ps], wr=[(rs, bi)])
        P.op("act", lambda e: e.activation(out=rs[:, s:s + w], in_=rs[:, s:s + w], func=AF.Sqrt), rd=[(rs, bi)], wr=[(rs, bi)])
        P.op("dve", lambda e: e.reciprocal(out=rs[:, s:s + w], in_=rs[:, s:s + w]), rd=[(rs, bi)], wr=[(rs, bi)])
    return rs


def mixer_out_ffn(C, xt, hT, T, mods, col, wout, wup, wdn, fcw, fcb, mask):
    P = C.P
    for m in range(8):
        def consume(ps, bi, s, w, m=m):
            P.op("dve", lambda e: e.scalar_tensor_tensor(out=xt[:, m, s:s + w], in0=ps[:, :w], scalar=mods[:, 16 + m, col:col + 1], in1=xt[:, m, s:s + w], op0=ALU.mult, op1=ALU.add), rd=[ps, mods, (xt, bi)], wr=[(xt, bi)])
        linear_T(C, wout, m * 128, 128, hT, T, consume)
    h2 = hT
    norm_mod(C, xt, h2, T, mods, 3, 4, col)
    n1 = T - 2
    for f in range(DFF // 128):
        zs = []
        for half in range(2):
            z = C.rot("zrow", [128, 1032], 2)
            def consume(ps, bi, s, w, z=z):
                P.op("act", lambda e: e.copy(out=z[:, s:s + w], in_=ps[:, :w]), rd=[ps], wr=[(z, bi)])
            linear_T(C, wup, half * DFF + f * 128, 128, h2, T, consume)
            u = C.rot("urow", [128, 1032], 3)
            conv3m(C, z, u, T, 1, 2, fcw[:, half * 22 + f, :], fcb[:, half * 22 + f:half * 22 + f + 1], mask)
            zs.append(u)
        u1, u2 = zs
        P.op("act", lambda e: e.activation(out=u1[:, :n1], in_=u1[:, :n1], func=AF.Silu), rd=[u1], wr=[u1])
        P.op("dve", lambda e: e.tensor_tensor(out=u1[:, :n1], in0=u1[:, :n1], in1=u2[:, :n1], op=ALU.mult), rd=[u1, u2], wr=[u1])
        wd = C.rot("wd", [128, 1024], 2)
        P.dma(wd[:], wdn[f * 128:(f + 1) * 128, :], rd=[wdn], wr=[wd], q=C.q())
        for m in range(8):
            for bi, (s, w) in enumerate(blocks(n1)):
                ps = C.ps()
                P.op("pe", lambda e: e.matmul(ps[:, :w], wd[:, m * 128:(m + 1) * 128], u1[:, s:s + w], start=True, stop=True), rd=[wd, u1], wr=[ps])
                P.op("dve", lambda e: e.scalar_tensor_tensor(out=xt[:, m, 1 + s:1 + s + w], in0=ps[:, :w], scalar=mods[:, 40 + m, col:col + 1], in1=xt[:, m, 1 + s:1 + s + w], op0=ALU.mult, op1=ALU.add), rd=[ps, mods, xt], wr=[xt])


def load_consts(P, specs):
    out = {}
    for nm, shp in specs:
        d = P.dram(nm, shp, kind="ExternalInput"); t = P.sb("c_" + nm, shp)
        P.dma(t[:], d[:], rd=[d], wr=[t]); out[nm] = t
    return out


SEG_C = [(1024, 0), (1024, 0), (64, 1)]
L1_CHUNKS = ([(i * 128, 128, "qk", i) for i in range(8)] + [(1024 + i * 128, 128, "v", 8 + i) for i in range(4)] +
             [(1536 + i * 128, 128, "plain", None) for i in range(4)] + [(2048, 16, "plain", None)])
GD_ROWS = 2064
ML_ROWS = 4 * 512


def layer1_prep(C, xt, hT, T, n, mods1, col, win1, K, mask, si, gd_out, ml_out):
    P = C.P
    H = 2
    norm_mod(C, xt, hT, T, mods1, 0, 1, col)
    for (col0, mw, kind, idx) in L1_CHUNKS:
        z = C.rot("zrow", [128, 1032], 2)
        def consume(ps, bi, s, w, z=z, mw=mw):
            P.op("act", lambda e: e.copy(out=z[:mw, s:s + w], in_=ps[:mw, :w]), rd=[ps], wr=[(z, bi)])
        linear_T(C, win1, col0, mw, hT, T, consume)
        if kind == "plain":
            P.dma(gd_out[col0:col0 + mw, :], z[:mw, H:H + n], rd=[z], wr=[(gd_out, col0)], q=C.q())
            continue
        u = C.rot("urow", [128, 1032], 3)
        conv3m(C, z, u, T, 2, 2, K["gcw"][:, idx, :], K["gcb"][:, idx:idx + 1], mask)
        P.op("act", lambda e: e.activation(out=u[:, :n], in_=u[:, :n], func=AF.Silu), rd=[u], wr=[u])
        if kind == "qk":
            rs = rms_rs(C, [(u, lambda s, w, u=u: u[:, s:s + w], 128)], n, 1.0)
            P.op("dve", lambda e: e.tensor_tensor(out=u[:, :n], in0=u[:, :n], in1=rs[:, :n], op=ALU.mult), rd=[u, rs], wr=[u])
        P.dma(gd_out[col0:col0 + mw, :], u[:, :n], rd=[u], wr=[(gd_out, col0)], q=C.q())
    if "lq" not in K:
        K["lq"] = P.sb("lq", [128, 3, 1024]); K["lkv"] = P.sb("lkv", [128, 2, 1024]); K["lkr"] = P.sb("lkr", [64, 1024]); K["ropeT"] = P.sb("ropeT", [64, 2, 1024])
    lqf, lkvf, lkrf, ropef = K["lq"], K["lkv"], K["lkr"], K["ropeT"]
    class V:
        def __init__(s_, tt): s_.tt = tt
    lq, lkv, lkr = lqf, lkvf, lkrf
    P.dma(ropef[:, :, :n], K[f"roped{si}"][:], rd=[K[f"roped{si}"]], wr=[ropef])
    def into(dst, dfn, col0, mw):
        def consume(ps, bi, s, w):
            lo, hi = max(s, H), min(s + w, H + n)
            if hi > lo:
                P.op("act", lambda e: e.copy(out=dfn(lo - H, hi - H), in_=ps[:mw, lo - s:hi - s]), rd=[ps], wr=[dst])
        linear_T(C, win1, col0, mw, hT, T, consume)
    for k in range(3):
        into(lq, lambda a, b, k=k: lq[:, k, a:b], 2064 + k * 128, 128)
    for k in range(2):
        into(lkv, lambda a, b, k=k: lkv[:, k, a:b], 2448 + k * 128, 128)
    into(lkr, lambda a, b: lkr[:, a:b], 2704, 64)
    rs = rms_rs(C, [(lq, lambda s, w, k=k: lq[:, k, s:s + w], 128) for k in range(3)], n, 1.0 / 384)
    for k in range(3):
        P.op("dve", lambda e: e.tensor_tensor(out=lq[:, k, :n], in0=lq[:, k, :n], in1=rs[:, :n], op=ALU.mult), rd=[lq, rs], wr=[lq])
        P.op("act", lambda e: e.activation(out=lq[:, k, :n], in_=lq[:, k, :n], func=AF.Identity, scale=K["qng"][:, k:k + 1]), rd=[lq, K["qng"]], wr=[lq])
    rs = rms_rs(C, [(lkv, lambda s, w, k=k: lkv[:, k, s:s + w], 128) for k in range(2)], n, 1.0 / 256)
    for k in range(2):
        P.op("dve", lambda e: e.tensor_tensor(out=lkv[:, k, :n], in0=lkv[:, k, :n], in1=rs[:, :n], op=ALU.mult), rd=[lkv, rs], wr=[lkv])
        P.op("act", lambda e: e.activation(out=lkv[:, k, :n], in_=lkv[:, k, :n], func=AF.Identity, scale=K["kvng"][:, k:k + 1]), rd=[lkv, K["kvng"]], wr=[lkv])
    rope = ropef

    def up(wdram, kc, src, col0, mw, dst):
        wt = C.rot("wup2", [128, 3, 128], 3)
        P.dma(wt[:, :kc, :mw], wdram[:, col0:col0 + mw].rearrange("(k p) m -> p k m", p=128), rd=[wdram], wr=[wt], q=C.q())
        for bi, (s, w) in enumerate(blocks(n)):
            ps = C.ps()
            for k in range(kc):
                P.op("pe", lambda e: e.matmul(ps[:mw, :w], wt[:, k, :mw], src[:, k, s:s + w], start=(k == 0), stop=(k == kc - 1)), rd=[wt, src], wr=[ps])
            P.op("act", lambda e: e.copy(out=dst[:mw, s:s + w], in_=ps[:mw, :w]), rd=[ps], wr=[dst])

    def finish_qk(A, B, gname, scale, h, rowbase):
        rs = rms_rs(C, [(A, lambda s, w: A[:, s:s + w], 128), (B, lambda s, w: B[:64, s:s + w], 64)], n, 1.0 / DQK)
        P.op("dve", lambda e: e.tensor_tensor(out=A[:, :n], in0=A[:, :n], in1=rs[:, :n], op=ALU.mult), rd=[A, rs], wr=[A])
        P.op("dve", lambda e: e.tensor_scalar(out=A[:, :n], in0=A[:, :n], scalar1=K[gname][:, 0:1], scalar2=scale, op0=ALU.mult, op1=ALU.mult), rd=[A, K[gname]], wr=[A])
        Bn = C.rot("Bn", [64, 1024], 1)
        P.op("dve", lambda e: e.tensor_tensor(out=Bn[:, :n], in0=B[:64, :n], in1=rs[:64, :n], op=ALU.mult), rd=[B, rs], wr=[Bn])
        P.op("dve", lambda e: e.tensor_scalar(out=Bn[:, :n], in0=Bn[:, :n], scalar1=K[gname][:64, 1:2], scalar2=scale, op0=ALU.mult, op1=ALU.mult), rd=[Bn, K[gname]], wr=[Bn])
        Br = C.rot("Br", [64, 1024], 1)
        for bi, (s, w) in enumerate(blocks(n)):
            ps = C.ps()
            P.op("pe", lambda e: e.matmul(ps[:64, :w], K["JT"][:, :], Bn[:, s:s + w], start=True, stop=True), rd=[K["JT"], Bn], wr=[ps])
            P.op("dve", lambda e: e.tensor_tensor(out=Br[:, s:s + w], in0=ps[:64, :w], in1=rope[:, 1, s:s + w], op=ALU.mult), rd=[ps, rope], wr=[Br])
        P.op("dve", lambda e: e.tensor_tensor(out=Bn[:, :n], in0=Bn[:, :n], in1=rope[:, 0, :n], op=ALU.mult), rd=[Bn, rope], wr=[Bn])
        P.op("dve", lambda e: e.tensor_tensor(out=Bn[:, :n], in0=Bn[:, :n], in1=Br[:, :n], op=ALU.add), rd=[Bn, Br], wr=[Bn])
        P.dma(ml_out[rowbase:rowbase + 128, :], A[:, :n], rd=[A], wr=[(ml_out, rowbase)], q=C.q())
        P.dma(ml_out[rowbase + 128:rowbase + 192, :], Bn[:, :n], rd=[Bn], wr=[(ml_out, rowbase + 128)], q=C.q())

    for h in range(4):
        base = h * 512
        A = C.rot("qA", [128, 1024], 2); B = C.rot("qB", [128, 1024], 2)
        up(K["wq"], 3, lq, h * DQK, 128, A)
        up(K["wq"], 3, lq, h * DQK + 128, 64, B)
        finish_qk(A, B, "qn2", float(DQK ** -0.5), h, base)
        A = C.rot("qA", [128, 1024], 2)
        up(K["wkv"], 2, lkv, h * 256, 128, A)
        finish_qk(A, lkr, "kn2", 1.0, h, base + 192)
        V = C.rot("qB", [128, 1024], 2)
        up(K["wkv"], 2, lkv, h * 256 + 128, 128, V)
        P.dma(ml_out[base + 384:base + 512, :], V[:, :n], rd=[V], wr=[(ml_out, base + 384)], q=C.q())


def build_C(layer1=True, H=2, segs=SEG_C):
    nc = bass.Bass("TRN2", target_bir_lowering=False)
    es = ExitStack()
    with es:
        P = Prog(nc, es)
        C = Ctx(P)
        I = lambda nm, s: P.dram(nm, s, kind="ExternalInput")
        xin = [I(f"x{i}", [128, 8, n + 2 * H]) for i, (n, _) in enumerate(segs)]
        min_ = [I(f"m{i}", [128, 8, n + 2 * H]) for i, (n, _) in enumerate(segs)]
        cT = I("cT", [128, 8, 2])
        modw = I("modw", [1024, 6144]); modbT = I("modbT", [128, 48])
        wout = I("wout", [1024, 1024]); wup = I("wup", [1024, 2 * DFF]); wdn = I("wdn", [DFF, 1024])
        specs = [("masks", [128, len(segs), 2]), ("fcw", [128, 44, 3]), ("fcb", [128, 44])]
        if layer1:
            modw1 = I("modw1", [1024, 6144]); modbT1 = I("modbT1", [128, 48]); win1 = I("win1", [1024, 2768])
            specs += [("gcw", [128, 12, 3]), ("gcb", [128, 12]), ("qng", [128, 3]), ("kvng", [128, 2]), ("qn2", [128, 2]), ("kn2", [128, 2]), ("JT", [64, 64])]
        K = load_consts(P, specs)
        if layer1:
            for i, (n, _) in enumerate(segs):
                K[f"roped{i}"] = I(f"rope{i}", [64, 2, n])
        if layer1:
            K["wq"] = I("wq", [384, 768]); K["wkv"] = I("wkv", [256, 1024])
        xo = [P.dram(f"xo{i}", [128, 8, n], kind="ExternalOutput") for i, (n, _) in enumerate(segs)]
        if layer1:
            gd = [P.dram(f"gd{i}", [GD_ROWS, n], kind="ExternalOutput") for i, (n, _) in enumerate(segs)]
            ml = [P.dram(f"ml{i}", [ML_ROWS, n], kind="ExternalOutput") for i, (n, _) in enumerate(segs)]
        mods = compute_mods(C, modw, modbT, cT)
        if layer1:
            sc1 = None
            mods1 = compute_mods1(C, modw1, modbT1, cT)
        TM = max(n for n, _ in segs) + 2 * H
        xt = P.sb("xt", [128, 8, TM]); hT = P.sb("hT", [128, 8, TM])
        for si, (n, col) in enumerate(segs):
            T = n + 2 * H
            P.dma(xt[:, :, :T], xin[si][:], rd=[xin[si]], wr=[xt])
            P.dma(hT[:, :, :T], min_[si][:], rd=[min_[si]], wr=[hT], q="pool")
            mixer_out_ffn(C, xt, hT, T, mods, col, wout, wup, wdn, K["fcw"], K["fcb"], K["masks"][:, si, :])
            P.dma(xo[si][:], xt[:, :, H:H + n], rd=[xt], wr=[xo[si]])
            if layer1:
                layer1_prep(C, xt, hT, T, n, mods1, col, win1, K, K["masks"][:, si, :], si, gd[si], ml[si])
        P.finish()
        print("phase C instructions", P.nins, P.cnt)
    return nc


def compute_mods1(C, modw, modbT, cT):
    P = C.P
    sc = P.sb("sc1", [128, 8, 2]); P.dma(sc[:], cT[:], rd=[cT], wr=[sc])
    P.op("act", lambda e: e.activation(out=sc[:], in_=sc[:], func=AF.Silu), rd=[sc], wr=[sc])
    mb = P.sb("mb1", [128, 48]); P.dma(mb[:], modbT[:], rd=[modbT], wr=[mb])
    mods = P.sb("mods1", [128, 48, 2])
    for m in range(48):
        wt = C.rot("win", [128, 8, 128], 3)
        P.dma(wt[:], modw[:, m * 128:(m + 1) * 128].rearrange("(k p) m -> p k m", p=128), rd=[modw], wr=[wt], q=C.q())
        ps = C.ps()
        for k in range(8):
            P.op("pe", lambda e: e.matmul(ps[:, 0:2], wt[:, k, :], sc[:, k, :], start=(k == 0), stop=(k == 7)), rd=[wt, sc], wr=[ps])
        P.op("dve", lambda e: e.tensor_scalar(out=mods[:, m, :], in0=ps[:, 0:2], scalar1=mb[:, m:m + 1], scalar2=None, op0=ALU.add), rd=[ps, mb], wr=[(mods, m)])
    for j in (1, 4):
        P.op("dve", lambda e: e.tensor_scalar(out=mods[:, j * 8:(j + 1) * 8, :], in0=mods[:, j * 8:(j + 1) * 8, :], scalar1=1.0, scalar2=None, op0=ALU.add), rd=[mods], wr=[mods])
    return mods


def rope_tab(pos):
    n = len(pos)
    row = (pos // 64).astype(np.float32); colp = (pos % 64).astype(np.float32)
    inv = (np.float32(10000.0) ** (-np.arange(16, dtype=np.float32) / np.float32(16))).astype(np.float32)
    ang = np.concatenate([row[:, None] * inv, colp[:, None] * inv], -1).astype(np.float32)
    c, s = np.cos(ang).astype(np.float32).T, np.sin(ang).astype(np.float32).T
    return np.ascontiguousarray(np.stack([np.concatenate([c, c], 0), np.concatenate([s, s], 0)], 1))


def c_maps(inp, layer, xfull, cfull, mixfull, mixc, layer1=True, H=2):
    maps = []
    JT = np.zeros((64, 64), np.float32)
    for m_ in range(32):
        JT[m_ + 32, m_] = -1.0; JT[m_, m_ + 32] = 1.0
    for k in range(8):
        b, qtr = k // 4, k % 4
        m = {}
        mk = np.zeros((128, 3, 2), np.float32)
        for si in range(2):
            st = qtr * 2048 + si * 1024
            m[f"x{si}"], l, r = seg_slice_T(xfull[b], st, 1024, H); mk[:, si, 0] = l; mk[:, si, 1] = r
            m[f"m{si}"], _, _ = seg_slice_T(mixfull[b], st, 1024, H)
            if layer1:
                m[f"rope{si}"] = rope_tab(np.arange(st, st + 1024))
        m["x2"], l, r = seg_slice_T(cfull[b], qtr * 64, 64, H); mk[:, 2, 0] = l; mk[:, 2, 1] = r
        m["m2"], _, _ = seg_slice_T(mixc[b], qtr * 64, 64, H)
        m["masks"] = mk
        m["cT"] = np.ascontiguousarray(np.stack([colT(inp["c"][b]), colT(inp["c_ctx"])], -1))
        m["modw"] = inp["mod_w"][layer]; m["modbT"] = colT(inp["mod_b"][layer])
        m["wout"] = inp["ab_w_out"][0] if layer == 0 else inp["cd_w_out"][0]
        m["wup"] = inp["ffn_w_up"][layer]; m["wdn"] = inp["ffn_w_down"][layer]
        m["fcw"] = np.ascontiguousarray(inp["ffn_conv_w"][layer].T.reshape(44, 128, 3).transpose(1, 0, 2))
        m["fcb"] = colT(inp["ffn_conv_b"][layer])
        if layer1:
            r0 = np.zeros((64, 2, 64), np.float32); r0[:, 0, :] = 1.0
            m["rope2"] = r0
            m["modw1"] = inp["mod_w"][1]; m["modbT1"] = colT(inp["mod_b"][1]); m["win1"] = inp["cd_w_in"][0]
            m["gcw"] = np.ascontiguousarray(inp["c_conv_w"][0].T.reshape(12, 128, 3).transpose(1, 0, 2)); m["gcb"] = colT(inp["c_conv_b"][0])
            m["qng"] = colT(inp["d_q_norm_g"][0]); m["kvng"] = colT(inp["d_kv_norm_g"][0])
            def g2(v):
                o = np.zeros((128, 2), np.float32); o[:, 0] = v[:128]; o[:64, 1] = v[128:192]
                return o
            m["qn2"] = g2(inp["d_qn_g"][0]); m["kn2"] = g2(inp["d_kn_g"][0]); m["JT"] = JT
            m["wq"] = inp["d_w_q_up"][0]; m["wkv"] = inp["d_w_kv_up"][0]
        maps.append(m)
    return maps


def unT(a):
    return a.transpose(2, 1, 0).reshape(a.shape[2], 1024)


def gather_C(results, layer1=True):
    x2 = np.zeros((2, 8192, 1024), np.float32); c2 = np.zeros((2, 256, 1024), np.float32)
    gdT = np.zeros((2, GD_ROWS, 8192), np.float32); gdc = np.zeros((2, GD_ROWS, 256), np.float32)
    mlT = np.zeros((2, ML_ROWS, 8192), np.float32); mlc = np.zeros((2, ML_ROWS, 256), np.float32)
    for k in range(8):
        b, qtr = k // 4, k % 4
        r = results[k]
        for si in range(2):
            sl = slice(qtr * 2048 + si * 1024, qtr * 2048 + (si + 1) * 1024)
            x2[b][sl] = unT(r[f"xo{si}"])
            if layer1:
                gdT[b][:, sl] = r[f"gd{si}"]; mlT[b][:, sl] = r[f"ml{si}"]
        if "xo2" in r:
            cs = slice(qtr * 64, (qtr + 1) * 64)
            c2[b][cs] = unT(r["xo2"])
            if layer1:
                gdc[b][:, cs] = r["gd2"]; mlc[b][:, cs] = r["ml2"]
    return x2, c2, gdT, gdc, mlT, mlc


def gdn_part(P, C, qT, kT, ktok, vtok, ggtok, gab, gconst, cm, normg, out_g):
    DK = 128
    M = P.sb("gM", [64, 5, 64]); P.dma(M[:], cm[:], rd=[cm], wr=[M])
    gc = P.sb("ggc", [64, 4]); P.dma(gc[:], gconst[:], rd=[gconst], wr=[gc])
    ng = P.sb("gng", [64, 128]); P.dma(ng[:], normg[:], rd=[normg], wr=[ng])
    g4 = P.sb("gg4", [64, 4, NCH]); P.dma(g4[:], gab[:], rd=[gab], wr=[g4])
    hsum = P.sb("ghsum", [64, NCH, 128])
    S = P.sb("gS", [128, 128])
    la = P.sb("gla", [64, 2, NCH]); gs = P.sb("ggs", [64, 2, NCH]); eg = P.sb("geg", [64, 2, NCH]); egl = P.sb("gegl", [64, 2, NCH])
    nbeg = P.sb("gnbeg", [64, 2, NCH]); nbeta = P.sb("gnbeta", [64, 2, NCH]); eglast = P.sb("geglast", [128, 2, NCH]); Aex = P.sb("gAex", [64, 2])
    IDN = M[:, 4, :]
    for d in range(2):
        beta, ga = g4[:, 2 * d, :], g4[:, 2 * d + 1, :]
        P.op("act", lambda e: e.activation(out=beta, in_=beta, func=AF.Sigmoid), rd=[g4], wr=[g4])
        P.op("act", lambda e: e.activation(out=Aex[:, d:d + 1], in_=gc[:, 2 * d:2 * d + 1], func=AF.Exp), rd=[gc], wr=[Aex])
        P.op("act", lambda e: e.activation(out=ga, in_=ga, func=AF.Exp, bias=gc[:, 2 * d + 1:2 * d + 2], scale=1.0), rd=[g4, gc], wr=[g4])
        P.op("dve", lambda e: e.tensor_scalar(out=ga, in0=ga, scalar1=1.0, scalar2=None, op0=ALU.add), rd=[g4], wr=[g4])
        P.op("act", lambda e: e.activation(out=ga, in_=ga, func=AF.Ln), rd=[g4], wr=[g4])
        P.op("dve", lambda e: e.tensor_scalar(out=la[:, d, :], in0=ga, scalar1=Aex[:, d:d + 1], scalar2=-1.0, op0=ALU.mult, op1=ALU.mult), rd=[g4, Aex], wr=[la])
        ps = C.ps()
        P.op("pe", lambda e: e.matmul(ps[:64, :NCH], M[:, d, :], la[:, d, :], start=True, stop=True), rd=[M, la], wr=[ps])
        ps2 = C.ps()
        P.op("pe", lambda e: e.matmul(ps2[:, :NCH], C.ones[:64, :], la[:, d, :], start=True, stop=True), rd=[C.ones, la], wr=[ps2])
        P.op("act", lambda e: e.copy(out=gs[:, d, :], in_=ps[:64, :NCH]), rd=[ps], wr=[gs])
        P.op("act", lambda e: e.activation(out=eg[:, d, :], in_=gs[:, d, :], func=AF.Exp), rd=[gs], wr=[eg])
        P.op("act", lambda e: e.activation(out=eglast[:, d, :], in_=ps2[:, :NCH], func=AF.Exp), rd=[ps2], wr=[eglast])
        tmp = C.rot("gtmp", [64, NCH], 2)
        P.op("dve", lambda e: e.tensor_tensor(out=tmp[:], in0=ps2[:64, :NCH], in1=gs[:, d, :], op=ALU.subtract), rd=[ps2, gs], wr=[tmp])
        P.op("act", lambda e: e.activation(out=egl[:, d, :], in_=tmp[:], func=AF.Exp), rd=[tmp], wr=[egl])
        P.op("dve", lambda e: e.tensor_scalar(out=nbeta[:, d, :], in0=beta, scalar1=-1.0, scalar2=None, op0=ALU.mult), rd=[g4], wr=[nbeta])
        P.op("dve", lambda e: e.tensor_tensor(out=nbeg[:, d, :], in0=nbeta[:, d, :], in1=eg[:, d, :], op=ALU.mult), rd=[nbeta, eg], wr=[nbeg])
    Ss = [S, P.sb("gS_b", [128, 128])]
    P.op("dve", lambda e: e.memset(hsum[:], 0.0), wr=[hsum])
    def indep(d, Q):
        MI, MIT, MS = (M[:, 1, :], M[:, 0, :], M[:, 3, :]) if d == 0 else (M[:, 0, :], M[:, 1, :], M[:, 2, :])
        segs = SEGS if d == 0 else [SEGS[0]] + SEGS[:0:-1]
        for (c0, ncs) in segs:
            T = ncs * 64
            qs = C.rot("qs%d" % d, [128, 1024], 2); ks = C.rot("ks%d" % d, [128, 1024], 2)
            kk = C.rot("kk%d" % d, [64, 16, 128], 2); vv = C.rot("vv%d" % d, [64, 16, 129], 2)
            P.dma(qs[:, :T], qT[:, c0 * 64:c0 * 64 + T], rd=[qT], wr=[qs], q="sp")
            yield
            P.dma(ks[:, :T], kT[:, c0 * 64:c0 * 64 + T], rd=[kT], wr=[ks], q="pool")
            yield
            P.dma(kk[:, :ncs, :], ktok[:, c0:c0 + ncs, :], rd=[ktok], wr=[kk], q="sp")
            yield
            P.dma(vv[:, :ncs, 0:128], vtok[:, c0:c0 + ncs, :], rd=[vtok], wr=[vv], q="pool")
            yield
            order = range(ncs) if d == 0 else range(ncs - 1, -1, -1)
            for cl in order:
                c = c0 + cl
                tk = slice(cl * 64, cl * 64 + 64)
                pg = C.ps()
                P.op("pe", lambda e: e.matmul(pg[:, :64], la[:, d, c:c + 1].to_broadcast([64, 128]), M[:, d, :], start=True, stop=True), rd=[la, M], wr=[pg])
                yield
                egrow = C.rot("egrow%d" % d, [128, 64], 2)
                P.op("act", lambda e: e.activation(out=egrow[:], in_=pg[:, :64], func=AF.Exp), rd=[pg], wr=[egrow])
                yield
                dm = C.rot("dm%d" % d, [64, 64], 2); dmt = C.rot("dmt%d" % d, [64, 64], 2)
                P.op("dve", lambda e: e.tensor_scalar(out=dm[:], in0=pg[:64, :64], scalar1=gs[:, d, c:c + 1], scalar2=0.0, op0=ALU.subtract, op1=ALU.max), rd=[pg, gs], wr=[dm])
                yield
                P.op("dve", lambda e: e.tensor_scalar(out=dmt[:], in0=pg[:64, :64], scalar1=gs[:, d, c:c + 1], scalar2=0.0, op0=ALU.subtract, op1=ALU.min), rd=[pg, gs], wr=[dmt])
                yield
                P.op("act", lambda e: e.activation(out=dm[:], in_=dm[:], func=AF.Exp, scale=-1.0), rd=[dm], wr=[dm])
                yield
                P.op("act", lambda e: e.activation(out=dmt[:], in_=dmt[:], func=AF.Exp), rd=[dmt], wr=[dmt])
                yield
                P.op("dve", lambda e: e.tensor_tensor(out=dm[:], in0=dm[:], in1=MS, op=ALU.mult), rd=[dm, M], wr=[dm])
                yield
                P.op("dve", lambda e: e.tensor_tensor(out=dmt[:], in0=dmt[:], in1=MIT, op=ALU.mult), rd=[dmt, M], wr=[dmt])
                yield
                qtl = C.rot("qtl%d" % d, [128, 64], 3)
                P.op("dve", lambda e: e.tensor_tensor(out=qtl[:], in0=qs[:, tk], in1=egrow[:], op=ALU.mult), rd=[qs, egrow], wr=[qtl])
                yield
                pG = C.ps()
                P.op("pe", lambda e: e.matmul(pG[:64, :64], ks[:, tk], ks[:, tk], start=True, stop=True), rd=[ks], wr=[pG])
                yield
                XT = C.rot("XT%d" % d, [64, 64], 2); X = C.rot("X%d" % d, [64, 64], 2); Rm = C.rot("Rm%d" % d, [64, 64], 2)
                P.op("dve", lambda e: e.tensor_tensor(out=XT[:], in0=pG[:64, :64], in1=dm[:], op=ALU.mult), rd=[pG, dm], wr=[XT])
                yield
                P.op("dve", lambda e: e.tensor_scalar(out=XT[:], in0=XT[:], scalar1=nbeta[:, d, c:c + 1], scalar2=None, op0=ALU.mult), rd=[XT, nbeta], wr=[XT])
                yield
                pa = C.ps()
                P.op("pe", lambda e: e.matmul(pa[:64, :64], ks[:, tk], qs[:, tk], start=True, stop=True), rd=[ks, qs], wr=[pa])
                yield
                at = C.rot("at%d" % d, [64, 64], 3)
                P.op("dve", lambda e: e.tensor_tensor(out=at[:], in0=pa[:64, :64], in1=dmt[:], op=ALU.mult), rd=[pa, dmt], wr=[at])
                yield
                px = C.ps()
                P.op("pe", lambda e: e.matmul(px[:64, :64], XT[:], IDN, start=True, stop=True), rd=[XT, M], wr=[px])
                yield
                P.op("act", lambda e: e.copy(out=X[:], in_=px[:64, :64]), rd=[px], wr=[X])
                yield
                P.op("dve", lambda e: e.tensor_tensor(out=Rm[:], in0=px[:64, :64], in1=IDN, op=ALU.add), rd=[px, M], wr=[Rm])
                yield
                for lvl in range(5):
                    pyt = C.ps()
                    P.op("pe", lambda e: e.matmul(pyt[:64, :64], X[:], XT[:], start=True, stop=True), rd=[X, XT], wr=[pyt])
                    yield
                    if lvl < 4:
                        py = C.ps()
                        P.op("pe", lambda e: e.matmul(py[:64, :64], XT[:], X[:], start=True, stop=True), rd=[X, XT], wr=[py])
                        yield
                    XT2 = C.rot("XT%d" % d, [64, 64], 2)
                    P.op("act", lambda e: e.copy(out=XT2[:], in_=pyt[:64, :64]), rd=[pyt], wr=[XT2])
                    yield
                    if lvl < 4:
                        X2 = C.rot("X%d" % d, [64, 64], 2)
                        P.op("dve", lambda e: e.tensor_copy(out=X2[:], in_=py[:64, :64]), rd=[py], wr=[X2])
                        yield
                    pr = C.ps()
                    P.op("pe", lambda e: e.matmul(pr[:64, :64], XT2[:], Rm[:], start=True, stop=True), rd=[XT2, Rm], wr=[pr])
                    yield
                    R2 = C.rot(("Rf%d" if lvl == 4 else "Rm%d") % d, [64, 64], 3 if lvl == 4 else 2)
                    P.op("dve", lambda e: e.tensor_tensor(out=R2[:], in0=pr[:64, :64], in1=Rm[:], op=ALU.add), rd=[pr, Rm], wr=[R2])
                    yield
                    XT, Rm = XT2, R2
                    if lvl < 4:
                        X = X2
                vb = C.rot("vb%d" % d, [64, 128], 3); kh = C.rot("kh%d" % d, [64, 128], 3)
                P.op("dve", lambda e: e.tensor_scalar(out=vb[:], in0=vv[:, cl, 0:128], scalar1=nbeta[:, d, c:c + 1], scalar2=-1.0, op0=ALU.mult, op1=ALU.mult), rd=[vv, nbeta], wr=[vb])
                yield
                P.op("act", lambda e: e.activation(out=kh[:], in_=kk[:, cl, :], func=AF.Identity, scale=egl[:, d, c:c + 1]), rd=[kk, egl], wr=[kh])
                yield
                Q.append((c, cl, tk, ks, Rm, at, qtl, vb, kh))
                while len(Q) >= 2:
                    yield
    def dep(d, Q):
        S = Ss[d]
        P.op("dve", lambda e: e.memset(S[:], 0.0), wr=[S])
        yield
        for _ in range(NCH):
            while not Q:
                yield
            (c, cl, tk, ks, Rm, at, qtl, vb, kh) = Q.pop(0)
            pks = C.ps()
            P.op("pe", lambda e: e.matmul(pks[:64, :128], ks[:, tk], S[:], start=True, stop=True), rd=[ks, S], wr=[pks])
            yield
            rr = C.rot("rr%d" % d, [64, 128], 2)
            P.op("dve", lambda e: e.scalar_tensor_tensor(out=rr[:], in0=pks[:64, :128], scalar=nbeg[:, d, c:c + 1], in1=vb[:], op0=ALU.mult, op1=ALU.add), rd=[pks, nbeg, vb], wr=[rr])
            yield
            pv = C.ps()
            P.op("pe", lambda e: e.matmul(pv[:64, :128], Rm[:], rr[:], start=True, stop=True), rd=[Rm, rr], wr=[pv])
            yield
            vn = C.rot("vn%d" % d, [64, 128], 2)
            P.op("act", lambda e: e.copy(out=vn[:], in_=pv[:64, :128]), rd=[pv], wr=[vn])
            yield
            po = C.ps()
            P.op("pe", lambda e: e.matmul(po[:64, :128], at[:], vn[:], start=True, stop=False), rd=[at, vn], wr=[po])
            yield
            P.op("pe", lambda e: e.matmul(po[:64, :128], qtl[:], S[:], start=False, stop=True), rd=[qtl, S], wr=[po])
            yield
            pu = C.ps()
            P.op("pe", lambda e: e.matmul(pu[:, :128], kh[:], vn[:], start=True, stop=True), rd=[kh, vn], wr=[pu])
            yield
            P.op("dve", lambda e: e.tensor_scalar(out=S[:], in0=S[:], scalar1=eglast[:, d, c:c + 1], scalar2=None, op0=ALU.mult), rd=[S, eglast], wr=[S])
            yield
            P.op("dve", lambda e: e.tensor_tensor(out=S[:], in0=pu[:, :128], in1=S[:], op=ALU.add), rd=[S, pu], wr=[S])
            yield
            sc = float(DK ** -0.5)
            P.op("dve", lambda e: e.scalar_tensor_tensor(out=hsum[:, c, :], in0=po[:64, :128], scalar=sc, in1=hsum[:, c, :], op0=ALU.mult, op1=ALU.add), rd=[po, (hsum, c)], wr=[(hsum, c)])
            yield
    Qs = [[], []]
    run_interleaved([indep(0, Qs[0]), dep(0, Qs[0]), indep(1, Qs[1]), dep(1, Qs[1])], C, [(0, 2), (4, 2), (2, 2), (6, 2)])
    for (c0, ncs) in SEGS:
        mo = C.rot("kk0", [64, 16, 128], 2)
        P.dma(mo[:, :ncs, :], ggtok[:, c0:c0 + ncs, :], rd=[ggtok], wr=[mo], q="sp")
        P.op("act", lambda e: e.activation(out=mo[:, :ncs, :], in_=mo[:, :ncs, :], func=AF.Silu), rd=[mo], wr=[mo])
        for cl in range(ncs):
            c = c0 + cl
            sq = C.rot("hsq", [64, 128], 2); t = C.rot("den0", [64, 4], 3)
            P.op("act", lambda e: e.activation(out=sq[:], in_=hsum[:, c, :], func=AF.Square, accum_out=t[:, 0:1]), rd=[(hsum, c)], wr=[sq, t])
            P.op("dve", lambda e: e.tensor_scalar(out=t[:, 1:2], in0=t[:, 0:1], scalar1=1.0 / 128, scalar2=EPS, op0=ALU.mult, op1=ALU.add), rd=[t], wr=[t])
            P.op("act", lambda e: e.activation(out=t[:, 1:2], in_=t[:, 1:2], func=AF.Sqrt), rd=[t], wr=[t])
            P.op("dve", lambda e: e.reciprocal(out=t[:, 2:3], in_=t[:, 1:2]), rd=[t], wr=[t])
            P.op("dve", lambda e: e.scalar_tensor_tensor(out=hsum[:, c, :], in0=hsum[:, c, :], scalar=t[:, 2:3], in1=ng[:], op0=ALU.mult, op1=ALU.mult), rd=[(hsum, c), t, ng], wr=[(hsum, c)])
            P.op("dve", lambda e: e.tensor_tensor(out=hsum[:, c, :], in0=hsum[:, c, :], in1=mo[:, cl, :], op=ALU.mult), rd=[(hsum, c), mo], wr=[(hsum, c)])
        P.dma(out_g[:, c0:c0 + ncs, :], hsum[:, c0:c0 + ncs, :], rd=[hsum], wr=[(out_g, c0)], q=C.q())


def mla_part(P, C, qA, qB, kA, kB, vtok, out_a):
    NKT = 66
    kAs = P.sb("mkA", [128, 8448]); kBs = P.sb("mkB", [64, 8448]); va = P.sb("mva", [128, NKT, 129])
    P.dma(kAs[:], kA[:], rd=[kA], wr=[kAs], q="sp")
    P.dma(kBs[:], kB[:], rd=[kB], wr=[kBs], q="pool")
    P.op("dve", lambda e: e.memset(va[:, :, 128:129], 1.0), wr=[va])
    P.dma(va[:, :, 0:128], vtok[:], rd=[vtok], wr=[va], q="sp")
    accs = C.psb[0:4]
    sci = [0]
    for qg in range(16):
        qa = C.rot("mqa", [128, 512], 2); qb = C.rot("mqb", [64, 512], 2)
        P.dma(qa[:], qA[:, qg * 512:(qg + 1) * 512], rd=[qA], wr=[qa], q="sp")
        P.dma(qb[:], qB[:, qg * 512:(qg + 1) * 512], rd=[qB], wr=[qb], q="pool")
        for kt in range(NKT):
            ps = C.psb[4 + sci[0] % 4]; sci[0] += 1
            ksl = slice(kt * 128, (kt + 1) * 128)
            P.op("pe", lambda e: e.matmul(ps[:, :512], kAs[:, ksl], qa[:], start=True, stop=False), rd=[kAs, qa], wr=[ps])
            P.op("pe", lambda e: e.matmul(ps[:, :512], kBs[:, ksl], qb[:], start=False, stop=True), rd=[kBs, qb], wr=[ps])
            pT = C.rot("mpT", [128, 512], 3)
            P.op("act", lambda e: e.activation(out=pT[:], in_=ps[:, :512], func=AF.Exp), rd=[ps], wr=[pT])
            for j in range(4):
                P.op("pe", lambda e: e.matmul(accs[j][:, :129], pT[:, j * 128:(j + 1) * 128], va[:, kt, :], start=(kt == 0), stop=(kt == NKT - 1)), rd=[pT, va], wr=[accs[j]])
        for j in range(4):
            t = C.rot("mden", [128, 1], 4); o = C.rot("mo", [128, 128], 4)
            P.op("dve", lambda e: e.reciprocal(out=t[:], in_=accs[j][:, 128:129]), rd=[accs[j]], wr=[t])
            P.op("dve", lambda e: e.tensor_scalar(out=o[:], in0=accs[j][:, 0:128], scalar1=t[:, 0:1], scalar2=None, op0=ALU.mult), rd=[accs[j], t], wr=[o])
            r0 = qg * 512 + j * 128
            P.dma(out_a[r0:r0 + 128, :], o[:], rd=[o], wr=[(out_a, r0)], q=C.q())


def build_D(parts=("gdn", "mla")):
    nc = bass.Bass("TRN2", target_bir_lowering=False)
    es = ExitStack()
    with es:
        P = Prog(nc, es)
        C = Ctx(P)
        I = lambda n, s: P.dram(n, s, kind="ExternalInput")
        if "gdn" in parts:
            with ExitStack() as es2:
                old, P.es = P.es, es2
                C.tmp = {}
                qT = I("gqT", [128, 8448]); kT = I("gkT", [128, 8448]); ktok = I("gktok", [64, NCH, 128]); vtok = I("gvtok", [64, NCH, 128]); ggtok = I("gggtok", [64, NCH, 128])
                gab = I("gab", [64, 4, NCH]); gconst = I("gconst", [64, 4]); cm = I("gcm", [64, 5, 64]); normg = I("gnormg", [64, 128])
                out_g = P.dram("out_g", [64, NCH, 128], kind="ExternalOutput")
                gdn_part(P, C, qT, kT, ktok, vtok, ggtok, gab, gconst, cm, normg, out_g)
                P.barrier()
                P.es = old
                C.tmp = {}
        if "mla" in parts:
            with ExitStack() as es2:
                old, P.es = P.es, es2
                qA = I("mqA", [128, 8192]); qB = I("mqB", [64, 8192]); kA = I("mkA", [128, 8448]); kB = I("mkB", [64, 8448]); vt = I("mvtok", [128, 66, 128])
                out_a = P.dram("out_a", [8192, 128], kind="ExternalOutput")
                mla_part(P, C, qA, qB, kA, kB, vt, out_a)
                P.barrier()
                P.es = old
                C.tmp = {}
        P.finish()
        print("phase D instructions", P.nins, P.cnt)
    return nc


def d_maps(inp, gdT, gdc, mlT, mlc):
    maps = []
    jj = np.arange(64)
    trif = (jj[:, None] <= jj[None, :]).astype(np.float32)
    sf = (jj[:, None] < jj[None, :]).astype(np.float32)
    cm = np.ascontiguousarray(np.stack([trif, trif.T, sf, sf.T, np.eye(64, dtype=np.float32)], 1))
    for k in range(8):
        b, h = k // 4, k % 4
        full = np.concatenate([gdc[b], gdT[b]], axis=1)
        rows = lambda base: full[base + h * 128: base + (h + 1) * 128]
        m = {"gqT": np.ascontiguousarray(rows(0)), "gkT": np.ascontiguousarray(rows(512)),
             "gktok": tok64(rows(512).T), "gvtok": tok64(rows(1024).T), "gggtok": tok64(rows(1536).T)}
        g4 = np.stack([full[2048 + h], full[2056 + h], full[2048 + 4 + h], full[2056 + 4 + h]], 0)
        m["gab"] = np.ascontiguousarray(g4.reshape(4, NCH, 64).transpose(2, 0, 1))
        al, db = inp["c_a_log"][0], inp["c_dt_bias"][0]
        m["gconst"] = np.ascontiguousarray(np.broadcast_to(np.array([al[0, h], db[0, h], al[1, h], db[1, h]], np.float32)[None], (64, 4)))
        m["gcm"] = cm
        m["gnormg"] = np.ascontiguousarray(np.broadcast_to(inp["c_norm_g"][0][None], (64, 128)))
        base = h * 512
        mfull = np.concatenate([mlc[b], mlT[b]], axis=1)
        m["mqA"] = np.ascontiguousarray(mlT[b][base:base + 128]); m["mqB"] = np.ascontiguousarray(mlT[b][base + 128:base + 192])
        m["mkA"] = np.ascontiguousarray(mfull[base + 192:base + 320]); m["mkB"] = np.ascontiguousarray(mfull[base + 320:base + 384])
        v = mfull[base + 384:base + 512].T
        m["mvtok"] = np.ascontiguousarray(v.reshape(66, 128, 128).transpose(1, 0, 2))
        maps.append(m)
    return maps


def build_B():
    nc = bass.Bass("TRN2", target_bir_lowering=False)
    es = ExitStack()
    with es:
        P = Prog(nc, es)
        C = Ctx(P)
        I = lambda n, s: P.dram(n, s, kind="ExternalInput")
        with ExitStack() as es2:
            old, P.es = P.es, es2
            qT = I("qT", [128, 8448]); kT = I("kT", [128, 8448]); ktok = I("ktok", [64, NCH, 128]); vtok = I("vtok", [64, NCH, 128]); motok = I("motok", [64, NCH, 128])
            g4 = I("g4", [64, 4, NCH]); gbias = I("gbias", [64, 4]); tri = I("tri", [64, 2, 64]); normg = I("normg", [64, 128])
            out_m = P.dram("out_m", [64, NCH, 128], kind="ExternalOutput")
            mlstm_part(P, C, qT, kT, ktok, vtok, motok, g4, gbias, tri, normg, out_m)
            P.barrier()
            P.es = old
            C.tmp = {}
        wts = {}
        for nm, shp in (("w1", [33, 64]), ("w2", [64, 64]), ("w3", [64, 64]), ("w4", [64, 4, 128]), ("fr", [64, 1]), ("fb", [64, 3]), ("nd", [128, 4]), ("dsk", [128, 2])):
            d = I("h_" + nm, shp); t = P.sb("s_" + nm, shp); P.dma(t[:], d[:], rd=[d], wr=[t]); wts[nm] = t
        P.op("dve", lambda e: e.tensor_scalar(out=wts["fb"][:], in0=wts["fb"][:], scalar1=wts["fr"][:, 0:1], scalar2=None, op0=ALU.mult), rd=[wts["fb"], wts["fr"]], wr=[wts["fb"]])
        K = load_consts2(P, DFT_SPECS)
        uin = I("ud", [64, 3, 128, 128]); featT = I("feat8192", [33, 8192]); tn = I("tn8192", [128, 8192])
        scratch = P.dram("hscr", [4, 128, 8192])
        outd = P.dram("yd", [64, 128, 128], kind="ExternalOutput")
        hyena_dft_part(P, C, uin, featT, tn, wts, K, scratch, outd)
        P.barrier()
        C.tmp = {}
        for L in (256,):
            with ExitStack() as es2:
                old, P.es = P.es, es2
                uin = I(f"u{L}", [128, 3, L]); featT = I(f"feat{L}", [33, L]); tn = I(f"tn{L}", [128, L])
                out = P.dram(f"y{L}", [128, L], kind="ExternalOutput")
                hyena_part(P, C, L, uin, featT, tn, wts, out, str(L))
                P.barrier()
                P.es = old
                C.tmp = {}
        P.finish()
    return nc


def kernel(**inp):
    inp = {k: np.ascontiguousarray(np.asarray(v, dtype=np.float32)) for k, v in inp.items()}
    cores = list(range(8))
    zT, zcT = run_A(inp)
    mb = mlstm_maps(inp, zT, zcT); hb = hyena_dft_maps(inp, zT, hyena_maps(inp, zT, zcT))
    res = run_bass_kernel_spmd(build_B(), [{**a, **b} for a, b in zip(mb, hb)], core_ids=cores)
    mix = np.zeros((2, 8192, 1024), np.float32); mixc = np.zeros((2, 256, 1024), np.float32)
    for k in cores:
        b, h = k // 4, k % 4
        r = res.results[k]
        o = r["out_m"].transpose(1, 0, 2).reshape(8448, 128)
        mixc[b][:, h * 128:(h + 1) * 128] = o[:256]; mix[b][:, h * 128:(h + 1) * 128] = o[256:]
        mix[b][:, 512 + h * 128:512 + (h + 1) * 128] = r["yd"].transpose(1, 0, 2).reshape(128, 8192).T
        mixc[b][:, 512 + h * 128:512 + (h + 1) * 128] = r["y256"].T
    res = run_bass_kernel_spmd(build_C(), c_maps(inp, 0, inp["x"], inp["ctx"], mix, mixc), core_ids=cores)
    x2, c2, gdT, gdc, mlT, mlc = gather_C(res.results)
    res = run_bass_kernel_spmd(build_D(), d_maps(inp, gdT, gdc, mlT, mlc), core_ids=cores)
    mix1 = np.zeros((2, 8192, 1024), np.float32)
    for k in cores:
        b, h = k // 4, k % 4
        r = res.results[k]
        mix1[b][:, h * 128:(h + 1) * 128] = r["out_g"].transpose(1, 0, 2).reshape(8448, 128)[256:]
        mix1[b][:, 512 + h * 128:512 + (h + 1) * 128] = r["out_a"]
    maps = c_maps(inp, 1, x2, c2, mix1, np.zeros_like(mixc), layer1=False)
    for m_ in maps:
        m_.pop("x2"); m_.pop("m2"); m_["masks"] = np.ascontiguousarray(m_["masks"][:, :2, :])
    res = run_bass_kernel_spmd(build_C(layer1=False, segs=SEG_C[:2]), maps, core_ids=cores)
    out = gather_C(res.results, layer1=False)[0]
    return np.ascontiguousarray(out.astype(np.float32))
```

```python
import numpy as np
from contextlib import ExitStack
import concourse.bass as bass
import concourse.mybir as mybir
from concourse.bass_utils import run_bass_kernel_spmd

F32 = mybir.dt.float32
BF16 = mybir.dt.bfloat16
ALU = mybir.AluOpType
AF = mybir.ActivationFunctionType
AX = mybir.AxisListType


class Dep:
    __slots__ = ("w", "r")

    def __init__(self):
        self.w = None
        self.r = {}


class TT:
    def __init__(self, t, name, psum=False):
        self.t = t
        self.name = name
        self.psum = psum
        self.whole = Dep()
        self.parts = {}

    def __getitem__(self, idx):
        return self.t[idx]


class Prog:
    COMPUTE = ("pe", "act", "dve", "pool")
    QUEUES = ("sp", "pool")
    NDS = 12

    def __init__(self, nc, es, same_sync=True):
        self.nc = nc
        self.es = es
        self.same_sync = same_sync
        self.e = {"pe": nc.tensor, "act": nc.scalar, "dve": nc.vector, "pool": nc.gpsimd, "sp": nc.sync}
        self.semh = {}
        for k in self.COMPUTE:
            self.semh[k] = es.enter_context(nc.semaphore("s_" + k))
        for q in self.QUEUES:
            for i in range(self.NDS):
                self.semh[("d", q, i)] = es.enter_context(nc.semaphore(f"d_{q}{i}"))
        self.cnt = {k: 0 for k in self.COMPUTE}
        self.dcnt = {q: 0 for q in self.QUEUES}
        self.dfinal = {}
        self.waited = {k: {} for k in self.e}
        self.nins = 0
        self.uid = 0

    def sb(self, name, shape, dtype=F32, es=None):
        self.uid += 1
        t = (es or self.es).enter_context(self.nc.sbuf_tensor(f"{name}_{self.uid}", list(shape), dtype))
        return TT(t, name)

    def ps(self, name, shape, dtype=F32, es=None):
        self.uid += 1
        t = (es or self.es).enter_context(self.nc.psum_tensor(f"{name}_{self.uid}", list(shape), dtype))
        return TT(t, name, psum=True)

    def dram(self, name, shape, dtype=F32, kind="Internal"):
        t = self.nc.dram_tensor(name, list(shape), dtype, kind=kind)
        return TT(t.ap(), name)

    def _deps(self, items):
        out = []
        for it in items:
            if isinstance(it, TT):
                out.append((it, None))
            else:
                out.append(it)
        return out

    def _recs(self, tt, key):
        if key is None:
            return [tt.whole] + list(tt.parts.values())
        if key not in tt.parts:
            tt.parts[key] = Dep()
        return [tt.whole, tt.parts[key]]

    def _need(self, eng, toks):
        best = {}
        for sk, v in toks:
            if sk == eng and (eng == "pe" or not self.same_sync):
                continue
            if v > best.get(sk, 0):
                best[sk] = v
        for sk, v in best.items():
            if self.waited[eng].get(sk, 0) >= v:
                continue
            self.e[eng].wait_ge(self.semh[sk], v)
            self.waited[eng][sk] = v
            self.nins += 1

    def op(self, eng, fn, rd=(), wr=(), dma=False):
        rd = self._deps(rd)
        wr = self._deps(wr)
        toks = []
        for tt, key in rd:
            for d in self._recs(tt, key):
                if d.w:
                    toks.append(d.w)
                if tt.psum:
                    toks.extend((sk, v) for sk, v in d.r.items() if sk != eng)
        for tt, key in wr:
            for d in self._recs(tt, key):
                if d.w:
                    toks.append(d.w)
                toks.extend(d.r.items())
        if dma:
            i = self.dcnt[eng]
            idx = i % self.NDS
            val = 16 * (i // self.NDS + 1)
            sk = ("d", eng, idx)
            if val > 16:
                toks.append((sk, val - 16))
            self._need(eng, toks)
            ins = fn(self.e[eng])
            ins.then_inc(self.semh[sk], 16)
            self.dcnt[eng] += 1
            self.dfinal[sk] = val
            tok = (sk, val)
        else:
            self._need(eng, toks)
            ins = fn(self.e[eng])
            ins.then_inc(self.semh[eng], 1)
            self.cnt[eng] += 1
            tok = (eng, self.cnt[eng])
        self.nins += 1
        for tt, key in rd:
            if key is None:
                d = tt.whole
            else:
                d = self._recs(tt, key)[1]
            if tok[1] > d.r.get(tok[0], 0):
                d.r[tok[0]] = tok[1]
        for tt, key in wr:
            if key is None:
                tt.whole.w = tok
                tt.whole.r = {}
                tt.parts = {}
            else:
                d = self._recs(tt, key)[1]
                d.w = tok
                d.r = {}
        return ins

    def dma(self, out, in_, rd=(), wr=(), q="sp", **kw):
        return self.op(q, lambda e: e.dma_start(out=out, in_=in_, **kw), rd=rd, wr=wr, dma=True)

    def barrier(self, engs=None):
        toks = [(k, v) for k, v in self.cnt.items() if v > 0] + list(self.dfinal.items())
        for eng in (engs or self.e.keys()):
            ss = self.same_sync
            self.same_sync = True
            self._need(eng, toks)
            self.same_sync = ss

    def finish(self):
        self.barrier(["sp"])


EPS = 1e-6
D = 1024

def blocks(T, mx=512):
    nb = (T + mx - 1) // mx
    base, rem = T // nb, T % nb
    out, s = [], 0
    for i in range(nb):
        w = base + (1 if i < rem else 0)
        out.append((s, w)); s += w
    return out


class Ctx:
    def __init__(self, P):
        self.P = P
        self.ones = P.sb("ones", [128, 128])
        P.op("dve", lambda e: e.memset(self.ones[:], 1.0), wr=[self.ones])
        self.psb = [P.ps(f"psb{i}", [128, 512]) for i in range(8)]
        self.psi = 0
        self.tmp = {}
        self.qi = 0

    pool = None

    def ps(self):
        if self.pool is not None:
            p = self.pool
            t = self.psb[p[0] + p[2] % p[1]]; p[2] += 1
            return t
        t = self.psb[self.psi % 8]; self.psi += 1
        return t

    def rot(self, name, shape, n=2, dtype=F32):
        if name not in self.tmp:
            self.tmp[name] = [[self.P.sb(f"{name}{i}", shape, dtype) for i in range(n)], 0]
        ent = self.tmp[name]
        t = ent[0][ent[1] % n]; ent[1] += 1
        return t

    def q(self):
        self.qi += 1
        return "sp" if self.qi % 2 == 0 else "pool"


def compute_mods(C, modw, modbT, cT):
    P = C.P
    sc = P.sb("sc", [128, 8, 2]); P.dma(sc[:], cT[:], rd=[cT], wr=[sc])
    P.op("act", lambda e: e.activation(out=sc[:], in_=sc[:], func=AF.Silu), rd=[sc], wr=[sc])
    mb = P.sb("mb", [128, 48]); P.dma(mb[:], modbT[:], rd=[modbT], wr=[mb])
    mods = P.sb("mods", [128, 48, 2])
    for m in range(48):
        wt = C.rot("win", [128, 8, 128], 3)
        P.dma(wt[:], modw[:, m * 128:(m + 1) * 128].rearrange("(k p) m -> p k m", p=128), rd=[modw], wr=[wt], q=C.q())
        ps = C.ps()
        for k in range(8):
            P.op("pe", lambda e: e.matmul(ps[:, 0:2], wt[:, k, :], sc[:, k, :], start=(k == 0), stop=(k == 7)), rd=[wt, sc], wr=[ps])
        P.op("dve", lambda e: e.tensor_scalar(out=mods[:, m, :], in0=ps[:, 0:2], scalar1=mb[:, m:m + 1], scalar2=None, op0=ALU.add), rd=[ps, mb], wr=[(mods, m)])
    for j in (1, 4):
        P.op("dve", lambda e: e.tensor_scalar(out=mods[:, j * 8:(j + 1) * 8, :], in0=mods[:, j * 8:(j + 1) * 8, :], scalar1=1.0, scalar2=None, op0=ALU.add), rd=[mods], wr=[mods])
    return mods


def norm_mod(C, xt, hT, T, mods, jsh, jsc, col):
    P = C.P
    for bi, (s, w) in enumerate(blocks(T)):
        ps = C.ps()
        for c in range(8):
            sq = C.rot("sq", [128, 512], 3)
            P.op("act", lambda e: e.activation(out=sq[:, :w], in_=xt[:, c, s:s + w], func=AF.Square), rd=[(xt, bi)], wr=[sq])
            P.op("pe", lambda e: e.matmul(ps[:, :w], C.ones[:], sq[:, :w], start=(c == 0), stop=(c == 7)), rd=[sq, C.ones], wr=[ps])
        rs = C.rot("rs", [128, 512], 2)
        P.op("dve", lambda e: e.tensor_scalar(out=rs[:, :w], in0=ps[:, :w], scalar1=1.0 / D, scalar2=EPS, op0=ALU.mult, op1=ALU.add), rd=[ps], wr=[rs])
        P.op("act", lambda e: e.activation(out=rs[:, :w], in_=rs[:, :w], func=AF.Sqrt), rd=[rs], wr=[rs])
        P.op("dve", lambda e: e.reciprocal(out=rs[:, :w], in_=rs[:, :w]), rd=[rs], wr=[rs])
        for c in range(8):
            P.op("dve", lambda e: e.tensor_tensor(out=hT[:, c, s:s + w], in0=xt[:, c, s:s + w], in1=rs[:, :w], op=ALU.mult), rd=[(xt, bi), rs], wr=[(hT, bi)])
            P.op("act", lambda e: e.activation(out=hT[:, c, s:s + w], in_=hT[:, c, s:s + w], func=AF.Identity,
                                               scale=mods[:, jsc * 8 + c, col:col + 1], bias=mods[:, jsh * 8 + c, col:col + 1]), rd=[(hT, bi), mods], wr=[(hT, bi)])


def linear_T(C, wdram, col0, mw, hT, T, consume, kc=8, wname="win"):
    P = C.P
    wt = C.rot(wname, [128, kc, 128], 3)
    P.dma(wt[:, :, :mw], wdram[:, col0:col0 + mw].rearrange("(k p) m -> p k m", p=128), rd=[wdram], wr=[wt], q=C.q())
    for bi, (s, w) in enumerate(blocks(T)):
        ps = C.ps()
        for k in range(kc):
            P.op("pe", lambda e: e.matmul(ps[:mw, :w], wt[:, k, :mw], hT[:, k, s:s + w], start=(k == 0), stop=(k == kc - 1)), rd=[wt, (hT, bi)], wr=[ps])
        consume(ps, bi, s, w)


def conv3(C, z, u, T, H, wcol, bcol, mask, mw=128):
    P = C.P
    n = T - 2 * H
    P.op("dve", lambda e: e.tensor_scalar(out=z[:mw, 0:H], in0=z[:mw, 0:H], scalar1=mask[:mw, 0:1], scalar2=None, op0=ALU.mult), rd=[z], wr=[z])
    P.op("dve", lambda e: e.tensor_scalar(out=z[:mw, T - H:T], in0=z[:mw, T - H:T], scalar1=mask[:mw, 1:2], scalar2=None, op0=ALU.mult), rd=[z], wr=[z])
    P.op("dve", lambda e: e.tensor_scalar(out=u[:mw, 0:n], in0=z[:mw, H - 1:H - 1 + n], scalar1=wcol[:mw, 0:1], scalar2=bcol[:mw, 0:1], op0=ALU.mult, op1=ALU.add), rd=[z], wr=[u])
    P.op("dve", lambda e: e.scalar_tensor_tensor(out=u[:mw, 0:n], in0=z[:mw, H:H + n], scalar=wcol[:mw, 1:2], in1=u[:mw, 0:n], op0=ALU.mult, op1=ALU.add), rd=[z, u], wr=[u])
    P.op("dve", lambda e: e.scalar_tensor_tensor(out=u[:mw, 0:n], in0=z[:mw, H + 1:H + 1 + n], scalar=wcol[:mw, 2:3], in1=u[:mw, 0:n], op0=ALU.mult, op1=ALU.add), rd=[z, u], wr=[u])


A_CHUNKS = [(i * 128, 128, None) for i in range(16)] + [(2048, 16, None)] + [(2064 + i * 128, 128, i) for i in range(12)]
SEG_A = [(1024, 0), (1024, 0), (64, 1)]

def build_A():
    H = 1
    nc = bass.Bass("TRN2", target_bir_lowering=False)
    es = ExitStack()
    with es:
        P = Prog(nc, es)
        C = Ctx(P)
        xin = [P.dram(f"x{i}", [128, 8, n + 2 * H], kind="ExternalInput") for i, (n, _) in enumerate(SEG_A)]
        masks = P.dram("masks", [128, 3, 2], kind="ExternalInput")
        cT = P.dram("cT", [128, 8, 2], kind="ExternalInput")
        modw = P.dram("modw", [1024, 6144], kind="ExternalInput")
        modbT = P.dram("modbT", [128, 48], kind="ExternalInput")
        win = P.dram("win", [1024, 3600], kind="ExternalInput")
        cw = P.dram("cw", [128, 12, 3], kind="ExternalInput")
        cb = P.dram("cb", [128, 12], kind="ExternalInput")
        outs = [P.dram(f"o{i}", [3600, n], kind="ExternalOutput") for i, (n, _) in enumerate(SEG_A)]
        mk = P.sb("mk", [128, 3, 2]); P.dma(mk[:], masks[:], rd=[masks], wr=[mk])
        cwt = P.sb("cwt", [128, 12, 3]); P.dma(cwt[:], cw[:], rd=[cw], wr=[cwt])
        cbt = P.sb("cbt", [128, 12]); P.dma(cbt[:], cb[:], rd=[cb], wr=[cbt])
        mods = compute_mods(C, modw, modbT, cT)
        modw1 = P.dram("modw1", [1024, 6144], kind="ExternalInput"); modbT1 = P.dram("modbT1", [128, 48], kind="ExternalInput")
        mods1 = compute_mods1(C, modw1, modbT1, cT)
        mo0 = P.dram("mods0o", [128, 48, 2], kind="ExternalOutput"); mo1 = P.dram("mods1o", [128, 48, 2], kind="ExternalOutput")
        P.dma(mo0[:], mods[:], rd=[mods], wr=[mo0]); P.dma(mo1[:], mods1[:], rd=[mods1], wr=[mo1])
        TM = 1024 + 2 * H
        xt = P.sb("xt", [128, 8, TM]); hT = P.sb("hT", [128, 8, TM])
        for si, (n, col) in enumerate(SEG_A):
            T = n + 2 * H
            P.dma(xt[:, :, :T], xin[si][:], rd=[xin[si]], wr=[xt])
            norm_mod(C, xt, hT, T, mods, 0, 1, col)
            for (col0, mw, hy) in A_CHUNKS:
                z = C.rot("zrow", [128, TM], 3)
                def consume(ps, bi, s, w, z=z, mw=mw):
                    P.op("act", lambda e: e.copy(out=z[:mw, s:s + w], in_=ps[:mw, :w]), rd=[ps], wr=[(z, bi)])
                linear_T(C, win, col0, mw, hT, T, consume)
                if hy is None:
                    P.dma(outs[si][col0:col0 + mw, :], z[:mw, H:H + n], rd=[z], wr=[(outs[si], col0)], q=C.q())
                else:
                    u = C.rot("urow", [128, 1024], 2)
                    conv3(C, z, u, T, H, cwt[:, hy, :], cbt[:, hy:hy + 1], mk[:, si, :])
                    P.dma(outs[si][col0:col0 + mw, :], u[:mw, 0:n], rd=[u], wr=[(outs[si], col0)], q=C.q())
        P.finish()
        print("phase A instructions", P.nins)
    return nc


def seg_slice_T(xb, start, n, H):
    L = xb.shape[0]
    out = np.zeros((n + 2 * H, xb.shape[1]), np.float32)
    lo, hi = start - H, start + n + H
    a, b = max(lo, 0), min(hi, L)
    out[a - lo:b - lo] = xb[a:b]
    return np.ascontiguousarray(out.T.reshape(8, 128, n + 2 * H).transpose(1, 0, 2)), float(lo >= 0), float(hi <= L)


def colT(v):
    return np.ascontiguousarray(v.reshape(-1, 128).T)


MODS_CACHE = [None, None]


def run_A(inp):
    H = 1
    nc = build_A()
    maps = []
    for k in range(8):
        b, qtr = k // 4, k % 4
        m = {}
        mk = np.zeros((128, 3, 2), np.float32)
        for si in range(2):
            xs, l, r = seg_slice_T(inp["x"][b], qtr * 2048 + si * 1024, 1024, H)
            m[f"x{si}"] = xs; mk[:, si, 0] = l; mk[:, si, 1] = r
        xs, l, r = seg_slice_T(inp["ctx"][b], qtr * 64, 64, H)
        m["x2"] = xs; mk[:, 2, 0] = l; mk[:, 2, 1] = r
        m["masks"] = mk
        m["cT"] = np.ascontiguousarray(np.stack([colT(inp["c"][b]), colT(inp["c_ctx"])], -1))
        m["modw"] = inp["mod_w"][0]
        m["modbT"] = colT(inp["mod_b"][0])
        m["modw1"] = inp["mod_w"][1]
        m["modbT1"] = colT(inp["mod_b"][1])
        m["win"] = inp["a_w_in"][0]
        m["cw"] = np.ascontiguousarray(inp["b_conv_w"][0].T.reshape(12, 128, 3).transpose(1, 0, 2))
        m["cb"] = colT(inp["b_conv_b"][0])
        maps.append(m)
    res = run_bass_kernel_spmd(nc, maps, core_ids=list(range(8)))
    zT = np.zeros((2, 3600, 8192), np.float32); zcT = np.zeros((2, 3600, 256), np.float32)
    for k in range(8):
        b, qtr = k // 4, k % 4
        r = res.results[k]
        for si in range(2):
            zT[b][:, qtr * 2048 + si * 1024: qtr * 2048 + (si + 1) * 1024] = r[f"o{si}"]
        zcT[b][:, qtr * 64:(qtr + 1) * 64] = r["o2"]
    MODS_CACHE[0] = [np.ascontiguousarray(res.results[k]["mods0o"]) for k in range(8)]
    MODS_CACHE[1] = [np.ascontiguousarray(res.results[k]["mods1o"]) for k in range(8)]
    return zT, zcT


NCH = 132
SEGS = [(0, 4)] + [(4 + 16 * i, 16) for i in range(8)]


def run_interleaved(gens, C=None, pools=None):
    gens = list(gens)
    idx = {id(g): i for i, g in enumerate(gens)}
    pl = [[b, n, 0] for (b, n) in pools] if pools else None
    while gens:
        for g in list(gens):
            if pl:
                C.pool = pl[idx[id(g)]]
            try:
                next(g)
            except StopIteration:
                gens.remove(g)
    if pl:
        C.pool = None

MODE = 0
def mlstm_part(P, C, qT, kT, ktok, vtok, motok, g4, gbias, tri, normg, out_m):
    DH = 128
    tr = P.sb("tri", [64, 2, 64]); P.dma(tr[:], tri[:], rd=[tri], wr=[tr])
    gb = P.sb("gb", [64, 4]); P.dma(gb[:], gbias[:], rd=[gbias], wr=[gb])
    ng = P.sb("ng", [64, 128]); P.dma(ng[:], normg[:], rd=[normg], wr=[ng])
    g = P.sb("g", [64, 4, NCH]); P.dma(g[:], g4[:], rd=[g4], wr=[g])
    hsum = P.sb("hsum", [64, NCH, 128])
    S = P.sb("S", [128, 129])
    Et = P.sb("Et", [64, 2, NCH]); wt = P.sb("wt", [64, 2, NCH]); w2t = P.sb("w2t", [64, 2, NCH]); ebl = P.sb("ebl", [128, 2, NCH])
    STEP = 10 ** 9
    cntr = [0]
    def po(*a, **k):
        cntr[0] += 1
        if cntr[0] <= STEP:
            P.op(*a, **k)
    for d in range(2 if MODE != 4 else 0):
        li, lf = g[:, 2 * d, :], g[:, 2 * d + 1, :]
        po("dve", lambda e: e.tensor_scalar(out=li, in0=li, scalar1=gb[:, 2 * d:2 * d + 1], scalar2=None, op0=ALU.add), rd=[g, gb], wr=[g])
        po("act", lambda e: e.activation(out=lf, in_=lf, func=AF.Sigmoid, bias=gb[:, 2 * d + 1:2 * d + 2], scale=1.0), rd=[g, gb], wr=[g])
        po("act", lambda e: e.activation(out=lf, in_=lf, func=AF.Ln), rd=[g], wr=[g])
        ps = C.ps()
        po("pe", lambda e: e.matmul(ps[:64, :NCH], tr[:, d, :], lf, start=True, stop=True), rd=[tr, g], wr=[ps])
        ps2 = C.ps()
        po("pe", lambda e: e.matmul(ps2[:, :NCH], C.ones[:64, :], lf, start=True, stop=True), rd=[C.ones, g], wr=[ps2])
        po("act", lambda e: e.activation(out=Et[:, d, :], in_=ps[:64, :NCH], func=AF.Exp), rd=[ps], wr=[Et])
        po("act", lambda e: e.activation(out=ebl[:, d, :], in_=ps2[:, :NCH], func=AF.Exp), rd=[ps2], wr=[ebl])
        tmp = C.rot("gtmp", [64, NCH], 2)
        po("dve", lambda e: e.tensor_tensor(out=tmp[:], in0=ps[:64, :NCH], in1=li, op=ALU.subtract), rd=[g, ps], wr=[tmp])
        po("act", lambda e: e.activation(out=wt[:, d, :], in_=tmp[:], func=AF.Exp, scale=-1.0), rd=[tmp], wr=[wt])
        po("dve", lambda e: e.tensor_scalar(out=wt[:, d, :], in0=wt[:, d, :], scalar1=float(DH ** -0.5), scalar2=None, op0=ALU.mult), rd=[wt], wr=[wt])
        tmp2 = C.rot("gtmp", [64, NCH], 2)
        po("dve", lambda e: e.tensor_tensor(out=tmp2[:], in0=ps2[:64, :NCH], in1=tmp[:], op=ALU.subtract), rd=[tmp, ps2], wr=[tmp2])
        po("act", lambda e: e.activation(out=w2t[:, d, :], in_=tmp2[:], func=AF.Exp), rd=[tmp2], wr=[w2t])
        po("dve", lambda e: e.tensor_scalar(out=w2t[:, d, :], in0=w2t[:, d, :], scalar1=float(DH ** -0.5), scalar2=None, op0=ALU.mult), rd=[w2t], wr=[w2t])
    P.op("dve", lambda e: e.memset(hsum[:], 0.0), wr=[hsum])
    Ss = [S, P.sb("S_b", [128, 129])]
    def scan_dir(d):
        S = Ss[d]
        P.op("dve", lambda e: e.memset(S[:], 0.0), wr=[S])
        yield
        segs = SEGS if d == 0 else [SEGS[0]] + SEGS[:0:-1]
        for (c0, ncs) in segs:
            T = ncs * 64
            qs = C.rot("qs%d" % d, [128, 1024], 2); ks = C.rot("ks%d" % d, [128, 1024], 2)
            kk = C.rot("kk%d" % d, [64, 16, 128], 2); vv = C.rot("vv%d" % d, [64, 16, 129], 2)
            P.dma(qs[:, :T], qT[:, c0 * 64:c0 * 64 + T], rd=[qT], wr=[qs], q="sp")
            yield
            P.dma(ks[:, :T], kT[:, c0 * 64:c0 * 64 + T], rd=[kT], wr=[ks], q="pool")
            yield
            P.dma(kk[:, :ncs, :], ktok[:, c0:c0 + ncs, :], rd=[ktok], wr=[kk], q="sp")
            yield
            P.op("dve", lambda e: e.memset(vv[:, :, 128:129], 1.0), wr=[vv])
            yield
            P.dma(vv[:, :ncs, 0:128], vtok[:, c0:c0 + ncs, :], rd=[vtok], wr=[vv], q="pool")
            yield
            order = range(ncs) if d == 0 else range(ncs - 1, -1, -1)
            for cl in order:
                c = c0 + cl
                tk = slice(cl * 64, cl * 64 + 64)
                vh = C.rot("vh%d" % d, [64, 129], 3); vh2 = C.rot("vh2%d" % d, [64, 129], 3)
                P.op("dve", lambda e: e.tensor_scalar(out=vh[:], in0=vv[:, cl, :], scalar1=wt[:, d, c:c + 1], scalar2=None, op0=ALU.mult), rd=[vv, wt], wr=[vh])
                yield
                P.op("act", lambda e: e.activation(out=vh2[:], in_=vv[:, cl, :], func=AF.Identity, scale=w2t[:, d, c:c + 1]), rd=[vv, w2t], wr=[vh2])
                yield
                ps = C.ps()
                P.op("pe", lambda e: e.matmul(ps[:64, :64], ks[:, tk], qs[:, tk], start=True, stop=True), rd=[ks, qs], wr=[ps])
                yield
                sm = C.rot("sm%d" % d, [64, 64], 3)
                P.op("dve", lambda e: e.tensor_tensor(out=sm[:], in0=ps[:64, :64], in1=tr[:, d, :], op=ALU.mult), rd=[ps, tr], wr=[sm])
                yield
                pn = C.ps()
                P.op("pe", lambda e: e.matmul(pn[:64, :129], sm[:], vh[:], start=True, stop=False), rd=[sm, vh], wr=[pn])
                yield
                P.op("pe", lambda e: e.matmul(pn[:64, :129], qs[:, tk], S[:], start=False, stop=True), rd=[qs, S], wr=[pn])
                yield
                pu = C.ps()
                P.op("pe", lambda e: e.matmul(pu[:, :129], kk[:, cl, :], vh2[:], start=True, stop=True), rd=[kk, vh2], wr=[pu])
                yield
                P.op("dve", lambda e: e.tensor_scalar(out=S[:], in0=S[:], scalar1=ebl[:, d, c:c + 1], scalar2=None, op0=ALU.mult), rd=[S, ebl], wr=[S])
                yield
                P.op("dve", lambda e: e.tensor_tensor(out=S[:], in0=pu[:, :129], in1=S[:], op=ALU.add), rd=[S, pu], wr=[S])
                yield
                t = C.rot("den%d" % d, [64, 4], 3)
                P.op("dve", lambda e: e.tensor_tensor(out=t[:, 0:1], in0=pn[:64, 128:129], in1=Et[:, d, c:c + 1], op=ALU.mult), rd=[pn, Et], wr=[t])
                yield
                P.op("act", lambda e: e.activation(out=t[:, 1:2], in_=t[:, 0:1], func=AF.Abs), rd=[t], wr=[t])
                yield
                P.op("dve", lambda e: e.tensor_scalar(out=t[:, 1:2], in0=t[:, 1:2], scalar1=1.0, scalar2=None, op0=ALU.max), rd=[t], wr=[t])
                yield
                P.op("dve", lambda e: e.reciprocal(out=t[:, 2:3], in_=t[:, 1:2]), rd=[t], wr=[t])
                yield
                P.op("dve", lambda e: e.tensor_tensor(out=t[:, 3:4], in0=t[:, 2:3], in1=Et[:, d, c:c + 1], op=ALU.mult), rd=[t, Et], wr=[t])
                yield
                P.op("dve", lambda e: e.scalar_tensor_tensor(out=hsum[:, c, :], in0=pn[:64, 0:128], scalar=t[:, 3:4], in1=hsum[:, c, :], op0=ALU.mult, op1=ALU.add), rd=[pn, t, (hsum, c)], wr=[(hsum, c)])
                yield
    run_interleaved([scan_dir(0), scan_dir(1)])
    for (c0, ncs) in (SEGS if MODE in (0, 3) else []):
        mo = C.rot("kk0", [64, 16, 128], 2)
        P.dma(mo[:, :ncs, :], motok[:, c0:c0 + ncs, :], rd=[motok], wr=[mo], q="sp")
        P.op("act", lambda e: e.activation(out=mo[:, :ncs, :], in_=mo[:, :ncs, :], func=AF.Sigmoid), rd=[mo], wr=[mo])
        for cl in range(ncs):
            c = c0 + cl
            sq = C.rot("hsq", [64, 128], 2); t = C.rot("den0", [64, 4], 3)
            P.op("act", lambda e: e.activation(out=sq[:], in_=hsum[:, c, :], func=AF.Square, accum_out=t[:, 0:1]), rd=[(hsum, c)], wr=[sq, t])
            P.op("dve", lambda e: e.tensor_scalar(out=t[:, 1:2], in0=t[:, 0:1], scalar1=1.0 / 128, scalar2=EPS, op0=ALU.mult, op1=ALU.add), rd=[t], wr=[t])
            P.op("act", lambda e: e.activation(out=t[:, 1:2], in_=t[:, 1:2], func=AF.Sqrt), rd=[t], wr=[t])
            P.op("dve", lambda e: e.reciprocal(out=t[:, 2:3], in_=t[:, 1:2]), rd=[t], wr=[t])
            P.op("dve", lambda e: e.scalar_tensor_tensor(out=hsum[:, c, :], in0=hsum[:, c, :], scalar=t[:, 2:3], in1=ng[:], op0=ALU.mult, op1=ALU.mult), rd=[(hsum, c), t, ng], wr=[(hsum, c)])
            P.op("dve", lambda e: e.tensor_tensor(out=hsum[:, c, :], in0=hsum[:, c, :], in1=mo[:, cl, :], op=ALU.mult), rd=[(hsum, c), mo], wr=[(hsum, c)])
    for (c0, ncs) in SEGS:
        P.dma(out_m[:, c0:c0 + ncs, :], hsum[:, c0:c0 + ncs, :], rd=[hsum], wr=[(out_m, c0)], q=C.q())


def build_B1():
    nc = bass.Bass("TRN2", target_bir_lowering=False)
    es = ExitStack()
    with es:
        P = Prog(nc, es)
        C = Ctx(P)
        I = lambda n, s: P.dram(n, s, kind="ExternalInput")
        qT = I("qT", [128, 8448]); kT = I("kT", [128, 8448]); ktok = I("ktok", [64, NCH, 128]); vtok = I("vtok", [64, NCH, 128]); motok = I("motok", [64, NCH, 128])
        g4 = I("g4", [64, 4, NCH]); gbias = I("gbias", [64, 4]); tri = I("tri", [64, 2, 64]); normg = I("normg", [64, 128])
        out_m = P.dram("out_m", [64, NCH, 128], kind="ExternalOutput")
        mlstm_part(P, C, qT, kT, ktok, vtok, motok, g4, gbias, tri, normg, out_m)
        P.finish()
        print("phase B1 instructions", P.nins)
    return nc


def tok64(a):
    T, d = a.shape
    return np.ascontiguousarray(a.reshape(T // 64, 64, d).transpose(1, 0, 2))


def mlstm_maps(inp, zT, zcT):
    maps = []
    jj = np.arange(64)
    trif = (jj[:, None] <= jj[None, :]).astype(np.float32)
    tri = np.ascontiguousarray(np.stack([trif, trif.T], 1))
    for k in range(8):
        b, h = k // 4, k % 4
        full = np.concatenate([zcT[b], zT[b]], axis=1)
        rows = lambda base: full[base + h * 128: base + (h + 1) * 128]
        m = {"qT": np.ascontiguousarray(rows(0)), "kT": np.ascontiguousarray(rows(512)),
             "ktok": tok64(rows(512).T), "vtok": tok64(rows(1024).T), "motok": tok64(rows(1536).T)}
        gi = [full[2048 + d * 4 + h] for d in range(2)]; gf = [full[2056 + d * 4 + h] for d in range(2)]
        g4 = np.stack([gi[0], gf[0], gi[1], gf[1]], 0)
        m["g4"] = np.ascontiguousarray(g4.reshape(4, NCH, 64).transpose(2, 0, 1))
        ib, fb = inp["a_i_bias"][0], inp["a_f_bias"][0]
        m["gbias"] = np.ascontiguousarray(np.broadcast_to(np.array([ib[h], fb[h], ib[4 + h], fb[4 + h]], np.float32)[None], (64, 4)))
        m["tri"] = tri
        m["normg"] = np.ascontiguousarray(np.broadcast_to(inp["a_norm_g"][0][h * 128:(h + 1) * 128][None], (64, 128)))
        maps.append(m)
    return maps


import math
I32 = mybir.dt.int32
TWO_PI = float(2 * np.pi)


def hyena_feat(L):
    t = np.arange(L, dtype=np.float32)
    t_norm = (t / np.float32(L - 1)).astype(np.float32)
    w = (np.float32(2.0 * math.pi) * t / np.float32(L)).astype(np.float32)
    bands = np.linspace(1e-4, 15, 16, dtype=np.float32)
    ang = (w[:, None] * bands).astype(np.float32)
    feat = np.concatenate([t_norm[:, None], np.cos(ang), -np.sin(ang)], axis=-1).astype(np.float32)
    return np.ascontiguousarray(feat.T), np.ascontiguousarray(np.broadcast_to(t_norm[None], (128, L)))


def hyena_part(P, C, L, uin, featT, tn, wts, out, tag):
    HA = P.sb("HA" + tag, [128, L]); HB = P.sb("HB" + tag, [128, L]); U = P.sb("U" + tag, [128, L]); Y = P.sb("Y" + tag, [128, L]); X = P.sb("X" + tag, [128, L])
    w1, w2, w3, w4, fr, fb, nd, dsk = (wts[k] for k in ("w1", "w2", "w3", "w4", "fr", "fb", "nd", "dsk"))
    blks = blocks(L)
    nb = len(blks)
    P.dma(U[:], uin[:, 0, :], rd=[uin], wr=[U])
    for o in range(2):
        l1p = P.sb(f"l1p{tag}{o}", [128, 2, nb + 1])
        P.op("dve", lambda e: e.memset(l1p[:], 0.0), wr=[l1p])
        P.dma(X[:], uin[:, 1 + o, :], rd=[uin], wr=[X], q="pool")
        for bi, (s, w) in enumerate(blks):
            ft = C.rot("ft", [33, 512], 2); tnb = C.rot("tnb", [128, 512], 2)
            P.dma(ft[:, :w], featT[:, s:s + w], rd=[featT], wr=[ft], q="sp")
            P.dma(tnb[:, :w], tn[:, s:s + w], rd=[tn], wr=[tnb], q="pool")
            hin, kdim = ft, 33
            for li, wl in enumerate((w1, w2, w3)):
                ps = C.ps()
                P.op("pe", lambda e: e.matmul(ps[:64, :w], wl[:kdim, :], hin[:kdim, :w], start=True, stop=True), rd=[wl, hin], wr=[ps])
                pre = C.rot("pre", [64, 512], 2); kf = C.rot("kf", [64, 512], 2); ki = C.rot("ki", [64, 512], 2, I32)
                P.op("dve", lambda e: e.tensor_scalar(out=pre[:, :w], in0=ps[:64, :w], scalar1=fr[:, 0:1], scalar2=fb[:, li:li + 1], op0=ALU.mult, op1=ALU.add), rd=[ps, fr, fb], wr=[pre])
                P.op("dve", lambda e: e.tensor_scalar(out=kf[:, :w], in0=pre[:, :w], scalar1=1.0 / TWO_PI, scalar2=None, op0=ALU.mult), rd=[pre], wr=[kf])
                P.op("dve", lambda e: e.tensor_copy(out=ki[:, :w], in_=kf[:, :w]), rd=[kf], wr=[ki])
                P.op("dve", lambda e: e.tensor_copy(out=kf[:, :w], in_=ki[:, :w]), rd=[ki], wr=[kf])
                P.op("dve", lambda e: e.scalar_tensor_tensor(out=pre[:, :w], in0=kf[:, :w], scalar=-TWO_PI, in1=pre[:, :w], op0=ALU.mult, op1=ALU.add), rd=[kf, pre], wr=[pre])
                P.op("dve", lambda e: e.tensor_scalar(out=pre[:, :w], in0=pre[:, :w], scalar1=3.14159, scalar2=-3.14159, op0=ALU.min, op1=ALU.max), rd=[pre], wr=[pre])
                hcur = C.rot(f"hl{li}", [64, 512], 2)
                P.op("act", lambda e: e.activation(out=hcur[:, :w], in_=pre[:, :w], func=AF.Sin), rd=[pre], wr=[hcur])
                hin, kdim = hcur, 64
            for di, Hd in enumerate((HA, HB)):
                gi = o * 2 + di
                ps = C.ps()
                P.op("pe", lambda e: e.matmul(ps[:, :w], w4[:, gi, :], hin[:, :w], start=True, stop=True), rd=[w4, hin], wr=[ps])
                dec = C.rot("dec", [128, 512], 2)
                P.op("act", lambda e: e.activation(out=dec[:, :w], in_=tnb[:, :w], func=AF.Exp, scale=nd[:, gi:gi + 1]), rd=[tnb, nd], wr=[dec])
                P.op("dve", lambda e: e.tensor_tensor(out=Hd[:, s:s + w], in0=ps[:, :w], in1=dec[:, :w], op=ALU.mult), rd=[ps, dec], wr=[(Hd, bi)])
                ab = C.rot("ab", [128, 512], 2)
                P.op("act", lambda e: e.activation(out=ab[:, :w], in_=Hd[:, s:s + w], func=AF.Abs, accum_out=l1p[:, di, bi:bi + 1]), rd=[(Hd, bi)], wr=[ab, (l1p, (di, bi))])
        t4 = P.sb(f"t4{tag}{o}", [128, 6])
        P.op("dve", lambda e: e.tensor_reduce(out=t4[:, 0:1], in_=l1p[:], axis=AX.XY, op=ALU.add), rd=[l1p], wr=[t4])
        P.op("act", lambda e: e.activation(out=t4[:, 1:2], in_=HB[:, 0:1], func=AF.Abs), rd=[HB], wr=[t4])
        P.op("dve", lambda e: e.tensor_tensor(out=t4[:, 2:3], in0=t4[:, 0:1], in1=t4[:, 1:2], op=ALU.subtract), rd=[t4], wr=[t4])
        P.op("dve", lambda e: e.reciprocal(out=t4[:, 3:4], in_=t4[:, 2:3]), rd=[t4], wr=[t4])
        P.op("dve", lambda e: e.tensor_scalar(out=HA[:], in0=HA[:], scalar1=t4[:, 3:4], scalar2=None, op0=ALU.mult), rd=[HA, t4], wr=[HA])
        P.op("dve", lambda e: e.tensor_scalar(out=HB[:], in0=HB[:], scalar1=t4[:, 3:4], scalar2=None, op0=ALU.mult), rd=[HB, t4], wr=[HB])
        P.op("dve", lambda e: e.tensor_tensor(out=HA[:, 0:1], in0=HA[:, 0:1], in1=dsk[:, o:o + 1], op=ALU.add), rd=[HA, dsk], wr=[HA])
        P.op("dve", lambda e: e.tensor_scalar(out=Y[:], in0=U[:], scalar1=HA[:, 0:1], scalar2=None, op0=ALU.mult), rd=[U, HA], wr=[Y])
        for off in range(1, L):
            n = L - off
            P.op("dve", lambda e: e.scalar_tensor_tensor(out=Y[:, off:L], in0=U[:, 0:n], scalar=HA[:, off:off + 1], in1=Y[:, off:L], op0=ALU.mult, op1=ALU.add), rd=[U, HA, Y], wr=[Y])
            P.op("dve", lambda e: e.scalar_tensor_tensor(out=Y[:, 0:n], in0=U[:, off:L], scalar=HB[:, off:off + 1], in1=Y[:, 0:n], op0=ALU.mult, op1=ALU.add), rd=[U, HB, Y], wr=[Y])
        P.op("dve", lambda e: e.tensor_tensor(out=U[:], in0=Y[:], in1=X[:], op=ALU.mult), rd=[Y, X], wr=[U])
    P.dma(out[:], U[:], rd=[U], wr=[out])


def build_H(Ls=(8192, 256)):
    nc = bass.Bass("TRN2", target_bir_lowering=False)
    es = ExitStack()
    with es:
        P = Prog(nc, es)
        C = Ctx(P)
        I = lambda n, s: P.dram(n, s, kind="ExternalInput")
        wts = {}
        for nm, shp in (("w1", [33, 64]), ("w2", [64, 64]), ("w3", [64, 64]), ("w4", [64, 4, 128]), ("fr", [64, 1]), ("fb", [64, 3]), ("nd", [128, 4]), ("dsk", [128, 2])):
            d = I("h_" + nm, shp); t = P.sb("s_" + nm, shp); P.dma(t[:], d[:], rd=[d], wr=[t]); wts[nm] = t
        P.op("dve", lambda e: e.tensor_scalar(out=wts["fb"][:], in0=wts["fb"][:], scalar1=wts["fr"][:, 0:1], scalar2=None, op0=ALU.mult), rd=[wts["fb"], wts["fr"]], wr=[wts["fb"]])
        for L in Ls:
            with ExitStack() as es2:
                P.es, old = es2, P.es
                uin = I(f"u{L}", [128, 3, L]); featT = I(f"feat{L}", [33, L]); tn = I(f"tn{L}", [128, L])
                out = P.dram(f"y{L}", [128, L], kind="ExternalOutput")
                hyena_part(P, C, L, uin, featT, tn, wts, out, str(L))
                P.barrier()
                P.es = old
        P.finish()
        print("phase H instructions", P.nins, P.cnt)
    return nc


def hyena_maps(inp, zT, zcT, Ls=(8192, 256)):
    maps = []
    deltas = np.abs(np.linspace(math.log(1e-2) / 1.5, math.log(1e-2) / 0.3, 2048, dtype=np.float32))
    for k in range(8):
        b, h = k // 4, k % 4
        m = {}
        for L, src in ((8192, zT[b]), (256, zcT[b])):
            if L not in Ls:
                continue
            m[f"u{L}"] = np.ascontiguousarray(np.stack([src[2064 + j * 512 + h * 128: 2064 + j * 512 + (h + 1) * 128] for j in range(3)], 1))
            ft, tn = hyena_feat(L)
            m[f"feat{L}"] = ft; m[f"tn{L}"] = tn
        m["h_w1"] = inp["b_f_w1"][0]; m["h_w2"] = inp["b_f_w2"][0]; m["h_w3"] = inp["b_f_w3"][0]
        w4 = inp["b_f_w4"][0]
        m["h_w4"] = np.ascontiguousarray(np.stack([w4[:, g * 512 + h * 128: g * 512 + (h + 1) * 128] for g in range(4)], 1))
        m["h_fr"] = np.ascontiguousarray(inp["b_f_freq"][0][:, None])
        m["h_fb"] = np.ascontiguousarray(np.stack([inp["b_f_b1"][0], inp["b_f_b2"][0], inp["b_f_b3"][0]], 1))
        m["h_nd"] = np.ascontiguousarray(np.stack([-deltas[g * 512 + h * 128: g * 512 + (h + 1) * 128] for g in range(4)], 1))
        m["h_dsk"] = np.ascontiguousarray(inp["b_d"][0][:, h * 128:(h + 1) * 128].T)
        maps.append(m)
    return maps


NF = 16384


def load_consts2(P, specs):
    out = {}
    for nm, shp in specs:
        d = P.dram(nm, shp, kind="ExternalInput"); t = P.sb("c_" + nm, shp)
        P.dma(t[:], d[:], rd=[d], wr=[t]); out[nm] = t
    return out
GCH = 4


def dft_tables():
    a = np.arange(128, dtype=np.float64)
    th128 = 2 * np.pi * np.outer(a, a) / 128.0
    thN = 2 * np.pi * np.outer(a, a) / NF
    C2, S2 = np.cos(th128), np.sin(th128)
    F1 = np.concatenate([C2[:64], -S2[:64]], 1)
    Tc, Ts = np.cos(thN), -np.sin(thN)
    TT1 = np.stack([np.concatenate([Tc, Tc], 1)] * 2, 1)
    TT2 = np.stack([np.concatenate([Ts, Ts], 1)] * 2, 1)
    IC = np.concatenate([C2, S2], 1); IS = np.concatenate([-S2, C2], 1)
    Cf = C2[:, :64] / NF; nSf = -S2[:, :64] / NF
    f = lambda x: np.ascontiguousarray(x.astype(np.float32))
    return {"F1": f(F1), "TT1": f(TT1), "TT2": f(TT2), "C2": f(C2), "S2": f(S2), "IC": f(IC), "IS": f(IS), "Cf": f(Cf), "nSf": f(nSf)}


def fwd_pair(P, C, K, xs, ci, sid):
    pa = C.ps()
    for j in range(2):
        P.op("pe", lambda e: e.matmul(pa[:, j * 256:(j + 1) * 256], xs[:, ci + j, :], K["F1"][:, :], start=True, stop=True), rd=[xs, K["F1"]], wr=[pa])
        yield
    p1 = C.rot("fp1" + sid, [128, 512], 1); p2 = C.rot("fp2" + sid, [128, 512], 1)
    P.op("dve", lambda e: e.tensor_tensor(out=p1[:], in0=pa[:, :512], in1=K["TT1"][:].rearrange("p a b -> p (a b)"), op=ALU.mult), rd=[pa, K["TT1"]], wr=[p1])
    yield
    P.op("dve", lambda e: e.tensor_tensor(out=p2[:], in0=pa[:, :512], in1=K["TT2"][:].rearrange("p a b -> p (a b)"), op=ALU.mult), rd=[pa, K["TT2"]], wr=[p2])
    yield
    b1 = C.rot("fb1" + sid, [128, 2, 256], 1); b2 = C.rot("fb2" + sid, [128, 2, 256], 1)
    v1 = p1[:].rearrange("p (a b) -> p a b", a=2); v2 = p2[:].rearrange("p (a b) -> p a b", a=2)
    P.op("pool", lambda e: e.tensor_tensor(out=b1[:, :, 0:128], in0=v1[:, :, 0:128], in1=v2[:, :, 128:256], op=ALU.subtract), rd=[p1, p2], wr=[(b1, 0)])
    yield
    P.op("pool", lambda e: e.tensor_tensor(out=b1[:, :, 128:256], in0=v2[:, :, 0:128], in1=v1[:, :, 128:256], op=ALU.add), rd=[p1, p2], wr=[(b1, 1)])
    yield
    P.op("act", lambda e: e.copy(out=b2[:, :, 0:128], in_=b1[:, :, 128:256]), rd=[(b1, 1)], wr=[(b2, 0)])
    yield
    P.op("pool", lambda e: e.tensor_tensor(out=b2[:, :, 128:256], in0=v2[:, :, 128:256], in1=v1[:, :, 0:128], op=ALU.subtract), rd=[p1, p2], wr=[(b2, 1)])
    yield
    px = C.ps()
    P.op("pe", lambda e: e.matmul(px[:, :512], K["C2"][:, :], b1[:].rearrange("p a b -> p (a b)"), start=True, stop=False), rd=[K["C2"], b1], wr=[px])
    yield
    P.op("pe", lambda e: e.matmul(px[:, :512], K["S2"][:, :], b2[:].rearrange("p a b -> p (a b)"), start=False, stop=True), rd=[K["S2"], b2], wr=[px])
    yield
    return px


def conv_group(P, C, K, xs, hf, hb, gate, outt, sid):
    qr = C.rot("qr" + sid, [128, GCH, 128], 1); qi = C.rot("qi" + sid, [128, GCH, 128], 1)
    for ci in range(0, GCH, 2):
        pf = yield from fwd_pair(P, C, K, hf, ci, sid)
        hfs = C.rot("hfs" + sid, [128, 512], 1)
        P.op("act", lambda e: e.copy(out=hfs[:], in_=pf[:, :512]), rd=[pf], wr=[hfs])
        yield
        pb = yield from fwd_pair(P, C, K, hb, ci, sid)
        k1 = C.rot("k1" + sid, [128, 2, 256], 1); k2 = C.rot("k2" + sid, [128, 2, 256], 1)
        hv = hfs[:].rearrange("p (a b) -> p a b", a=2); pbv = pb[:, :512].rearrange("p (a b) -> p a b", a=2)
        P.op("dve", lambda e: e.tensor_tensor(out=k1[:, :, 0:128], in0=pbv[:, :, 0:128], in1=hv[:, :, 0:128], op=ALU.add), rd=[pb, hfs], wr=[(k1, 0)])
        yield
        P.op("dve", lambda e: e.tensor_tensor(out=k2[:, :, 0:128], in0=pbv[:, :, 128:256], in1=hv[:, :, 128:256], op=ALU.subtract), rd=[pb, hfs], wr=[(k2, 0)])
        yield
        P.op("act", lambda e: e.copy(out=k1[:, :, 128:256], in_=k1[:, :, 0:128]), rd=[(k1, 0)], wr=[(k1, 1)])
        yield
        P.op("act", lambda e: e.activation(out=k2[:, :, 0:128], in_=k2[:, :, 0:128], func=AF.Identity, scale=-1.0), rd=[(k2, 0)], wr=[(k2, 0)])
        yield
        P.op("act", lambda e: e.copy(out=k2[:, :, 128:256], in_=k2[:, :, 0:128]), rd=[(k2, 0)], wr=[(k2, 1)])
        yield
        pu = yield from fwd_pair(P, C, K, xs, ci, sid)
        p1 = C.rot("fp1" + sid, [128, 512], 1); p2 = C.rot("fp2" + sid, [128, 512], 1)
        P.op("dve", lambda e: e.tensor_tensor(out=p1[:], in0=pu[:, :512], in1=k1[:].rearrange("p a b -> p (a b)"), op=ALU.mult), rd=[pu, k1], wr=[p1])
        yield
        P.op("dve", lambda e: e.tensor_tensor(out=p2[:], in0=pu[:, :512], in1=k2[:].rearrange("p a b -> p (a b)"), op=ALU.mult), rd=[pu, k2], wr=[p2])
        yield
        v1 = p1[:].rearrange("p (a b) -> p a b", a=2); v2 = p2[:].rearrange("p (a b) -> p a b", a=2)
        yr = C.rot("yr" + sid, [128, 2, 128], 1); yi = C.rot("yi" + sid, [128, 2, 128], 1)
        P.op("pool", lambda e: e.tensor_tensor(out=yr[:], in0=v1[:, :, 0:128], in1=v2[:, :, 128:256], op=ALU.subtract), rd=[p1, p2], wr=[yr])
        yield
        P.op("pool", lambda e: e.tensor_tensor(out=yi[:], in0=v2[:, :, 0:128], in1=v1[:, :, 128:256], op=ALU.add), rd=[p1, p2], wr=[yi])
        yield
        pp = C.ps()
        for j in range(2):
            P.op("pe", lambda e: e.matmul(pp[:, j * 256:(j + 1) * 256], yr[:, j, :], K["IC"][:, :], start=True, stop=False), rd=[yr, K["IC"]], wr=[pp])
            yield
            P.op("pe", lambda e: e.matmul(pp[:, j * 256:(j + 1) * 256], yi[:, j, :], K["IS"][:, :], start=False, stop=True), rd=[yi, K["IS"]], wr=[pp])
            yield
        p1 = C.rot("fp1" + sid, [128, 512], 1); p2 = C.rot("fp2" + sid, [128, 512], 1)
        P.op("dve", lambda e: e.tensor_tensor(out=p1[:], in0=pp[:, :512], in1=K["TT1"][:].rearrange("p a b -> p (a b)"), op=ALU.mult), rd=[pp, K["TT1"]], wr=[p1])
        yield
        P.op("dve", lambda e: e.tensor_tensor(out=p2[:], in0=pp[:, :512], in1=K["TT2"][:].rearrange("p a b -> p (a b)"), op=ALU.mult), rd=[pp, K["TT2"]], wr=[p2])
        yield
        v1 = p1[:].rearrange("p (a b) -> p a b", a=2); v2 = p2[:].rearrange("p (a b) -> p a b", a=2)
        P.op("pool", lambda e: e.tensor_tensor(out=qr[:, ci:ci + 2, :], in0=v1[:, :, 0:128], in1=v2[:, :, 128:256], op=ALU.add), rd=[p1, p2], wr=[(qr, ci)])
        yield
        P.op("pool", lambda e: e.tensor_tensor(out=qi[:, ci:ci + 2, :], in0=v1[:, :, 128:256], in1=v2[:, :, 0:128], op=ALU.subtract), rd=[p1, p2], wr=[(qi, ci)])
        yield
    py = C.ps()
    P.op("pe", lambda e: e.matmul(py[:64, :GCH * 128], K["Cf"][:, :], qr[:].rearrange("p a b -> p (a b)"), start=True, stop=False), rd=[K["Cf"], qr], wr=[py])
    yield
    P.op("pe", lambda e: e.matmul(py[:64, :GCH * 128], K["nSf"][:, :], qi[:].rearrange("p a b -> p (a b)"), start=False, stop=True), rd=[K["nSf"], qi], wr=[py])
    yield
    P.op("dve", lambda e: e.tensor_tensor(out=outt[:].rearrange("p a b -> p (a b)"), in0=py[:64, :GCH * 128], in1=gate[:].rearrange("p a b -> p (a b)"), op=ALU.mult), rd=[py, gate], wr=[outt])
    yield


def hyena_filters_to_dram(P, C, L, featT, tn, wts, scratch, tag):
    HA = P.sb("HA" + tag, [128, L]); HB = P.sb("HB" + tag, [128, L])
    w1, w2, w3, w4, fr, fb, nd, dsk = (wts[k] for k in ("w1", "w2", "w3", "w4", "fr", "fb", "nd", "dsk"))
    blks = blocks(L)
    nb = len(blks)
    for o in range(2):
        l1p = P.sb(f"l1p{tag}{o}", [128, 2, nb + 1])
        P.op("dve", lambda e: e.memset(l1p[:], 0.0), wr=[l1p])
        for bi, (s, w) in enumerate(blks):
            ft = C.rot("ft", [33, 512], 2); tnb = C.rot("tnb", [128, 512], 2)
            P.dma(ft[:, :w], featT[:, s:s + w], rd=[featT], wr=[ft], q="sp")
            P.dma(tnb[:, :w], tn[:, s:s + w], rd=[tn], wr=[tnb], q="pool")
            hin, kdim = ft, 33
            for li, wl in enumerate((w1, w2, w3)):
                ps = C.ps()
                P.op("pe", lambda e: e.matmul(ps[:64, :w], wl[:kdim, :], hin[:kdim, :w], start=True, stop=True), rd=[wl, hin], wr=[ps])
                pre = C.rot("pre", [64, 512], 2); kf = C.rot("kf", [64, 512], 2); ki = C.rot("ki", [64, 512], 2, I32)
                P.op("dve", lambda e: e.tensor_scalar(out=pre[:, :w], in0=ps[:64, :w], scalar1=fr[:, 0:1], scalar2=fb[:, li:li + 1], op0=ALU.mult, op1=ALU.add), rd=[ps, fr, fb], wr=[pre])
                P.op("dve", lambda e: e.tensor_scalar(out=kf[:, :w], in0=pre[:, :w], scalar1=1.0 / TWO_PI, scalar2=None, op0=ALU.mult), rd=[pre], wr=[kf])
                P.op("dve", lambda e: e.tensor_copy(out=ki[:, :w], in_=kf[:, :w]), rd=[kf], wr=[ki])
                P.op("dve", lambda e: e.tensor_copy(out=kf[:, :w], in_=ki[:, :w]), rd=[ki], wr=[kf])
                P.op("dve", lambda e: e.scalar_tensor_tensor(out=pre[:, :w], in0=kf[:, :w], scalar=-TWO_PI, in1=pre[:, :w], op0=ALU.mult, op1=ALU.add), rd=[kf, pre], wr=[pre])
                P.op("dve", lambda e: e.tensor_scalar(out=pre[:, :w], in0=pre[:, :w], scalar1=3.14159, scalar2=-3.14159, op0=ALU.min, op1=ALU.max), rd=[pre], wr=[pre])
                hcur = C.rot(f"hl{li}", [64, 512], 2)
                P.op("act", lambda e: e.activation(out=hcur[:, :w], in_=pre[:, :w], func=AF.Sin), rd=[pre], wr=[hcur])
                hin, kdim = hcur, 64
            for di, Hd in enumerate((HA, HB)):
                gi = o * 2 + di
                ps = C.ps()
                P.op("pe", lambda e: e.matmul(ps[:, :w], w4[:, gi, :], hin[:, :w], start=True, stop=True), rd=[w4, hin], wr=[ps])
                dec = C.rot("dec", [128, 512], 2)
                P.op("act", lambda e: e.activation(out=dec[:, :w], in_=tnb[:, :w], func=AF.Exp, scale=nd[:, gi:gi + 1]), rd=[tnb, nd], wr=[dec])
                P.op("dve", lambda e: e.tensor_tensor(out=Hd[:, s:s + w], in0=ps[:, :w], in1=dec[:, :w], op=ALU.mult), rd=[ps, dec], wr=[(Hd, bi)])
                ab = C.rot("ab", [128, 512], 2)
                P.op("act", lambda e: e.activation(out=ab[:, :w], in_=Hd[:, s:s + w], func=AF.Abs, accum_out=l1p[:, di, bi:bi + 1]), rd=[(Hd, bi)], wr=[ab, (l1p, (di, bi))])
        t4 = P.sb(f"t4{tag}{o}", [128, 6])
        P.op("dve", lambda e: e.tensor_reduce(out=t4[:, 0:1], in_=l1p[:], axis=AX.XY, op=ALU.add), rd=[l1p], wr=[t4])
        P.op("act", lambda e: e.activation(out=t4[:, 1:2], in_=HB[:, 0:1], func=AF.Abs), rd=[HB], wr=[t4])
        P.op("dve", lambda e: e.tensor_tensor(out=t4[:, 2:3], in0=t4[:, 0:1], in1=t4[:, 1:2], op=ALU.subtract), rd=[t4], wr=[t4])
        P.op("dve", lambda e: e.reciprocal(out=t4[:, 3:4], in_=t4[:, 2:3]), rd=[t4], wr=[t4])
        P.op("dve", lambda e: e.tensor_scalar(out=HA[:], in0=HA[:], scalar1=t4[:, 3:4], scalar2=None, op0=ALU.mult), rd=[HA, t4], wr=[HA])
        P.op("dve", lambda e: e.tensor_scalar(out=HB[:], in0=HB[:], scalar1=t4[:, 3:4], scalar2=None, op0=ALU.mult), rd=[HB, t4], wr=[HB])
        P.op("dve", lambda e: e.tensor_tensor(out=HA[:, 0:1], in0=HA[:, 0:1], in1=dsk[:, o:o + 1], op=ALU.add), rd=[HA, dsk], wr=[HA])
        P.op("dve", lambda e: e.memset(HB[:, 0:1], 0.0), rd=[], wr=[HB])
        P.dma(scratch[2 * o, :, :], HA[:], rd=[HA], wr=[(scratch, 2 * o)], q="sp")
        P.dma(scratch[2 * o + 1, :, :], HB[:], rd=[HB], wr=[(scratch, 2 * o + 1)], q="pool")


def hyena_dft_part(P, C, uin, featT, tn, wts, K, scratch, out):
    with ExitStack() as es2:
        old, P.es = P.es, es2
        hyena_filters_to_dram(P, C, 8192, featT, tn, wts, scratch, "D")
        P.barrier()
        P.es = old
        C.tmp = {}
    def stream(sid, groups):
        for g in groups:
            c0 = g * GCH
            xs = C.rot("gxs" + sid, [64, GCH, 128], 1); x1 = C.rot("gx1" + sid, [64, GCH, 128], 1); x2 = C.rot("gx2" + sid, [64, GCH, 128], 1)
            P.dma(xs[:], uin[:, 0, c0:c0 + GCH, :], rd=[uin], wr=[xs], q="sp")
            yield
            P.dma(x1[:], uin[:, 1, c0:c0 + GCH, :], rd=[uin], wr=[x1], q="pool")
            yield
            P.dma(x2[:], uin[:, 2, c0:c0 + GCH, :], rd=[uin], wr=[x2], q="sp")
            yield
            hs = []
            for gi in range(4):
                h = C.rot("gh%d" % gi + sid, [64, GCH, 128], 1)
                P.dma(h[:], scratch[gi, c0:c0 + GCH, :].rearrange("c (a b) -> a c b", b=128), rd=[(scratch, gi)], wr=[h], q=C.q())
                yield
                hs.append(h)
            y1 = C.rot("gy1" + sid, [64, GCH, 128], 1); y2 = C.rot("gy2" + sid, [64, GCH, 128], 1)
            yield from conv_group(P, C, K, xs, hs[0], hs[1], x1, y1, sid)
            yield from conv_group(P, C, K, y1, hs[2], hs[3], x2, y2, sid)
            P.dma(out[:, c0:c0 + GCH, :], y2[:], rd=[y2], wr=[(out, g)], q=C.q())
            yield
    NS = 4
    ng = 128 // GCH
    with ExitStack() as es3:
        old, P.es = P.es, es3
        run_interleaved([stream("s%d" % i, range(i, ng, NS)) for i in range(NS)])
        P.barrier()
        P.es = old
        C.tmp = {}


DFT_SPECS = [("F1", [64, 256]), ("TT1", [128, 2, 256]), ("TT2", [128, 2, 256]), ("C2", [128, 128]), ("S2", [128, 128]), ("IC", [128, 256]), ("IS", [128, 256]), ("Cf", [128, 64]), ("nSf", [128, 64])]


def build_H2():
    nc = bass.Bass("TRN2", target_bir_lowering=False)
    es = ExitStack()
    with es:
        P = Prog(nc, es)
        C = Ctx(P)
        I = lambda n, s: P.dram(n, s, kind="ExternalInput")
        wts = {}
        for nm, shp in (("w1", [33, 64]), ("w2", [64, 64]), ("w3", [64, 64]), ("w4", [64, 4, 128]), ("fr", [64, 1]), ("fb", [64, 3]), ("nd", [128, 4]), ("dsk", [128, 2])):
            d = I("h_" + nm, shp); t = P.sb("s_" + nm, shp); P.dma(t[:], d[:], rd=[d], wr=[t]); wts[nm] = t
        P.op("dve", lambda e: e.tensor_scalar(out=wts["fb"][:], in0=wts["fb"][:], scalar1=wts["fr"][:, 0:1], scalar2=None, op0=ALU.mult), rd=[wts["fb"], wts["fr"]], wr=[wts["fb"]])
        K = load_consts2(P, DFT_SPECS)
        uin = I("ud", [64, 3, 128, 128]); featT = I("feat8192", [33, 8192]); tn = I("tn8192", [128, 8192])
        scratch = P.dram("hscr", [4, 128, 8192])
        out = P.dram("yd", [64, 128, 128], kind="ExternalOutput")
        hyena_dft_part(P, C, uin, featT, tn, wts, K, scratch, out)
        P.finish()
        print("phase H2 instructions", P.nins, P.cnt)
    return nc


def hyena_dft_maps(inp, zT, maps):
    tabs = dft_tables()
    for k in range(8):
        b, h = k // 4, k % 4
        m = maps[k]
        u = m.pop("u8192")
        m["ud"] = np.ascontiguousarray(u.reshape(128, 3, 64, 128).transpose(2, 1, 0, 3))
        m.update(tabs)
    return maps


DFF = 2816
DQK = 192


def conv3m(C, z, u, T, H, hm, wcol, bcol, mask, mw=128):
    P = C.P
    n = T - 2 * H
    P.op("dve", lambda e: e.tensor_scalar(out=z[:mw, 0:hm], in0=z[:mw, 0:hm], scalar1=mask[:mw, 0:1], scalar2=None, op0=ALU.mult), rd=[z], wr=[z])
    P.op("dve", lambda e: e.tensor_scalar(out=z[:mw, T - hm:T], in0=z[:mw, T - hm:T], scalar1=mask[:mw, 1:2], scalar2=None, op0=ALU.mult), rd=[z], wr=[z])
    P.op("dve", lambda e: e.tensor_scalar(out=u[:mw, 0:n], in0=z[:mw, H - 1:H - 1 + n], scalar1=wcol[:mw, 0:1], scalar2=bcol[:mw, 0:1], op0=ALU.mult, op1=ALU.add), rd=[z], wr=[u])
    P.op("dve", lambda e: e.scalar_tensor_tensor(out=u[:mw, 0:n], in0=z[:mw, H:H + n], scalar=wcol[:mw, 1:2], in1=u[:mw, 0:n], op0=ALU.mult, op1=ALU.add), rd=[z, u], wr=[u])
    P.op("dve", lambda e: e.scalar_tensor_tensor(out=u[:mw, 0:n], in0=z[:mw, H + 1:H + 1 + n], scalar=wcol[:mw, 2:3], in1=u[:mw, 0:n], op0=ALU.mult, op1=ALU.add), rd=[z, u], wr=[u])


def rms_rs(C, srcs, n, inv_d, name="rsx"):
    P = C.P
    rs = C.rot(name, [128, 1032], 2)
    for bi, (s, w) in enumerate(blocks(n)):
        ps = C.ps()
        for i, (tt, fn, rows) in enumerate(srcs):
            sq = C.rot("sq", [128, 512], 3)
            P.op("act", lambda e: e.activation(out=sq[:rows, :w], in_=fn(s, w), func=AF.Square), rd=[tt], wr=[sq])
            P.op("pe", lambda e: e.matmul(ps[:, :w], C.ones[:rows, :], sq[:rows, :w], start=(i == 0), stop=(i == len(srcs) - 1)), rd=[sq, C.ones], wr=[ps])
        P.op("dve", lambda e: e.tensor_scalar(out=rs[:, s:s + w], in0=ps[:, :w], scalar1=inv_d, scalar2=EPS, op0=ALU.mult, op1=ALU.add), rd=[ps], wr=[(rs, bi)])
        P.op("act", lambda e: e.activation(out=rs[:, s:s + w], in_=rs[:, s:s + w], func=AF.Sqrt), rd=[(rs, bi)], wr=[(rs, bi)])
        P.op("dve", lambda e: e.reciprocal(out=rs[:, s:s + w], in_=rs[:, s:s + w]), rd=[(rs, bi)], wr=[(rs, bi)])
    return rs


def mixer_out_ffn(C, xt, hT, T, mods, col, wout, wup, wdn, fcw, fcb, mask):
    P = C.P
    for m in range(8):
        def consume(ps, bi, s, w, m=m):
            P.op("dve", lambda e: e.scalar_tensor_tensor(out=xt[:, m, s:s + w], in0=ps[:, :w], scalar=mods[:, 16 + m, col:col + 1], in1=xt[:, m, s:s + w], op0=ALU.mult, op1=ALU.add), rd=[ps, mods, (xt, bi)], wr=[(xt, bi)])
        linear_T(C, wout, m * 128, 128, hT, T, consume)
    h2 = hT
    norm_mod(C, xt, h2, T, mods, 3, 4, col)
    n1 = T - 2
    for f in range(DFF // 128):
        zs = []
        for half in range(2):
            z = C.rot("zrow", [128, 1032], 2)
            def consume(ps, bi, s, w, z=z):
                P.op("act", lambda e: e.copy(out=z[:, s:s + w], in_=ps[:, :w]), rd=[ps], wr=[(z, bi)])
            linear_T(C, wup, half * DFF + f * 128, 128, h2, T, consume)
            u = C.rot("urow", [128, 1032], 3)
            conv3m(C, z, u, T, 1, 2, fcw[:, half * 22 + f, :], fcb[:, half * 22 + f:half * 22 + f + 1], mask)
            zs.append(u)
        u1, u2 = zs
        P.op("act", lambda e: e.activation(out=u1[:, :n1], in_=u1[:, :n1], func=AF.Silu), rd=[u1], wr=[u1])
        P.op("dve", lambda e: e.tensor_tensor(out=u1[:, :n1], in0=u1[:, :n1], in1=u2[:, :n1], op=ALU.mult), rd=[u1, u2], wr=[u1])
        wd = C.rot("wd", [128, 1024], 2)
        P.dma(wd[:], wdn[f * 128:(f + 1) * 128, :], rd=[wdn], wr=[wd], q=C.q())
        for m in range(8):
            for bi, (s, w) in enumerate(blocks(n1)):
                ps = C.ps()
                P.op("pe", lambda e: e.matmul(ps[:, :w], wd[:, m * 128:(m + 1) * 128], u1[:, s:s + w], start=True, stop=True), rd=[wd, u1], wr=[ps])
                P.op("dve", lambda e: e.scalar_tensor_tensor(out=xt[:, m, 1 + s:1 + s + w], in0=ps[:, :w], scalar=mods[:, 40 + m, col:col + 1], in1=xt[:, m, 1 + s:1 + s + w], op0=ALU.mult, op1=ALU.add), rd=[ps, mods, xt], wr=[xt])


def load_consts(P, specs):
    out = {}
    for nm, shp in specs:
        d = P.dram(nm, shp, kind="ExternalInput"); t = P.sb("c_" + nm, shp)
        P.dma(t[:], d[:], rd=[d], wr=[t]); out[nm] = t
    return out


SEG_C = [(1024, 0), (1024, 0), (64, 1)]
L1_CHUNKS = ([(i * 128, 128, "qk", i) for i in range(8)] + [(1024 + i * 128, 128, "v", 8 + i) for i in range(4)] +
             [(1536 + i * 128, 128, "plain", None) for i in range(4)] + [(2048, 16, "plain", None)])
GD_ROWS = 2064
ML_ROWS = 4 * 512


def layer1_prep(C, xt, hT, T, n, mods1, col, win1, K, mask, si, gd_out, ml_out):
    P = C.P
    H = 2
    norm_mod(C, xt, hT, T, mods1, 0, 1, col)
    for (col0, mw, kind, idx) in L1_CHUNKS:
        z = C.rot("zrow", [128, 1032], 2)
        def consume(ps, bi, s, w, z=z, mw=mw):
            P.op("act", lambda e: e.copy(out=z[:mw, s:s + w], in_=ps[:mw, :w]), rd=[ps], wr=[(z, bi)])
        linear_T(C, win1, col0, mw, hT, T, consume)
        if kind == "plain":
            P.dma(gd_out[col0:col0 + mw, :], z[:mw, H:H + n], rd=[z], wr=[(gd_out, col0)], q=C.q())
            continue
        u = C.rot("urow", [128, 1032], 3)
        conv3m(C, z, u, T, 2, 2, K["gcw"][:, idx, :], K["gcb"][:, idx:idx + 1], mask)
        P.op("act", lambda e: e.activation(out=u[:, :n], in_=u[:, :n], func=AF.Silu), rd=[u], wr=[u])
        if kind == "qk":
            rs = rms_rs(C, [(u, lambda s, w, u=u: u[:, s:s + w], 128)], n, 1.0)
            P.op("dve", lambda e: e.tensor_tensor(out=u[:, :n], in0=u[:, :n], in1=rs[:, :n], op=ALU.mult), rd=[u, rs], wr=[u])
        P.dma(gd_out[col0:col0 + mw, :], u[:, :n], rd=[u], wr=[(gd_out, col0)], q=C.q())
    if "lq" not in K:
        K["lq"] = P.sb("lq", [128, 3, 1024]); K["lkv"] = P.sb("lkv", [128, 2, 1024]); K["lkr"] = P.sb("lkr", [64, 1024]); K["ropeT"] = P.sb("ropeT", [64, 2, 1024])
    lqf, lkvf, lkrf, ropef = K["lq"], K["lkv"], K["lkr"], K["ropeT"]
    class V:
        def __init__(s_, tt): s_.tt = tt
    lq, lkv, lkr = lqf, lkvf, lkrf
    P.dma(ropef[:, :, :n], K[f"roped{si}"][:], rd=[K[f"roped{si}"]], wr=[ropef])
    def into(dst, dfn, col0, mw):
        def consume(ps, bi, s, w):
            lo, hi = max(s, H), min(s + w, H + n)
            if hi > lo:
                P.op("act", lambda e: e.copy(out=dfn(lo - H, hi - H), in_=ps[:mw, lo - s:hi - s]), rd=[ps], wr=[dst])
        linear_T(C, win1, col0, mw, hT, T, consume)
    for k in range(3):
        into(lq, lambda a, b, k=k: lq[:, k, a:b], 2064 + k * 128, 128)
    for k in range(2):
        into(lkv, lambda a, b, k=k: lkv[:, k, a:b], 2448 + k * 128, 128)
    into(lkr, lambda a, b: lkr[:, a:b], 2704, 64)
    rs = rms_rs(C, [(lq, lambda s, w, k=k: lq[:, k, s:s + w], 128) for k in range(3)], n, 1.0 / 384)
    for k in range(3):
        P.op("dve", lambda e: e.tensor_tensor(out=lq[:, k, :n], in0=lq[:, k, :n], in1=rs[:, :n], op=ALU.mult), rd=[lq, rs], wr=[lq])
        P.op("act", lambda e: e.activation(out=lq[:, k, :n], in_=lq[:, k, :n], func=AF.Identity, scale=K["qng"][:, k:k + 1]), rd=[lq, K["qng"]], wr=[lq])
    rs = rms_rs(C, [(lkv, lambda s, w, k=k: lkv[:, k, s:s + w], 128) for k in range(2)], n, 1.0 / 256)
    for k in range(2):
        P.op("dve", lambda e: e.tensor_tensor(out=lkv[:, k, :n], in0=lkv[:, k, :n], in1=rs[:, :n], op=ALU.mult), rd=[lkv, rs], wr=[lkv])
        P.op("act", lambda e: e.activation(out=lkv[:, k, :n], in_=lkv[:, k, :n], func=AF.Identity, scale=K["kvng"][:, k:k + 1]), rd=[lkv, K["kvng"]], wr=[lkv])
    rope = ropef

    def up(wdram, kc, src, col0, mw, dst):
        wt = C.rot("wup2", [128, 3, 128], 3)
        P.dma(wt[:, :kc, :mw], wdram[:, col0:col0 + mw].rearrange("(k p) m -> p k m", p=128), rd=[wdram], wr=[wt], q=C.q())
        for bi, (s, w) in enumerate(blocks(n)):
            ps = C.ps()
            for k in range(kc):
                P.op("pe", lambda e: e.matmul(ps[:mw, :w], wt[:, k, :mw], src[:, k, s:s + w], start=(k == 0), stop=(k == kc - 1)), rd=[wt, src], wr=[ps])
            P.op("act", lambda e: e.copy(out=dst[:mw, s:s + w], in_=ps[:mw, :w]), rd=[ps], wr=[dst])

    def finish_qk(A, B, gname, scale, h, rowbase):
        rs = rms_rs(C, [(A, lambda s, w: A[:, s:s + w], 128), (B, lambda s, w: B[:64, s:s + w], 64)], n, 1.0 / DQK)
        P.op("dve", lambda e: e.tensor_tensor(out=A[:, :n], in0=A[:, :n], in1=rs[:, :n], op=ALU.mult), rd=[A, rs], wr=[A])
        P.op("dve", lambda e: e.tensor_scalar(out=A[:, :n], in0=A[:, :n], scalar1=K[gname][:, 0:1], scalar2=scale, op0=ALU.mult, op1=ALU.mult), rd=[A, K[gname]], wr=[A])
        Bn = C.rot("Bn", [64, 1024], 1)
        P.op("dve", lambda e: e.tensor_tensor(out=Bn[:, :n], in0=B[:64, :n], in1=rs[:64, :n], op=ALU.mult), rd=[B, rs], wr=[Bn])
        P.op("dve", lambda e: e.tensor_scalar(out=Bn[:, :n], in0=Bn[:, :n], scalar1=K[gname][:64, 1:2], scalar2=scale, op0=ALU.mult, op1=ALU.mult), rd=[Bn, K[gname]], wr=[Bn])
        Br = C.rot("Br", [64, 1024], 1)
        for bi, (s, w) in enumerate(blocks(n)):
            ps = C.ps()
            P.op("pe", lambda e: e.matmul(ps[:64, :w], K["JT"][:, :], Bn[:, s:s + w], start=True, stop=True), rd=[K["JT"], Bn], wr=[ps])
            P.op("dve", lambda e: e.tensor_tensor(out=Br[:, s:s + w], in0=ps[:64, :w], in1=rope[:, 1, s:s + w], op=ALU.mult), rd=[ps, rope], wr=[Br])
        P.op("dve", lambda e: e.tensor_tensor(out=Bn[:, :n], in0=Bn[:, :n], in1=rope[:, 0, :n], op=ALU.mult), rd=[Bn, rope], wr=[Bn])
        P.op("dve", lambda e: e.tensor_tensor(out=Bn[:, :n], in0=Bn[:, :n], in1=Br[:, :n], op=ALU.add), rd=[Bn, Br], wr=[Bn])
        P.dma(ml_out[rowbase:rowbase + 128, :], A[:, :n], rd=[A], wr=[(ml_out, rowbase)], q=C.q())
        P.dma(ml_out[rowbase + 128:rowbase + 192, :], Bn[:, :n], rd=[Bn], wr=[(ml_out, rowbase + 128)], q=C.q())

    for h in range(4):
        base = h * 512
        A = C.rot("qA", [128, 1024], 2); B = C.rot("qB", [128, 1024], 2)
        up(K["wq"], 3, lq, h * DQK, 128, A)
        up(K["wq"], 3, lq, h * DQK + 128, 64, B)
        finish_qk(A, B, "qn2", float(DQK ** -0.5), h, base)
        A = C.rot("qA", [128, 1024], 2)
        up(K["wkv"], 2, lkv, h * 256, 128, A)
        finish_qk(A, lkr, "kn2", 1.0, h, base + 192)
        V = C.rot("qB", [128, 1024], 2)
        up(K["wkv"], 2, lkv, h * 256 + 128, 128, V)
        P.dma(ml_out[base + 384:base + 512, :], V[:, :n], rd=[V], wr=[(ml_out, base + 384)], q=C.q())


def build_C(layer1=True, H=2, segs=SEG_C):
    nc = bass.Bass("TRN2", target_bir_lowering=False)
    es = ExitStack()
    with es:
        P = Prog(nc, es)
        C = Ctx(P)
        I = lambda nm, s: P.dram(nm, s, kind="ExternalInput")
        xin = [I(f"x{i}", [128, 8, n + 2 * H]) for i, (n, _) in enumerate(segs)]
        min_ = [I(f"m{i}", [128, 8, n + 2 * H]) for i, (n, _) in enumerate(segs)]
        wout = I("wout", [1024, 1024]); wup = I("wup", [1024, 2 * DFF]); wdn = I("wdn", [DFF, 1024])
        specs = [("masks", [128, len(segs), 2]), ("fcw", [128, 44, 3]), ("fcb", [128, 44])]
        if layer1:
            win1 = I("win1", [1024, 2768])
            specs += [("gcw", [128, 12, 3]), ("gcb", [128, 12]), ("qng", [128, 3]), ("kvng", [128, 2]), ("qn2", [128, 2]), ("kn2", [128, 2]), ("JT", [64, 64])]
        K = load_consts(P, specs)
        if layer1:
            for i, (n, _) in enumerate(segs):
                K[f"roped{i}"] = I(f"rope{i}", [64, 2, n])
        if layer1:
            K["wq"] = I("wq", [384, 768]); K["wkv"] = I("wkv", [256, 1024])
        xo = [P.dram(f"xo{i}", [128, 8, n], kind="ExternalOutput") for i, (n, _) in enumerate(segs)]
        if layer1:
            gd = [P.dram(f"gd{i}", [GD_ROWS, n], kind="ExternalOutput") for i, (n, _) in enumerate(segs)]
            ml = [P.dram(f"ml{i}", [ML_ROWS, n], kind="ExternalOutput") for i, (n, _) in enumerate(segs)]
        md = I("modsin", [128, 48, 2]); mods = P.sb("mods", [128, 48, 2]); P.dma(mods[:], md[:], rd=[md], wr=[mods])
        if layer1:
            md1 = I("mods1in", [128, 48, 2]); mods1 = P.sb("mods1", [128, 48, 2]); P.dma(mods1[:], md1[:], rd=[md1], wr=[mods1])
        TM = max(n for n, _ in segs) + 2 * H
        xt = P.sb("xt", [128, 8, TM]); hT = P.sb("hT", [128, 8, TM])
        for si, (n, col) in enumerate(segs):
            T = n + 2 * H
            P.dma(xt[:, :, :T], xin[si][:], rd=[xin[si]], wr=[xt])
            P.dma(hT[:, :, :T], min_[si][:], rd=[min_[si]], wr=[hT], q="pool")
            mixer_out_ffn(C, xt, hT, T, mods, col, wout, wup, wdn, K["fcw"], K["fcb"], K["masks"][:, si, :])
            P.dma(xo[si][:], xt[:, :, H:H + n], rd=[xt], wr=[xo[si]])
            if layer1:
                layer1_prep(C, xt, hT, T, n, mods1, col, win1, K, K["masks"][:, si, :], si, gd[si], ml[si])
        P.finish()
        print("phase C instructions", P.nins, P.cnt)
    return nc


def compute_mods1(C, modw, modbT, cT):
    P = C.P
    sc = P.sb("sc1", [128, 8, 2]); P.dma(sc[:], cT[:], rd=[cT], wr=[sc])
    P.op("act", lambda e: e.activation(out=sc[:], in_=sc[:], func=AF.Silu), rd=[sc], wr=[sc])
    mb = P.sb("mb1", [128, 48]); P.dma(mb[:], modbT[:], rd=[modbT], wr=[mb])
    mods = P.sb("mods1", [128, 48, 2])
    for m in range(48):
        wt = C.rot("win", [128, 8, 128], 3)
        P.dma(wt[:], modw[:, m * 128:(m + 1) * 128].rearrange("(k p) m -> p k m", p=128), rd=[modw], wr=[wt], q=C.q())
        ps = C.ps()
        for k in range(8):
            P.op("pe", lambda e: e.matmul(ps[:, 0:2], wt[:, k, :], sc[:, k, :], start=(k == 0), stop=(k == 7)), rd=[wt, sc], wr=[ps])
        P.op("dve", lambda e: e.tensor_scalar(out=mods[:, m, :], in0=ps[:, 0:2], scalar1=mb[:, m:m + 1], scalar2=None, op0=ALU.add), rd=[ps, mb], wr=[(mods, m)])
    for j in (1, 4):
        P.op("dve", lambda e: e.tensor_scalar(out=mods[:, j * 8:(j + 1) * 8, :], in0=mods[:, j * 8:(j + 1) * 8, :], scalar1=1.0, scalar2=None, op0=ALU.add), rd=[mods], wr=[mods])
    return mods


def rope_tab(pos):
    n = len(pos)
    row = (pos // 64).astype(np.float32); colp = (pos % 64).astype(np.float32)
    inv = (np.float32(10000.0) ** (-np.arange(16, dtype=np.float32) / np.float32(16))).astype(np.float32)
    ang = np.concatenate([row[:, None] * inv, colp[:, None] * inv], -1).astype(np.float32)
    c, s = np.cos(ang).astype(np.float32).T, np.sin(ang).astype(np.float32).T
    return np.ascontiguousarray(np.stack([np.concatenate([c, c], 0), np.concatenate([s, s], 0)], 1))


def c_maps(inp, layer, xfull, cfull, mixfull, mixc, layer1=True, H=2):
    maps = []
    JT = np.zeros((64, 64), np.float32)
    for m_ in range(32):
        JT[m_ + 32, m_] = -1.0; JT[m_, m_ + 32] = 1.0
    for k in range(8):
        b, qtr = k // 4, k % 4
        m = {}
        mk = np.zeros((128, 3, 2), np.float32)
        for si in range(2):
            st = qtr * 2048 + si * 1024
            m[f"x{si}"], l, r = seg_slice_T(xfull[b], st, 1024, H); mk[:, si, 0] = l; mk[:, si, 1] = r
            m[f"m{si}"], _, _ = seg_slice_T(mixfull[b], st, 1024, H)
            if layer1:
                m[f"rope{si}"] = rope_tab(np.arange(st, st + 1024))
        m["x2"], l, r = seg_slice_T(cfull[b], qtr * 64, 64, H); mk[:, 2, 0] = l; mk[:, 2, 1] = r
        m["m2"], _, _ = seg_slice_T(mixc[b], qtr * 64, 64, H)
        m["masks"] = mk
        m["modsin"] = MODS_CACHE[layer][k]
        m["wout"] = inp["ab_w_out"][0] if layer == 0 else inp["cd_w_out"][0]
        m["wup"] = inp["ffn_w_up"][layer]; m["wdn"] = inp["ffn_w_down"][layer]
        m["fcw"] = np.ascontiguousarray(inp["ffn_conv_w"][layer].T.reshape(44, 128, 3).transpose(1, 0, 2))
        m["fcb"] = colT(inp["ffn_conv_b"][layer])
        if layer1:
            r0 = np.zeros((64, 2, 64), np.float32); r0[:, 0, :] = 1.0
            m["rope2"] = r0
            m["mods1in"] = MODS_CACHE[1][k]; m["win1"] = inp["cd_w_in"][0]
            m["gcw"] = np.ascontiguousarray(inp["c_conv_w"][0].T.reshape(12, 128, 3).transpose(1, 0, 2)); m["gcb"] = colT(inp["c_conv_b"][0])
            m["qng"] = colT(inp["d_q_norm_g"][0]); m["kvng"] = colT(inp["d_kv_norm_g"][0])
            def g2(v):
                o = np.zeros((128, 2), np.float32); o[:, 0] = v[:128]; o[:64, 1] = v[128:192]
                return o
            m["qn2"] = g2(inp["d_qn_g"][0]); m["kn2"] = g2(inp["d_kn_g"][0]); m["JT"] = JT
            m["wq"] = inp["d_w_q_up"][0]; m["wkv"] = inp["d_w_kv_up"][0]
        maps.append(m)
    return maps


def unT(a):
    return a.transpose(2, 1, 0).reshape(a.shape[2], 1024)


def gather_C(results, layer1=True):
    x2 = np.zeros((2, 8192, 1024), np.float32); c2 = np.zeros((2, 256, 1024), np.float32)
    gdT = np.zeros((2, GD_ROWS, 8192), np.float32); gdc = np.zeros((2, GD_ROWS, 256), np.float32)
    mlT = np.zeros((2, ML_ROWS, 8192), np.float32); mlc = np.zeros((2, ML_ROWS, 256), np.float32)
    for k in range(8):
        b, qtr = k // 4, k % 4
        r = results[k]
        for si in range(2):
            sl = slice(qtr * 2048 + si * 1024, qtr * 2048 + (si + 1) * 1024)
            x2[b][sl] = unT(r[f"xo{si}"])
            if layer1:
                gdT[b][:, sl] = r[f"gd{si}"]; mlT[b][:, sl] = r[f"ml{si}"]
        if "xo2" in r:
            cs = slice(qtr * 64, (qtr + 1) * 64)
            c2[b][cs] = unT(r["xo2"])
            if layer1:
                gdc[b][:, cs] = r["gd2"]; mlc[b][:, cs] = r["ml2"]
    return x2, c2, gdT, gdc, mlT, mlc


def gdn_part(P, C, qT, kT, ktok, vtok, ggtok, gab, gconst, cm, normg, out_g):
    DK = 128
    M = P.sb("gM", [64, 5, 64]); P.dma(M[:], cm[:], rd=[cm], wr=[M])
    gc = P.sb("ggc", [64, 4]); P.dma(gc[:], gconst[:], rd=[gconst], wr=[gc])
    ng = P.sb("gng", [64, 128]); P.dma(ng[:], normg[:], rd=[normg], wr=[ng])
    g4 = P.sb("gg4", [64, 4, NCH]); P.dma(g4[:], gab[:], rd=[gab], wr=[g4])
    hsum = P.sb("ghsum", [64, NCH, 128])
    S = P.sb("gS", [128, 128])
    la = P.sb("gla", [64, 2, NCH]); gs = P.sb("ggs", [64, 2, NCH]); eg = P.sb("geg", [64, 2, NCH]); egl = P.sb("gegl", [64, 2, NCH])
    nbeg = P.sb("gnbeg", [64, 2, NCH]); nbeta = P.sb("gnbeta", [64, 2, NCH]); eglast = P.sb("geglast", [128, 2, NCH]); Aex = P.sb("gAex", [64, 2])
    IDN = M[:, 4, :]
    for d in range(2):
        beta, ga = g4[:, 2 * d, :], g4[:, 2 * d + 1, :]
        P.op("act", lambda e: e.activation(out=beta, in_=beta, func=AF.Sigmoid), rd=[g4], wr=[g4])
        P.op("act", lambda e: e.activation(out=Aex[:, d:d + 1], in_=gc[:, 2 * d:2 * d + 1], func=AF.Exp), rd=[gc], wr=[Aex])
        P.op("act", lambda e: e.activation(out=ga, in_=ga, func=AF.Exp, bias=gc[:, 2 * d + 1:2 * d + 2], scale=1.0), rd=[g4, gc], wr=[g4])
        P.op("dve", lambda e: e.tensor_scalar(out=ga, in0=ga, scalar1=1.0, scalar2=None, op0=ALU.add), rd=[g4], wr=[g4])
        P.op("act", lambda e: e.activation(out=ga, in_=ga, func=AF.Ln), rd=[g4], wr=[g4])
        P.op("dve", lambda e: e.tensor_scalar(out=la[:, d, :], in0=ga, scalar1=Aex[:, d:d + 1], scalar2=-1.0, op0=ALU.mult, op1=ALU.mult), rd=[g4, Aex], wr=[la])
        ps = C.ps()
        P.op("pe", lambda e: e.matmul(ps[:64, :NCH], M[:, d, :], la[:, d, :], start=True, stop=True), rd=[M, la], wr=[ps])
        ps2 = C.ps()
        P.op("pe", lambda e: e.matmul(ps2[:, :NCH], C.ones[:64, :], la[:, d, :], start=True, stop=True), rd=[C.ones, la], wr=[ps2])
        P.op("act", lambda e: e.copy(out=gs[:, d, :], in_=ps[:64, :NCH]), rd=[ps], wr=[gs])
        P.op("act", lambda e: e.activation(out=eg[:, d, :], in_=gs[:, d, :], func=AF.Exp), rd=[gs], wr=[eg])
        P.op("act", lambda e: e.activation(out=eglast[:, d, :], in_=ps2[:, :NCH], func=AF.Exp), rd=[ps2], wr=[eglast])
        tmp = C.rot("gtmp", [64, NCH], 2)
        P.op("dve", lambda e: e.tensor_tensor(out=tmp[:], in0=ps2[:64, :NCH], in1=gs[:, d, :], op=ALU.subtract), rd=[ps2, gs], wr=[tmp])
        P.op("act", lambda e: e.activation(out=egl[:, d, :], in_=tmp[:], func=AF.Exp), rd=[tmp], wr=[egl])
        P.op("dve", lambda e: e.tensor_scalar(out=nbeta[:, d, :], in0=beta, scalar1=-1.0, scalar2=None, op0=ALU.mult), rd=[g4], wr=[nbeta])
        P.op("dve", lambda e: e.tensor_tensor(out=nbeg[:, d, :], in0=nbeta[:, d, :], in1=eg[:, d, :], op=ALU.mult), rd=[nbeta, eg], wr=[nbeg])
    Ss = [S, P.sb("gS_b", [128, 128])]
    P.op("dve", lambda e: e.memset(hsum[:], 0.0), wr=[hsum])
    def indep(d, Q):
        MI, MIT, MS = (M[:, 1, :], M[:, 0, :], M[:, 3, :]) if d == 0 else (M[:, 0, :], M[:, 1, :], M[:, 2, :])
        segs = SEGS if d == 0 else [SEGS[0]] + SEGS[:0:-1]
        for (c0, ncs) in segs:
            T = ncs * 64
            qs = C.rot("qs%d" % d, [128, 1024], 2); ks = C.rot("ks%d" % d, [128, 1024], 2)
            kk = C.rot("kk%d" % d, [64, 16, 128], 2); vv = C.rot("vv%d" % d, [64, 16, 129], 2)
            P.dma(qs[:, :T], qT[:, c0 * 64:c0 * 64 + T], rd=[qT], wr=[qs], q="sp")
            yield
            P.dma(ks[:, :T], kT[:, c0 * 64:c0 * 64 + T], rd=[kT], wr=[ks], q="pool")
            yield
            P.dma(kk[:, :ncs, :], ktok[:, c0:c0 + ncs, :], rd=[ktok], wr=[kk], q="sp")
            yield
            P.dma(vv[:, :ncs, 0:128], vtok[:, c0:c0 + ncs, :], rd=[vtok], wr=[vv], q="pool")
            yield
            order = range(ncs) if d == 0 else range(ncs - 1, -1, -1)
            for cl in order:
                c = c0 + cl
                tk = slice(cl * 64, cl * 64 + 64)
                pg = C.ps()
                P.op("pe", lambda e: e.matmul(pg[:, :64], la[:, d, c:c + 1].to_broadcast([64, 128]), M[:, d, :], start=True, stop=True), rd=[la, M], wr=[pg])
                yield
                egrow = C.rot("egrow%d" % d, [128, 64], 2)
                P.op("act", lambda e: e.activation(out=egrow[:], in_=pg[:, :64], func=AF.Exp), rd=[pg], wr=[egrow])
                yield
                dm = C.rot("dm%d" % d, [64, 64], 2); dmt = C.rot("dmt%d" % d, [64, 64], 2)
                P.op("dve", lambda e: e.tensor_scalar(out=dm[:], in0=pg[:64, :64], scalar1=gs[:, d, c:c + 1], scalar2=0.0, op0=ALU.subtract, op1=ALU.max), rd=[pg, gs], wr=[dm])
                yield
                P.op("dve", lambda e: e.tensor_scalar(out=dmt[:], in0=pg[:64, :64], scalar1=gs[:, d, c:c + 1], scalar2=0.0, op0=ALU.subtract, op1=ALU.min), rd=[pg, gs], wr=[dmt])
                yield
                P.op("act", lambda e: e.activation(out=dm[:], in_=dm[:], func=AF.Exp, scale=-1.0), rd=[dm], wr=[dm])
                yield
                P.op("act", lambda e: e.activation(out=dmt[:], in_=dmt[:], func=AF.Exp), rd=[dmt], wr=[dmt])
                yield
                P.op("dve", lambda e: e.tensor_tensor(out=dm[:], in0=dm[:], in1=MS, op=ALU.mult), rd=[dm, M], wr=[dm])
                yield
                P.op("dve", lambda e: e.tensor_tensor(out=dmt[:], in0=dmt[:], in1=MIT, op=ALU.mult), rd=[dmt, M], wr=[dmt])
                yield
                qtl = C.rot("qtl%d" % d, [128, 64], 3)
                P.op("dve", lambda e: e.tensor_tensor(out=qtl[:], in0=qs[:, tk], in1=egrow[:], op=ALU.mult), rd=[qs, egrow], wr=[qtl])
                yield
                pG = C.ps()
                P.op("pe", lambda e: e.matmul(pG[:64, :64], ks[:, tk], ks[:, tk], start=True, stop=True), rd=[ks], wr=[pG])
                yield
                XT = C.rot("XT%d" % d, [64, 64], 2); X = C.rot("X%d" % d, [64, 64], 2); Rm = C.rot("Rm%d" % d, [64, 64], 2)
                P.op("dve", lambda e: e.tensor_tensor(out=XT[:], in0=pG[:64, :64], in1=dm[:], op=ALU.mult), rd=[pG, dm], wr=[XT])
                yield
                P.op("dve", lambda e: e.tensor_scalar(out=XT[:], in0=XT[:], scalar1=nbeta[:, d, c:c + 1], scalar2=None, op0=ALU.mult), rd=[XT, nbeta], wr=[XT])
                yield
                pa = C.ps()
                P.op("pe", lambda e: e.matmul(pa[:64, :64], ks[:, tk], qs[:, tk], start=True, stop=True), rd=[ks, qs], wr=[pa])
                yield
                at = C.rot("at%d" % d, [64, 64], 3)
                P.op("dve", lambda e: e.tensor_tensor(out=at[:], in0=pa[:64, :64], in1=dmt[:], op=ALU.mult), rd=[pa, dmt], wr=[at])
                yield
                px = C.ps()
                P.op("pe", lambda e: e.matmul(px[:64, :64], XT[:], IDN, start=True, stop=True), rd=[XT, M], wr=[px])
                yield
                P.op("act", lambda e: e.copy(out=X[:], in_=px[:64, :64]), rd=[px], wr=[X])
                yield
                P.op("dve", lambda e: e.tensor_tensor(out=Rm[:], in0=px[:64, :64], in1=IDN, op=ALU.add), rd=[px, M], wr=[Rm])
                yield
                for lvl in range(5):
                    pyt = C.ps()
                    P.op("pe", lambda e: e.matmul(pyt[:64, :64], X[:], XT[:], start=True, stop=True), rd=[X, XT], wr=[pyt])
                    yield
                    if lvl < 4:
                        py = C.ps()
                        P.op("pe", lambda e: e.matmul(py[:64, :64], XT[:], X[:], start=True, stop=True), rd=[X, XT], wr=[py])
                        yield
                    XT2 = C.rot("XT%d" % d, [64, 64], 2)
                    P.op("act", lambda e: e.copy(out=XT2[:], in_=pyt[:64, :64]), rd=[pyt], wr=[XT2])
                    yield
                    if lvl < 4:
                        X2 = C.rot("X%d" % d, [64, 64], 2)
                        P.op("dve", lambda e: e.tensor_copy(out=X2[:], in_=py[:64, :64]), rd=[py], wr=[X2])
                        yield
                    pr = C.ps()
                    P.op("pe", lambda e: e.matmul(pr[:64, :64], XT2[:], Rm[:], start=True, stop=True), rd=[XT2, Rm], wr=[pr])
                    yield
                    R2 = C.rot(("Rf%d" if lvl == 4 else "Rm%d") % d, [64, 64], 3 if lvl == 4 else 2)
                    P.op("dve", lambda e: e.tensor_tensor(out=R2[:], in0=pr[:64, :64], in1=Rm[:], op=ALU.add), rd=[pr, Rm], wr=[R2])
                    yield
                    XT, Rm = XT2, R2
                    if lvl < 4:
                        X = X2
                vb = C.rot("vb%d" % d, [64, 128], 3); kh = C.rot("kh%d" % d, [64, 128], 3)
                P.op("dve", lambda e: e.tensor_scalar(out=vb[:], in0=vv[:, cl, 0:128], scalar1=nbeta[:, d, c:c + 1], scalar2=-1.0, op0=ALU.mult, op1=ALU.mult), rd=[vv, nbeta], wr=[vb])
                yield
                P.op("act", lambda e: e.activation(out=kh[:], in_=kk[:, cl, :], func=AF.Identity, scale=egl[:, d, c:c + 1]), rd=[kk, egl], wr=[kh])
                yield
                Q.append((c, cl, tk, ks, Rm, at, qtl, vb, kh))
                while len(Q) >= 2:
                    yield
    def dep(d, Q):
        S = Ss[d]
        P.op("dve", lambda e: e.memset(S[:], 0.0), wr=[S])
        yield
        for _ in range(NCH):
            while not Q:
                yield
            (c, cl, tk, ks, Rm, at, qtl, vb, kh) = Q.pop(0)
            pks = C.ps()
            P.op("pe", lambda e: e.matmul(pks[:64, :128], ks[:, tk], S[:], start=True, stop=True), rd=[ks, S], wr=[pks])
            yield
            rr = C.rot("rr%d" % d, [64, 128], 2)
            P.op("dve", lambda e: e.scalar_tensor_tensor(out=rr[:], in0=pks[:64, :128], scalar=nbeg[:, d, c:c + 1], in1=vb[:], op0=ALU.mult, op1=ALU.add), rd=[pks, nbeg, vb], wr=[rr])
            yield
            pv = C.ps()
            P.op("pe", lambda e: e.matmul(pv[:64, :128], Rm[:], rr[:], start=True, stop=True), rd=[Rm, rr], wr=[pv])
            yield
            vn = C.rot("vn%d" % d, [64, 128], 2)
            P.op("act", lambda e: e.copy(out=vn[:], in_=pv[:64, :128]), rd=[pv], wr=[vn])
            yield
            po = C.ps()
            P.op("pe", lambda e: e.matmul(po[:64, :128], at[:], vn[:], start=True, stop=False), rd=[at, vn], wr=[po])
            yield
            P.op("pe", lambda e: e.matmul(po[:64, :128], qtl[:], S[:], start=False, stop=True), rd=[qtl, S], wr=[po])
            yield
            pu = C.ps()
            P.op("pe", lambda e: e.matmul(pu[:, :128], kh[:], vn[:], start=True, stop=True), rd=[kh, vn], wr=[pu])
            yield
            P.op("dve", lambda e: e.tensor_scalar(out=S[:], in0=S[:], scalar1=eglast[:, d, c:c + 1], scalar2=None, op0=ALU.mult), rd=[S, eglast], wr=[S])
            yield
            P.op("dve", lambda e: e.tensor_tensor(out=S[:], in0=pu[:, :128], in1=S[:], op=ALU.add), rd=[S, pu], wr=[S])
            yield
            sc = float(DK ** -0.5)
            P.op("dve", lambda e: e.scalar_tensor_tensor(out=hsum[:, c, :], in0=po[:64, :128], scalar=sc, in1=hsum[:, c, :], op0=ALU.mult, op1=ALU.add), rd=[po, (hsum, c)], wr=[(hsum, c)])
            yield
    Qs = [[], []]
    run_interleaved([indep(0, Qs[0]), dep(0, Qs[0]), indep(1, Qs[1]), dep(1, Qs[1])], C, [(0, 2), (4, 2), (2, 2), (6, 2)])
    for (c0, ncs) in SEGS:
        mo = C.rot("kk0", [64, 16, 128], 2)
        P.dma(mo[:, :ncs, :], ggtok[:, c0:c0 + ncs, :], rd=[ggtok], wr=[mo], q="sp")
        P.op("act", lambda e: e.activation(out=mo[:, :ncs, :], in_=mo[:, :ncs, :], func=AF.Silu), rd=[mo], wr=[mo])
        for cl in range(ncs):
            c = c0 + cl
            sq = C.rot("hsq", [64, 128], 2); t = C.rot("den0", [64, 4], 3)
            P.op("act", lambda e: e.activation(out=sq[:], in_=hsum[:, c, :], func=AF.Square, accum_out=t[:, 0:1]), rd=[(hsum, c)], wr=[sq, t])
            P.op("dve", lambda e: e.tensor_scalar(out=t[:, 1:2], in0=t[:, 0:1], scalar1=1.0 / 128, scalar2=EPS, op0=ALU.mult, op1=ALU.add), rd=[t], wr=[t])
            P.op("act", lambda e: e.activation(out=t[:, 1:2], in_=t[:, 1:2], func=AF.Sqrt), rd=[t], wr=[t])
            P.op("dve", lambda e: e.reciprocal(out=t[:, 2:3], in_=t[:, 1:2]), rd=[t], wr=[t])
            P.op("dve", lambda e: e.scalar_tensor_tensor(out=hsum[:, c, :], in0=hsum[:, c, :], scalar=t[:, 2:3], in1=ng[:], op0=ALU.mult, op1=ALU.mult), rd=[(hsum, c), t, ng], wr=[(hsum, c)])
            P.op("dve", lambda e: e.tensor_tensor(out=hsum[:, c, :], in0=hsum[:, c, :], in1=mo[:, cl, :], op=ALU.mult), rd=[(hsum, c), mo], wr=[(hsum, c)])
        P.dma(out_g[:, c0:c0 + ncs, :], hsum[:, c0:c0 + ncs, :], rd=[hsum], wr=[(out_g, c0)], q=C.q())


def mla_part(P, C, qA, qB, kA, kB, vtok, out_a):
    NKT = 66
    kAs = P.sb("mkA", [128, 8448]); kBs = P.sb("mkB", [64, 8448]); va = P.sb("mva", [128, NKT, 129])
    P.dma(kAs[:], kA[:], rd=[kA], wr=[kAs], q="sp")
    P.dma(kBs[:], kB[:], rd=[kB], wr=[kBs], q="pool")
    P.op("dve", lambda e: e.memset(va[:, :, 128:129], 1.0), wr=[va])
    P.dma(va[:, :, 0:128], vtok[:], rd=[vtok], wr=[va], q="sp")
    accs = C.psb[0:4]
    sci = [0]
    for qg in range(16):
        qa = C.rot("mqa", [128, 512], 2); qb = C.rot("mqb", [64, 512], 2)
        P.dma(qa[:], qA[:, qg * 512:(qg + 1) * 512], rd=[qA], wr=[qa], q="sp")
        P.dma(qb[:], qB[:, qg * 512:(qg + 1) * 512], rd=[qB], wr=[qb], q="pool")
        for kt in range(NKT):
            ps = C.psb[4 + sci[0] % 4]; sci[0] += 1
            ksl = slice(kt * 128, (kt + 1) * 128)
            P.op("pe", lambda e: e.matmul(ps[:, :512], kAs[:, ksl], qa[:], start=True, stop=False), rd=[kAs, qa], wr=[ps])
            P.op("pe", lambda e: e.matmul(ps[:, :512], kBs[:, ksl], qb[:], start=False, stop=True), rd=[kBs, qb], wr=[ps])
            pT = C.rot("mpT", [128, 512], 3)
            P.op("act", lambda e: e.activation(out=pT[:], in_=ps[:, :512], func=AF.Exp), rd=[ps], wr=[pT])
            for j in range(4):
                P.op("pe", lambda e: e.matmul(accs[j][:, :129], pT[:, j * 128:(j + 1) * 128], va[:, kt, :], start=(kt == 0), stop=(kt == NKT - 1)), rd=[pT, va], wr=[accs[j]])
        for j in range(4):
            t = C.rot("mden", [128, 1], 4); o = C.rot("mo", [128, 128], 4)
            P.op("dve", lambda e: e.reciprocal(out=t[:], in_=accs[j][:, 128:129]), rd=[accs[j]], wr=[t])
            P.op("dve", lambda e: e.tensor_scalar(out=o[:], in0=accs[j][:, 0:128], scalar1=t[:, 0:1], scalar2=None, op0=ALU.mult), rd=[accs[j], t], wr=[o])
            r0 = qg * 512 + j * 128
            P.dma(out_a[r0:r0 + 128, :], o[:], rd=[o], wr=[(out_a, r0)], q=C.q())


def build_D(parts=("gdn", "mla")):
    nc = bass.Bass("TRN2", target_bir_lowering=False)
    es = ExitStack()
    with es:
        P = Prog(nc, es)
        C = Ctx(P)
        I = lambda n, s: P.dram(n, s, kind="ExternalInput")
        if "gdn" in parts:
            with ExitStack() as es2:
                old, P.es = P.es, es2
                C.tmp = {}
                qT = I("gqT", [128, 8448]); kT = I("gkT", [128, 8448]); ktok = I("gktok", [64, NCH, 128]); vtok = I("gvtok", [64, NCH, 128]); ggtok = I("gggtok", [64, NCH, 128])
                gab = I("gab", [64, 4, NCH]); gconst = I("gconst", [64, 4]); cm = I("gcm", [64, 5, 64]); normg = I("gnormg", [64, 128])
                out_g = P.dram("out_g", [64, NCH, 128], kind="ExternalOutput")
                gdn_part(P, C, qT, kT, ktok, vtok, ggtok, gab, gconst, cm, normg, out_g)
                P.barrier()
                P.es = old
                C.tmp = {}
        if "mla" in parts:
            with ExitStack() as es2:
                old, P.es = P.es, es2
                qA = I("mqA", [128, 8192]); qB = I("mqB", [64, 8192]); kA = I("mkA", [128, 8448]); kB = I("mkB", [64, 8448]); vt = I("mvtok", [128, 66, 128])
                out_a = P.dram("out_a", [8192, 128], kind="ExternalOutput")
                mla_part(P, C, qA, qB, kA, kB, vt, out_a)
                P.barrier()
                P.es = old
                C.tmp = {}
        P.finish()
        print("phase D instructions", P.nins, P.cnt)
    return nc


def d_maps(inp, gdT, gdc, mlT, mlc):
    maps = []
    jj = np.arange(64)
    trif = (jj[:, None] <= jj[None, :]).astype(np.float32)
    sf = (jj[:, None] < jj[None, :]).astype(np.float32)
    cm = np.ascontiguousarray(np.stack([trif, trif.T, sf, sf.T, np.eye(64, dtype=np.float32)], 1))
    for k in range(8):
        b, h = k // 4, k % 4
        full = np.concatenate([gdc[b], gdT[b]], axis=1)
        rows = lambda base: full[base + h * 128: base + (h + 1) * 128]
        m = {"gqT": np.ascontiguousarray(rows(0)), "gkT": np.ascontiguousarray(rows(512)),
             "gktok": tok64(rows(512).T), "gvtok": tok64(rows(1024).T), "gggtok": tok64(rows(1536).T)}
        g4 = np.stack([full[2048 + h], full[2056 + h], full[2048 + 4 + h], full[2056 + 4 + h]], 0)
        m["gab"] = np.ascontiguousarray(g4.reshape(4, NCH, 64).transpose(2, 0, 1))
        al, db = inp["c_a_log"][0], inp["c_dt_bias"][0]
        m["gconst"] = np.ascontiguousarray(np.broadcast_to(np.array([al[0, h], db[0, h], al[1, h], db[1, h]], np.float32)[None], (64, 4)))
        m["gcm"] = cm
        m["gnormg"] = np.ascontiguousarray(np.broadcast_to(inp["c_norm_g"][0][None], (64, 128)))
        base = h * 512
        mfull = np.concatenate([mlc[b], mlT[b]], axis=1)
        m["mqA"] = np.ascontiguousarray(mlT[b][base:base + 128]); m["mqB"] = np.ascontiguousarray(mlT[b][base + 128:base + 192])
        m["mkA"] = np.ascontiguousarray(mfull[base + 192:base + 320]); m["mkB"] = np.ascontiguousarray(mfull[base + 320:base + 384])
        v = mfull[base + 384:base + 512].T
        m["mvtok"] = np.ascontiguousarray(v.reshape(66, 128, 128).transpose(1, 0, 2))
        maps.append(m)
    return maps


def build_B():
    nc = bass.Bass("TRN2", target_bir_lowering=False)
    es = ExitStack()
    with es:
        P = Prog(nc, es)
        C = Ctx(P)
        I = lambda n, s: P.dram(n, s, kind="ExternalInput")
        with ExitStack() as es2:
            old, P.es = P.es, es2
            qT = I("qT", [128, 8448]); kT = I("kT", [128, 8448]); ktok = I("ktok", [64, NCH, 128]); vtok = I("vtok", [64, NCH, 128]); motok = I("motok", [64, NCH, 128])
            g4 = I("g4", [64, 4, NCH]); gbias = I("gbias", [64, 4]); tri = I("tri", [64, 2, 64]); normg = I("normg", [64, 128])
            out_m = P.dram("out_m", [64, NCH, 128], kind="ExternalOutput")
            mlstm_part(P, C, qT, kT, ktok, vtok, motok, g4, gbias, tri, normg, out_m)
            P.barrier()
            P.es = old
            C.tmp = {}
        wts = {}
        for nm, shp in (("w1", [33, 64]), ("w2", [64, 64]), ("w3", [64, 64]), ("w4", [64, 4, 128]), ("fr", [64, 1]), ("fb", [64, 3]), ("nd", [128, 4]), ("dsk", [128, 2])):
            d = I("h_" + nm, shp); t = P.sb("s_" + nm, shp); P.dma(t[:], d[:], rd=[d], wr=[t]); wts[nm] = t
        P.op("dve", lambda e: e.tensor_scalar(out=wts["fb"][:], in0=wts["fb"][:], scalar1=wts["fr"][:, 0:1], scalar2=None, op0=ALU.mult), rd=[wts["fb"], wts["fr"]], wr=[wts["fb"]])
        K = load_consts2(P, DFT_SPECS)
        uin = I("ud", [64, 3, 128, 128]); featT = I("feat8192", [33, 8192]); tn = I("tn8192", [128, 8192])
        scratch = P.dram("hscr", [4, 128, 8192])
        outd = P.dram("yd", [64, 128, 128], kind="ExternalOutput")
        hyena_dft_part(P, C, uin, featT, tn, wts, K, scratch, outd)
        P.barrier()
        C.tmp = {}
        for L in (256,):
            with ExitStack() as es2:
                old, P.es = P.es, es2
                uin = I(f"u{L}", [128, 3, L]); featT = I(f"feat{L}", [33, L]); tn = I(f"tn{L}", [128, L])
                out = P.dram(f"y{L}", [128, L], kind="ExternalOutput")
                hyena_part(P, C, L, uin, featT, tn, wts, out, str(L))
                P.barrier()
                P.es = old
                C.tmp = {}
        P.finish()
    return nc


def kernel(**inp):
    inp = {k: np.ascontiguousarray(np.asarray(v, dtype=np.float32)) for k, v in inp.items()}
    cores = list(range(8))
    zT, zcT = run_A(inp)
    mb = mlstm_maps(inp, zT, zcT); hb = hyena_dft_maps(inp, zT, hyena_maps(inp, zT, zcT))
    res = run_bass_kernel_spmd(build_B(), [{**a, **b} for a, b in zip(mb, hb)], core_ids=cores)
    mix = np.zeros((2, 8192, 1024), np.float32); mixc = np.zeros((2, 256, 1024), np.float32)
    for k in cores:
        b, h = k // 4, k % 4
        r = res.results[k]
        o = r["out_m"].transpose(1, 0, 2).reshape(8448, 128)
        mixc[b][:, h * 128:(h + 1) * 128] = o[:256]; mix[b][:, h * 128:(h + 1) * 128] = o[256:]
        mix[b][:, 512 + h * 128:512 + (h + 1) * 128] = r["yd"].transpose(1, 0, 2).reshape(128, 8192).T
        mixc[b][:, 512 + h * 128:512 + (h + 1) * 128] = r["y256"].T
    res = run_bass_kernel_spmd(build_C(), c_maps(inp, 0, inp["x"], inp["ctx"], mix, mixc), core_ids=cores)
    x2, c2, gdT, gdc, mlT, mlc = gather_C(res.results)
    res = run_bass_kernel_spmd(build_D(), d_maps(inp, gdT, gdc, mlT, mlc), core_ids=cores)
    mix1 = np.zeros((2, 8192, 1024), np.float32)
    for k in cores:
        b, h = k // 4, k % 4
        r = res.results[k]
        mix1[b][:, h * 128:(h + 1) * 128] = r["out_g"].transpose(1, 0, 2).reshape(8448, 128)[256:]
        mix1[b][:, 512 + h * 128:512 + (h + 1) * 128] = r["out_a"]
    maps = c_maps(inp, 1, x2, c2, mix1, np.zeros_like(mixc), layer1=False)
    for m_ in maps:
        m_.pop("x2"); m_.pop("m2"); m_["masks"] = np.ascontiguousarray(m_["masks"][:, :2, :])
    res = run_bass_kernel_spmd(build_C(layer1=False, segs=SEG_C[:2]), maps, core_ids=cores)
    out = gather_C(res.results, layer1=False)[0]
    return np.ascontiguousarray(out.astype(np.float32))
```
